# Optimizing a Trainium2 kernel written in Bass

```python
import jax, jax.numpy as jnp
from jax import lax
import numpy as np

D_MODEL = 1024
BATCH = 4
SEQ = 8192
DEPTH = 2

GRID_W = 64
CTX_LEN = 256
HEAD_DIM = 64
ATTN_HEADS = 8
KV_HEADS = 2
Q_PER_KV = ATTN_HEADS // KV_HEADS
ATTN_WIDTH = ATTN_HEADS * HEAD_DIM
KV_WIDTH = KV_HEADS * HEAD_DIM
WINDOW = 128
BLOCK = 128
FOURIER_GROUPS = 4
FOURIER_GROUP_DIM = 64
FOURIER_WIDTH = FOURIER_GROUPS * FOURIER_GROUP_DIM
CONV_GROUPS = 4
CONV_WIDTH = 256
CONV_TAPS = 3
MIX_WIDTH = ATTN_WIDTH + FOURIER_WIDTH + CONV_WIDTH
PROJ_SIZES = (ATTN_WIDTH, KV_WIDTH, KV_WIDTH, ATTN_WIDTH,
              FOURIER_WIDTH, FOURIER_WIDTH,
              CONV_WIDTH, CONV_WIDTH, CONV_WIDTH, CONV_WIDTH)
PROJ_WIDTH = 2 * ATTN_WIDTH + 2 * KV_WIDTH + 2 * FOURIER_WIDTH + 4 * CONV_WIDTH
ROPE_FREQS = HEAD_DIM // 4
ROPE_BASE = 10000.0
NORM_EPS = 1e-6

kernel_name = 'hybrid_parallel_groups_flow_block'


def rmsnorm(x, g):
    xf = x.astype(jnp.float32)
    y = xf * lax.rsqrt(jnp.mean(xf * xf, axis=-1, keepdims=True) + NORM_EPS)
    return (y * g.astype(jnp.float32)).astype(x.dtype)


def split_cols(p, sizes):
    out = []
    start = 0
    for s in sizes:
        out.append(p[..., start:start + s])
        start += s
    return out


def axial_rope_tables(n):
    rows = n // GRID_W
    row = jnp.repeat(jnp.arange(rows, dtype=jnp.float32), GRID_W)
    col = jnp.tile(jnp.arange(GRID_W, dtype=jnp.float32), rows)
    inv_freq = jnp.power(ROPE_BASE, -jnp.arange(ROPE_FREQS, dtype=jnp.float32) / ROPE_FREQS)
    ang = jnp.concatenate([row[:, None] * inv_freq, col[:, None] * inv_freq], axis=-1)
    return jnp.cos(ang), jnp.sin(ang)


def apply_axial_rope(x, cos, sin):
    b, n, h, d = x.shape
    xr = x.astype(jnp.float32).reshape(b, n, h, 2, 2, ROPE_FREQS)
    x1 = xr[..., 0, :]
    x2 = xr[..., 1, :]
    cs = cos.reshape(n, 1, 2, ROPE_FREQS)
    sn = sin.reshape(n, 1, 2, ROPE_FREQS)
    out = jnp.stack([x1 * cs - x2 * sn, x1 * sn + x2 * cs], axis=-2)
    return out.reshape(b, n, h, d).astype(x.dtype)


def sink_softmax(s, sink_b):
    m = jnp.maximum(jnp.max(s, axis=-1, keepdims=True), sink_b)
    e = jnp.exp(s - m)
    return e / (jnp.sum(e, axis=-1, keepdims=True) + jnp.exp(sink_b - m))


def window_attention(q, k, v, kc, vc, sink_b):
    b, n = q.shape[:2]
    nb = n // BLOCK
    span = BLOCK + 2 * WINDOW
    nc = kc.shape[1]
    qb = (q * (HEAD_DIM ** -0.5)).reshape(b, nb, BLOCK, KV_HEADS, Q_PER_KV, HEAD_DIM).transpose(1, 0, 2, 3, 4, 5)
    pad = ((0, 0), (WINDOW, WINDOW), (0, 0), (0, 0))
    kp = jnp.pad(k, pad)
    vp = jnp.pad(v, pad)
    offs_q = jnp.arange(BLOCK)
    offs_k = jnp.arange(span) - WINDOW

    def one_block(args):
        qi, i = args
        start = i * BLOCK
        ks = lax.dynamic_slice_in_dim(kp, start, span, axis=1)
        vs = lax.dynamic_slice_in_dim(vp, start, span, axis=1)
        qpos = start + offs_q
        kpos = start + offs_k
        valid = (jnp.abs(qpos[:, None] - kpos[None, :]) <= WINDOW) & (kpos >= 0)[None, :] & (kpos < n)[None, :]
        s_loc = jnp.einsum('bqkgd,bnkd->bkgqn', qi, ks).astype(jnp.float32)
        s_loc = jnp.where(valid, s_loc, -jnp.inf)
        s_ctx = jnp.einsum('bqkgd,bnkd->bkgqn', qi, kc).astype(jnp.float32)
        p = sink_softmax(jnp.concatenate([s_ctx, s_loc], axis=-1), sink_b).astype(v.dtype)
        return (jnp.einsum('bkgqn,bnkd->bqkgd', p[..., :nc], vc)
                + jnp.einsum('bkgqn,bnkd->bqkgd', p[..., nc:], vs))

    o = lax.map(one_block, (qb, jnp.arange(nb)))
    return o.transpose(1, 0, 2, 3, 4, 5).reshape(b, n, ATTN_WIDTH)


def context_attention(q, k, v, sink_b):
    b, n = q.shape[:2]
    qg = (q * (HEAD_DIM ** -0.5)).reshape(b, n, KV_HEADS, Q_PER_KV, HEAD_DIM)
    s = jnp.einsum('bqkgd,bnkd->bkgqn', qg, k).astype(jnp.float32)
    p = sink_softmax(s, sink_b).astype(v.dtype)
    return jnp.einsum('bkgqn,bnkd->bqkgd', p, v).reshape(b, n, ATTN_WIDTH)


def fourier_mix(u, w_f):
    b, n, _ = u.shape
    ug = u.reshape(b, n, FOURIER_GROUPS, FOURIER_GROUP_DIM).astype(jnp.float32)
    f = jnp.fft.fft2(ug, axes=(1, 3), norm='ortho').real.astype(u.dtype)
    return jnp.einsum('bngc,gcd->bngd', f, w_f).reshape(b, n, FOURIER_WIDTH)


def short_conv_mix(z, b_gate, c_gate, w, bias):
    t = c_gate * z
    tp = jnp.pad(t, ((0, 0), (1, 1), (0, 0)))
    y = tp[:, :-2] * w[0] + tp[:, 1:-1] * w[1] + tp[:, 2:] * w[2] + bias
    return b_gate * y


def gated_merge(a, ga, f, gf, s, gc, w_out):
    h = jnp.concatenate([a * jax.nn.silu(ga), f * jax.nn.silu(gf), s * jax.nn.silu(gc)], axis=-1)
    return h @ w_out


def layer(x, ctx, c, c_ctx, w_mod, b_mod, g_pre, g_post, w_in, w_out, sink, w_fourier, conv_w, conv_b,
          cos, sin, update_ctx):
    b, n = x.shape[:2]
    nc = ctx.shape[1]
    shift, scale, gate = jnp.split(jax.nn.silu(c) @ w_mod + b_mod, 3, axis=-1)
    shift_c, scale_c, gate_c = jnp.split(jax.nn.silu(c_ctx) @ w_mod + b_mod, 3, axis=-1)
    sink_b = sink.astype(jnp.float32).reshape(1, KV_HEADS, Q_PER_KV, 1, 1)

    hx = rmsnorm(x, g_pre) * (1 + scale[:, None]) + shift[:, None]
    hc = rmsnorm(ctx, g_pre) * (1 + scale_c) + shift_c

    if update_ctx:
        qc, kc, vc, gac, ufc, gfc, zcc, bcc, ccc, gcc = split_cols(hc @ w_in, PROJ_SIZES)
    else:
        kc, vc = split_cols(hc @ w_in[:, ATTN_WIDTH:ATTN_WIDTH + 2 * KV_WIDTH], (KV_WIDTH, KV_WIDTH))
    kc = kc.reshape(b, nc, KV_HEADS, HEAD_DIM)
    vc = vc.reshape(b, nc, KV_HEADS, HEAD_DIM)

    q, k, v, ga, uf, gf, zc, bc, cc, gc = split_cols(hx @ w_in, PROJ_SIZES)
    q = apply_axial_rope(q.reshape(b, n, ATTN_HEADS, HEAD_DIM), cos, sin)
    k = apply_axial_rope(k.reshape(b, n, KV_HEADS, HEAD_DIM), cos, sin)
    v = v.reshape(b, n, KV_HEADS, HEAD_DIM)
    a = window_attention(q, k, v, kc, vc, sink_b)
    f = fourier_mix(uf, w_fourier)
    s = short_conv_mix(zc, bc, cc, conv_w, conv_b)
    y = gated_merge(a, ga, f, gf, s, gc, w_out)
    x_new = x + gate[:, None] * rmsnorm(y, g_post)

    if update_ctx:
        ac = context_attention(qc.reshape(b, nc, ATTN_HEADS, HEAD_DIM), kc, vc, sink_b)
        fc = fourier_mix(ufc, w_fourier)
        sc = short_conv_mix(zcc, bcc, ccc, conv_w, conv_b)
        yc = gated_merge(ac, gac, fc, gfc, sc, gcc, w_out)
        ctx = ctx + gate_c * rmsnorm(yc, g_post)
    return x_new, ctx


def setup_inputs(seed: int = 0) -> dict:
    key = jax.random.key(seed)
    ks = jax.random.split(key, 14)
    nrm = jax.random.normal
    x = nrm(ks[0], (BATCH, SEQ, D_MODEL), jnp.float32)
    c = nrm(ks[1], (BATCH, D_MODEL), jnp.float32)
    ctx = nrm(ks[2], (BATCH, CTX_LEN, D_MODEL), jnp.float32)
    c_ctx = nrm(ks[3], (D_MODEL,), jnp.float32)
    w_mod = nrm(ks[4], (DEPTH, D_MODEL, 3 * D_MODEL), jnp.float32) * (0.5 * D_MODEL ** -0.5)
    b_mod = 0.01 * nrm(ks[5], (DEPTH, 3 * D_MODEL), jnp.float32)
    g_pre = 1.0 + 0.05 * nrm(ks[6], (DEPTH, D_MODEL), jnp.float32)
    g_post = 1.0 + 0.05 * nrm(ks[7], (DEPTH, D_MODEL), jnp.float32)
    w_in = nrm(ks[8], (DEPTH, D_MODEL, PROJ_WIDTH), jnp.float32) * (D_MODEL ** -0.5)
    w_out = nrm(ks[9], (DEPTH, MIX_WIDTH, D_MODEL), jnp.float32) * (MIX_WIDTH ** -0.5)
    sink = 0.5 * nrm(ks[10], (DEPTH, ATTN_HEADS), jnp.float32)
    w_fourier = nrm(ks[11], (DEPTH, FOURIER_GROUPS, FOURIER_GROUP_DIM, FOURIER_GROUP_DIM), jnp.float32) * (FOURIER_GROUP_DIM ** -0.5)
    conv_w = nrm(ks[12], (DEPTH, CONV_TAPS, CONV_WIDTH), jnp.float32) * (CONV_TAPS ** -0.5)
    conv_b = 0.01 * nrm(ks[13], (DEPTH, CONV_WIDTH), jnp.float32)
    return {'x': x, 'c': c, 'ctx': ctx, 'c_ctx': c_ctx, 'w_mod': w_mod, 'b_mod': b_mod,
            'g_pre': g_pre, 'g_post': g_post, 'w_in': w_in, 'w_out': w_out, 'sink': sink,
            'w_fourier': w_fourier, 'conv_w': conv_w, 'conv_b': conv_b}


def reference(x, c, ctx, c_ctx, w_mod, b_mod, g_pre, g_post, w_in, w_out, sink, w_fourier, conv_w, conv_b):
    cos, sin = axial_rope_tables(x.shape[1])
    for l in range(DEPTH):
        x, ctx = layer(x, ctx, c, c_ctx, w_mod[l], b_mod[l], g_pre[l], g_post[l], w_in[l], w_out[l],
                       sink[l], w_fourier[l], conv_w[l], conv_b[l], cos, sin, l < DEPTH - 1)
    return x
```

```python
import contextlib
import numpy as np
import ml_dtypes
import concourse.bass as bass
import concourse.mybir as mybir
from concourse.bass_utils import run_bass_kernel_spmd

F32 = mybir.dt.float32
BF16 = mybir.dt.bfloat16
AF = mybir.ActivationFunctionType
ALU = mybir.AluOpType
SEM_ROT = 30000


class SemW:
    def __init__(s, h, name):
        s.h = h
        s.name = name
        s.cnt = 0


class Tok:
    def __init__(s, name=""):
        s.name = name
        s.w = None
        s.r = {}
        s.excl = False
        s.multi = False
        s.wm = {}


class Eng:
    def __init__(s, fwk, name, h, is_pe=False):
        s.fw = fwk
        s.name = name
        s.h = h
        s.is_pe = is_pe
        s.sem = fwk.new_sem("p_" + name)
        s.seen = {}

    def wait(s, ev):
        if ev is None:
            return
        sw, val = ev
        if s.seen.get(sw, 0) >= val:
            return
        if sw is s.sem:
            if s.is_pe or not s.fw.same_eng_sync:
                return
            assert val <= sw.cnt, "self-wait on future inc (%s)" % s.name
        s.h.wait_ge(sw.h, val)
        s.seen[sw] = val


class FW:
    def __init__(s, nc, stack, same_eng_sync=True):
        s.nc = nc
        s.stack = stack
        s.same_eng_sync = same_eng_sync
        s.nsem = 0
        s.all_sems = []
        s.pe = Eng(s, "pe", nc.tensor, is_pe=True)
        s.act = Eng(s, "act", nc.scalar)
        s.dve = Eng(s, "dve", nc.vector)
        s.pool = Eng(s, "pool", nc.gpsimd)
        s.sp = Eng(s, "sp", nc.sync)
        s.engs = [s.pe, s.act, s.dve, s.pool, s.sp]

    def new_sem(s, name):
        s.nsem += 1
        h = s.stack.enter_context(s.nc.semaphore("%s_%d" % (name, s.nsem)))
        sw = SemW(h, name)
        s.all_sems.append(sw)
        return sw

    def sbuf(s, name, shape, dt, stack=None):
        s.nalloc = getattr(s, "nalloc", 0) + 1
        return (stack or s.stack).enter_context(s.nc.sbuf_tensor("%s_%d" % (name, s.nalloc), list(shape), dt))

    def barrier(s):
        for e in s.engs:
            for sw in s.all_sems:
                if sw.cnt > 0:
                    e.wait((sw, sw.cnt))

    def psum(s, name, shape, dt=F32):
        return s.stack.enter_context(s.nc.psum_tensor(name, list(shape), dt))

    def _deps(s, eng, reads, writes):
        for b in reads:
            eng.wait(b.w)
            if b.multi:
                for sw, v in list(b.wm.items()):
                    eng.wait((sw, v))
            if b.excl:
                for sw, v in list(b.r.items()):
                    if sw is not eng.sem:
                        eng.wait((sw, v))
        for b in writes:
            if not b.multi:
                eng.wait(b.w)
            for sw, v in list(b.r.items()):
                eng.wait((sw, v))

    def _post(s, ev, reads, writes):
        sw, v = ev
        for b in reads:
            if b.r.get(sw, 0) < v:
                b.r[sw] = v
        for b in writes:
            if b.multi:
                if b.wm.get(sw, 0) < v:
                    b.wm[sw] = v
                continue
            b.w = ev
            b.r = {}

    def op(s, eng, fn, reads=(), writes=(), inc=True):
        s._deps(eng, reads, writes)
        inst = fn(eng.h)
        if eng.sem.cnt >= SEM_ROT:
            eng.sem = s.new_sem("p_" + eng.name)
        if inc:
            eng.sem.cnt += 1
            inst.then_inc(eng.sem.h, 1)
            ev = (eng.sem, eng.sem.cnt)
        else:
            assert eng.is_pe
            ev = (eng.sem, eng.sem.cnt + 1)
        s._post(ev, reads, writes)
        return inst

    def dma(s, q, dsem, out, in_, reads=(), writes=(), **kw):
        s._deps(q, reads, writes)
        inst = q.h.dma_start(out=out, in_=in_, **kw)
        dsem.cnt += 16
        inst.then_inc(dsem.h, 16)
        ev = (dsem, dsem.cnt)
        s._post(ev, reads, writes)
        return ev

    def wait_all(s, eng, toks):
        for b in toks:
            eng.wait(b.w)
            for sw, v in list(b.wm.items()):
                eng.wait((sw, v))
            for sw, v in list(b.r.items()):
                eng.wait((sw, v))


T0 = 4096
NT = 32
NG = 8
D = 1024
PW = 2816
NCTX = 256
EPS = 1e-6
C_Q, C_K, C_V, C_GA, C_UF, C_GF, C_ZC, C_BC, C_CC, C_GC = 0, 512, 640, 768, 1280, 1536, 1792, 2048, 2304, 2560


def build_nc(depth=2):
    nc = bass.Bass("TRN2", target_bir_lowering=False)

    def din(name, shape, dt=F32):
        return nc.dram_tensor(name, list(shape), dt, kind="ExternalInput").ap()

    x_d = din("x", [T0, D])
    ctx_d = din("ctx", [NCTX, D])
    cT_d = din("cT", [128, 8, 2])
    wmod_d = din("w_mod", [2, D, 3 * D])
    bmodT_d = din("bmodT", [2, 128, 24])
    gpreT_d = din("gpreT", [2, 128, 8])
    gpostT_d = din("gpostT", [2, 128, 8])
    win_d = din("w_in", [2, D, PW])
    wout_d = din("w_out", [2, D, D])
    sink_d = din("sink", [2, 8])
    wf_d = din("w_fourier", [2, 4, 64, 64])
    convwT_d = din("convwT", [2, 128, 2, 3])
    convbT_d = din("convbT", [2, 128, 2])
    cos_d = din("cosT", [128, T0])
    sin_d = din("sinT", [128, T0])
    identf_d = din("identf", [128, 128])
    c64_d = din("c64x2", [64, 128])
    s64_d = din("ns64x2", [64, 128])
    flags_d = din("flags", [128, 2])
    identb_d = din("identb", [128, 128], BF16)
    pm_d = din("pm", [128, 128], BF16)
    m1_d = din("m1", [128, 128], BF16)
    masks_d = din("masks", [128, 4, 128], BF16)
    cs256_d = din("cs256", [128, 2, 2, 256], BF16)
    g_d = din("gtab", [128, 64, 2, 64], BF16)
    out_d = nc.dram_tensor("out", [T0, D], F32, kind="ExternalOutput").ap()

    modscr = [nc.dram_tensor("modscr%d" % l, [2, 3 * D], F32).ap() for l in range(2)]
    x1_d = nc.dram_tensor("x1s", [T0, D], F32).ap()
    ctx1_d = nc.dram_tensor("ctx1s", [NCTX, D], F32).ap()
    ez_in = [[nc.dram_tensor("ezin%d_%d" % (l, q), [128, 2048], BF16).ap() for q in range(8)] for l in range(2)]
    ez_out = [[nc.dram_tensor("ezout%d_%d" % (l, q), [256, 2048], BF16).ap() for q in range(8)] for l in range(2)]
    import os
    DBG = bool(os.environ.get("KDBG"))
    dbg = {}
    if DBG:
        for nm, shp in [("dbg_kT", [128, (NT + 2) * 128]), ("dbg_v", [128, (NT + 2) * 256]), ("dbg_t", [128, 2 * (T0 + 2)]),
                        ("dbg_fT", [128, 2 * T0]), ("dbg_q", [128, 4 * 512]), ("dbg_hT", [128, 8 * 512]), ("dbg_sga", [128, 4 * 512])]:
            dbg[nm] = nc.dram_tensor(nm, shp, BF16, kind="ExternalOutput").ap()
    eh_in = [nc.dram_tensor("ehin%d" % l, [128, 640], BF16).ap() for l in range(2)]
    eh_out = [nc.dram_tensor("ehout%d" % l, [256, 640], BF16).ap() for l in range(2)]

    with contextlib.ExitStack() as st:
        fw = FW(nc, st)
        pe, act, dve, pool, sp = fw.pe, fw.act, fw.dve, fw.pool, fw.sp
        st.enter_context(nc.Block())

        import os
        wbf = fw.sbuf("wbf", [128, 8, PW], BF16); wbf_t = Tok("wbf")
        wob = fw.sbuf("wob", [128, 8, D], BF16); wob_t = Tok("wob")
        kT_all = fw.sbuf("kT_all", [128, (NT + 2) * 128], BF16); kT_t = [Tok("kT%d" % i) for i in range(NT + 2)]
        v_aug = fw.sbuf("v_aug", [128, NT + 2, 2, 128], BF16); v_t = [Tok("v%d" % i) for i in range(NT + 2)]
        t_all = fw.sbuf("t_all", [128, 2, T0 + 2], BF16); t_t = [Tok("t%d" % i) for i in range(NG + 2)]
        fT = fw.sbuf("fT", [128, 2, T0], BF16); fT_t = [Tok("fT0"), Tok("fT1")]
        kcT = fw.sbuf("kcT", [128, NCTX], BF16); kcT_t = Tok("kcT")
        vc_aug = fw.sbuf("vc_aug", [128, 2, 2, 128], BF16); vc_t = Tok("vc")
        ident_b = fw.sbuf("ident_b", [128, 128], BF16)
        pm_b = fw.sbuf("pm_b", [128, 128], BF16); m1_b = fw.sbuf("m1_b", [128, 128], BF16)
        masks_b = fw.sbuf("masks_b", [128, 4, 128], BF16)
        flags = fw.sbuf("flags", [128, 2], F32)
        AB = fw.sbuf("AB", [128, 2, 256], BF16); AB_t = Tok("AB")
        gg_b = fw.sbuf("gg_b", [128, D], F32); gg_t = Tok("gg")
        cT = fw.sbuf("cT", [128, 8, 2], F32); scs = fw.sbuf("scs", [128, 8, 2], F32)
        modT = fw.sbuf("modT", [128, 24, 2], F32); am = fw.sbuf("am", [128, 8, 2], F32); ggm = fw.sbuf("ggm", [128, 8, 2], F32)
        mod_t = Tok("mod")
        bmodT = fw.sbuf("bmodT", [128, 24], F32); gpreT = fw.sbuf("gpreT", [128, 8], F32); gpostT = fw.sbuf("gpostT", [128, 8], F32)
        esink = fw.sbuf("esink", [128, 8], F32); esink_t = Tok("esink")
        convw = fw.sbuf("convw", [128, 2, 3], F32); convb = fw.sbuf("convb", [128, 2], F32); conv_t = Tok("convp")
        const_t = Tok("const")
        ssq = [fw.sbuf("ssq%d" % l_, [128, NT], F32) for l_ in range(2)]; ssq_t = [Tok("ssq0"), Tok("ssq1")]
        rstd = [fw.sbuf("rstd%d" % l_, [128, NT], F32) for l_ in range(2)]; rstd_t = [Tok("rstd0"), Tok("rstd1")]
        SPECS = {
            "xs": ([128, D], F32, 1), "xe": ([128, D], F32, 1), "etmp": ([128, D], F32, 0), "junk": ([128, D], BF16, 0),
            "ss": ([128, 4], F32, 2), "xn": ([128, D], BF16, 2), "hxT": ([128, 8, 512], BF16, 0),
            "raw_b": ([128, 512], BF16, 0), "rt1": ([128, 512], F32, 0), "rt2": ([128, 512], F32, 0),
            "cs_sb": ([128, 2, 512], F32, 1), "qT_g": ([128, 4, 512], BF16, 0), "sga": ([128, 4, 512], BF16, 0),
            "sg2": ([128, 512], BF16, 0), "bc_sb": ([128, 512], F32, 0), "cy": ([128, 512], F32, 0),
            "hT": ([128, 8, 512], BF16, 0), "PT": ([128, 512], BF16, 4), "rec": ([128, 512], F32, 0), "ntmp": ([128, 512], F32, 0),
            "ufT": ([128, 2, 512], BF16, 0), "zc_sb": ([128, 512], F32, 0), "z_sb": ([128, 4, 8, 64], BF16, 2), "hxT2": ([128, 8, 512], BF16, 0), "modrow": ([2, 3 * D], F32, 0),
            "zctx": ([128, 2, 512], BF16, 0), "tctx": ([128, 2, NCTX + 2], BF16, 0), "fcT": ([128, 2, NCTX], BF16, 0),
            "hb": ([128, 640], BF16, 0), "th": ([128, 4], BF16, 0), "wst": ([128, PW], F32, 2),
            "G_b": ([128, 64, 2, 64], BF16, 0), "zin": ([128, 128, 64], BF16, 0), "Y_sb": ([128, 128, 128], BF16, 0),
            "ident_f": ([128, 128], F32, 0), "ones_f": ([128, 128], F32, 0), "c64": ([64, 128], F32, 0), "s64": ([64, 128], F32, 0),
            "cs256_b": ([128, 2, 2, 256], BF16, 0), "ggc_b": ([128, D], F32, 0),
            "cy2": ([128, 512], F32, 0), "sg3": ([128, 512], BF16, 0), "bc2": ([128, 512], F32, 0), "sgf": ([128, 512], BF16, 0),
        }
        NORM = ["xs", "junk", "ss", "xn", "hxT"]
        ROPE = ["raw_b", "rt1", "rt2", "cs_sb"]
        P2 = ["qT_g", "sga", "sg2", "bc_sb", "cy", "hT", "PT", "rec", "ntmp", "xe", "etmp"]
        P1 = ["ufT", "zc_sb"]
        import types
        V = types.SimpleNamespace()

        ARENA = 38400
        arena = fw.sbuf("arena", [128, ARENA], BF16)

        def carve(off, shape, dt):
            nel = 1
            for d_ in shape[1:]:
                nel *= d_
            sz = nel * (2 if dt == F32 else 1)
            sz = (sz + 15) // 16 * 16
            ap = arena[0:shape[0], off:off + nel * (2 if dt == F32 else 1)]
            if dt == F32:
                ap = ap.bitcast(F32)
            if len(shape) == 3:
                ap = ap.rearrange("p (a b) -> p a b", a=shape[1])
            elif len(shape) == 4:
                ap = ap.rearrange("p (a b c) -> p a b c", a=shape[1], b=shape[2])
            return ap, off + sz

        def alloc(stack, names, slots={}):
            off = 0
            for nm in names:
                shape, dt, ns = SPECS[nm]
                ns = slots.get(nm, ns)
                if ns == 0:
                    ap, off = carve(off, shape, dt)
                    setattr(V, nm, ap); setattr(V, nm + "_t", Tok(nm))
                else:
                    lst = []
                    for _ in range(ns):
                        ap, off = carve(off, shape, dt)
                        lst.append(ap)
                    setattr(V, nm, lst); setattr(V, nm + "_t", [Tok(nm) for _ in range(ns)])
            assert off <= ARENA, "arena overflow %d > %d" % (off, ARENA)
            V.hxT_t = [[Tok(), Tok()] for _ in range(4)]
            V.hT_t = [[Tok() for _ in range(4)] for _ in range(8)]
            V.qT_t = [Tok() for _ in range(4)]
            V.sga_t = [Tok() for _ in range(4)]
            V.ufT_t = [Tok(), Tok()]
            V.Y_t = [Tok(), Tok()]

        xs_sem = [fw.new_sem("xs") for _ in range(2)]; xe_sem = [fw.new_sem("xe") for _ in range(2)]
        cs_sem = [fw.new_sem("cs") for _ in range(2)]; wst_sem = [fw.new_sem("wst") for _ in range(2)]
        zin_sem = fw.new_sem("zin")

        TR = fw.psum("TR", [128, 8, 128], BF16); TR_t = Tok()
        PS = [fw.psum("ps%d" % i, [128, 512]) for i in range(7)]; PS_t = [Tok() for _ in range(7)]
        PJ = [0, 1]; STB = [2, 3]; PVB = 4; YB = [5, 6]
        TR_t.excl = True
        for t_ in PS_t:
            t_.excl = True
        rot = {}

        def nxt(key, n):
            v = rot.get(key, 0) % n
            rot[key] = rot.get(key, 0) + 1
            return v

        csem = fw.new_sem("const"); osem = fw.new_sem("out"); x1sem = fw.new_sem("x1"); zsem = fw.new_sem("zst")
        zsems = [fw.new_sem("zst0"), fw.new_sem("zst1")]
        msem2 = fw.new_sem("modld"); modscr_t = [Tok(), Tok()]
        hsem = fw.new_sem("halo"); ccsem = fw.new_sem("cc"); msem = fw.new_sem("misc")
        x1_t = Tok("x1"); ctx1_t = Tok("ctx1"); out_t = Tok("out")
        ezin_t = [Tok(), Tok()]; ezout_t = [[Tok() for _ in range(8)] for _ in range(2)];
        for t_ in ezin_t + [x1_t, ctx1_t, out_t]:
            t_.multi = True
        ehin_t = [Tok(), Tok()]; ehout_t = [Tok(), Tok()]

        def bc_last(ap, n):
            return bass.AP(ap.tensor, ap.offset, [list(d) for d in ap.ap] + [[0, n]])

        def bc_mid(ap, n):
            d = [list(x) for x in ap.ap]
            return bass.AP(ap.tensor, ap.offset, [d[0], [0, n]] + d[1:])

        for dst, src in [(ident_b, identb_d), (pm_b, pm_d), (m1_b, m1_d), (masks_b, masks_d), (flags, flags_d), (cT, cT_d)]:
            fw.dma(sp, csem, dst[:], src, writes=[const_t])
        fw.op(pool, lambda e: e.memset(AB[:], 0.0), writes=[AB_t])
        fw.op(pool, lambda e: e.memset(v_aug[:], 1.0), writes=v_t)
        fw.op(pool, lambda e: e.memset(vc_aug[:], 1.0), writes=[vc_t])
        fw.op(act, lambda e: e.activation(out=scs[:], in_=cT[:], func=AF.Silu), reads=[const_t], writes=[mod_t])

        def mm(out, lhsT, rhs, start, stop, reads, writes, last):
            fw.op(pe, lambda e: e.matmul(out, lhsT=lhsT, rhs=rhs, start=start, stop=stop), reads=reads, writes=writes, inc=last)

        def cast_copy(eng, out, in_, reads, writes):
            if eng is act:
                fw.op(act, lambda e: e.copy(out=out, in_=in_), reads=reads, writes=writes)
            else:
                fw.op(eng, lambda e: e.tensor_copy(out=out, in_=in_), reads=reads, writes=writes)

        def setup_layer(l, full):
            wst, wst_t, ident_f, ones_f, c64, s64, rt1, rt1_t = V.wst, V.wst_t, V.ident_f, V.ones_f, V.c64, V.s64, V.rt1, V.rt1_t
            st_t = Tok("setup")
            for dst, src in [(ident_f, identf_d), (c64, c64_d), (s64, s64_d)]:
                fw.dma(sp, csem, dst[:], src, writes=[st_t])
            fw.op(pool, lambda e: e.memset(ones_f[:], 1.0), writes=[st_t])
            for dst, src in [(bmodT, bmodT_d[l]), (gpreT, gpreT_d[l]), (gpostT, gpostT_d[l])]:
                fw.dma(sp, csem, dst[:], src, writes=[mod_t])
            fw.dma(sp, csem, convw[:], convwT_d[l], writes=[conv_t])
            fw.dma(sp, csem, convb[:], convbT_d[l], writes=[conv_t])
            fw.dma(sp, csem, esink[:], bass.AP(sink_d.tensor, l * 8, [[0, 128], [1, 8]]), writes=[esink_t])
            for tk in (st_t, mod_t, conv_t, esink_t):
                tk.w = (csem, csem.cnt)
            fw.op(act, lambda e: e.activation(out=esink[:], in_=esink[:], func=AF.Exp), reads=[], writes=[esink_t])
            mps = PS[PVB]; mps_t = PS_t[PVB]
            modrow, modrow_t = V.modrow, V.modrow_t
            wm_v = wmod_d[l].rearrange("(kc p) j -> p kc j", p=128)
            for jq in range(12):
                s_ = nxt("wst", 2)
                wv = wst[s_][:, 0:2048].rearrange("p (kc j) -> p kc j", kc=8)
                fw.dma(sp, wst_sem[s_], wv, wm_v[:, :, jq * 256:(jq + 1) * 256], writes=[wst_t[s_]])
                po = mps[0:2, (jq % 2) * 256:(jq % 2) * 256 + 256]
                for kc in range(8):
                    mm(po, scs[:, kc, :], wv[:, kc, :], kc == 0, kc == 7, [wst_t[s_], mod_t], [mps_t], kc == 7)
                cast_copy(act if jq % 2 == 0 else dve, modrow[0:2, jq * 256:(jq + 1) * 256], po, [mps_t], [modrow_t])
            fw.dma(sp, msem, modscr[l], modrow[0:2, :], reads=[modrow_t], writes=[modscr_t[l]])
            for v_ in range(2):
                for h_ in range(2):
                    src = modscr[l][v_:v_ + 1, h_ * 1536:(h_ + 1) * 1536].rearrange("o (j p) -> (o p) j", p=128)
                    fw.dma(sp, msem2, modT[:, h_ * 12:(h_ + 1) * 12, v_], src, reads=[modscr_t[l]], writes=[mod_t], allow_slow_non_contiguous=True)
            mod_t.w = (msem2, msem2.cnt)
            fw.op(dve, lambda e: e.tensor_tensor(out=modT[:], in0=modT[:], in1=bc_last(bmodT[:], 2), op=ALU.add), reads=[mod_t], writes=[mod_t])
            fw.op(dve, lambda e: e.scalar_tensor_tensor(out=am[:], in0=modT[:, 8:16, :], scalar=1.0, in1=bc_last(gpreT[:], 2), op0=ALU.add, op1=ALU.mult),
                  reads=[mod_t], writes=[mod_t])
            fw.op(dve, lambda e: e.tensor_tensor(out=ggm[:], in0=modT[:, 16:24, :], in1=bc_last(gpostT[:], 2), op=ALU.mult), reads=[mod_t], writes=[mod_t])
            for v in range(1):
                dstb, dst_t = gg_b, gg_t
                for hlf in range(2):
                    yb = PS[YB[hlf]]; yb_t = PS_t[YB[hlf]]
                    for k4 in range(4):
                        kc = hlf * 4 + k4
                        fw.op(dve, lambda e: e.tensor_scalar(out=rt1[:, 0:128], in0=ones_f[:], scalar1=ggm[:, kc, v:v + 1], scalar2=None, op0=ALU.mult),
                              reads=[mod_t, st_t], writes=[rt1_t])
                        mm(yb[:, k4 * 128:(k4 + 1) * 128], rt1[:, 0:128], ident_f[:], True, True, [rt1_t, st_t], [yb_t], True)
                    fw.op(act, lambda e: e.copy(out=dstb[:, hlf * 512:(hlf + 1) * 512], in_=yb[:]), reads=[yb_t], writes=[dst_t])
            for kc in range(8):
                s_ = nxt("wst", 2)
                fw.dma(sp, wst_sem[s_], wst[s_][:], win_d[l, kc * 128:(kc + 1) * 128, :], writes=[wst_t[s_]])
                cast_copy([dve, act][kc % 2], wbf[:, kc, :], wst[s_][:], [wst_t[s_]], [wbf_t])
            for k2 in range(4):
                s_ = nxt("wst", 2)
                wv = wst[s_][:, 0:2048].rearrange("p (a j) -> p a j", a=2)
                fw.dma(sp, wst_sem[s_], wv, wout_d[l, k2 * 256:(k2 + 1) * 256, :].rearrange("(a p) j -> p a j", p=128), writes=[wst_t[s_]])
                cast_copy([dve, act][k2 % 2], wob[:, k2 * 2:k2 * 2 + 2, :], wv, [wst_t[s_]], [wob_t])
            s_ = nxt("wst", 2)
            wfv = wst[s_][0:64, 0:256].rearrange("p (g d) -> p g d", g=4)
            fw.dma(sp, wst_sem[s_], wfv, wf_d[l].rearrange("g c d -> c g d"), writes=[wst_t[s_]])
            for ri, cm in enumerate([c64, s64]):
                pb = PS[STB[ri]]; pb_t = PS_t[STB[ri]]
                mm(pb[:, 0:256], cm[:], wst[s_][0:64, 0:256], True, True, [wst_t[s_], st_t], [pb_t], True)
                for cg in range(2):
                    fw.op(dve, lambda e: e.tensor_copy(out=AB[0:64, cg, ri * 128:ri * 128 + 64], in_=pb[0:64, (2 * cg) * 64:(2 * cg) * 64 + 64]),
                          reads=[pb_t], writes=[AB_t])
                    fw.op(dve, lambda e: e.tensor_copy(out=AB[64:128, cg, ri * 128 + 64:ri * 128 + 128], in_=pb[64:128, (2 * cg + 1) * 64:(2 * cg + 1) * 64 + 64]),
                          reads=[pb_t], writes=[AB_t])

        def norm_T(src_ap, src_toks, v, tl, rs_ap=None, rs_toks=()):
            xs, xs_t, ss, ss_t, xn, xn_t, hxT, hxT_t, junk, junk_t = V.xs, V.xs_t, V.ss, V.ss_t, V.xn, V.xn_t, V.hxT, V.hxT_t, V.junk, V.junk_t
            s_ = nxt("xs", len(xs))
            n_ = nxt("xn", 2)
            fw.dma(sp, xs_sem[s_], xs[s_][:], src_ap, reads=src_toks, writes=[xs_t[s_]])
            if rs_ap is None:
                fw.op(act, lambda e: e.activation(out=junk[:], in_=xs[s_][:], func=AF.Square, accum_out=ss[n_][:, 0:1]),
                      reads=[xs_t[s_]], writes=[junk_t, ss_t[n_]])
                fw.op(dve, lambda e: e.tensor_scalar(out=ss[n_][:, 1:2], in0=ss[n_][:, 0:1], scalar1=1.0 / D, scalar2=EPS, op0=ALU.mult, op1=ALU.add),
                      reads=[ss_t[n_]], writes=[ss_t[n_]])
                fw.op(act, lambda e: e.activation(out=ss[n_][:, 3:4], in_=ss[n_][:, 1:2], func=AF.Sqrt), reads=[ss_t[n_]], writes=[ss_t[n_]])
                fw.op(dve, lambda e: e.reciprocal(out=ss[n_][:, 2:3], in_=ss[n_][:, 3:4]), reads=[ss_t[n_]], writes=[ss_t[n_]])
                rs_ap = ss[n_][:, 2:3]; rs_toks = [ss_t[n_]]
            fw.op(act, lambda e: e.activation(out=xn[n_][:], in_=xs[s_][:], func=AF.Identity, scale=rs_ap),
                  reads=[xs_t[s_]] + list(rs_toks), writes=[xn_t[n_]])
            for kc in range(8):
                fw.op(pe, lambda e: e.transpose(out=TR[:, kc, :], in_=xn[n_][:, kc * 128:(kc + 1) * 128], identity=ident_b[:]),
                      reads=[xn_t[n_], const_t], writes=[TR_t], inc=(kc == 7))
            o = hxT[:, :, tl * 128:(tl + 1) * 128]
            wt = [hxT_t[tl][0], hxT_t[tl][1]]
            fw.op(dve, lambda e: e.tensor_tensor(out=o, in0=TR[:], in1=bc_last(am[:, :, v], 128), op=ALU.mult), reads=[TR_t, mod_t], writes=wt)
            fw.op(dve, lambda e: e.tensor_tensor(out=o, in0=o, in1=bc_last(modT[:, 0:8, v], 128), op=ALU.add), reads=[mod_t], writes=wt)

        def proj_fm(col, ntl):
            hxT, hxT_t = V.hxT, V.hxT_t
            b = PJ[nxt("pj", 2)]
            for kc in range(8):
                mm(PS[b][:, 0:ntl * 128], wbf[:, kc, col:col + 128], hxT[:, kc, 0:ntl * 128], kc == 0, kc == 7,
                   [wbf_t] + [hxT_t[tl][kc % 2] for tl in range(ntl)], [PS_t[b]], kc == 7)
            return PS[b], PS_t[b]

        def load_cs(g):
            cs_sb, cs_t = V.cs_sb, V.cs_sb_t
            s_ = nxt("cs", len(cs_sb))
            fw.dma(sp, cs_sem[s_], cs_sb[s_][:, 0, :], cos_d[:, g * 512:(g + 1) * 512], writes=[cs_t[s_]])
            fw.dma(sp, cs_sem[s_], cs_sb[s_][:, 1, :], sin_d[:, g * 512:(g + 1) * 512], writes=[cs_t[s_]])
            return s_

        def rope_chunk(ps, ps_t, cs_slot, out_ap, out_toks, n, split=False):
            raw_b, raw_t, rt1, rt1_t, rt2, rt2_t, cs_sb, cs_t = V.raw_b, V.raw_b_t, V.rt1, V.rt1_t, V.rt2, V.rt2_t, V.cs_sb, V.cs_sb_t
            fw.op(act, lambda e: e.copy(out=raw_b[:, 0:n], in_=ps[:, 0:n]), reads=[ps_t], writes=[raw_t])
            if os.environ.get("KR2"):
                return
            fw.op(dve, lambda e: e.tensor_tensor(out=rt1[:, 0:n], in0=ps[:, 0:n], in1=cs_sb[cs_slot][:, 0, 0:n], op=ALU.mult),
                  reads=[ps_t, cs_t[cs_slot]], writes=[rt1_t])
            def part_b():
                b = PJ[nxt("pj", 2)]
                mm(PS[b][:, 0:n], pm_b[:], raw_b[:, 0:n], True, True, [raw_t, const_t], [PS_t[b]], True)
                fw.op(dve, lambda e: e.tensor_tensor(out=rt2[:, 0:n], in0=PS[b][:, 0:n], in1=cs_sb[cs_slot][:, 1, 0:n], op=ALU.mult),
                      reads=[PS_t[b], cs_t[cs_slot]], writes=[rt2_t])
                fw.op(dve, lambda e: e.tensor_tensor(out=out_ap, in0=rt1[:, 0:n], in1=rt2[:, 0:n], op=ALU.add),
                      reads=[rt1_t, rt2_t], writes=out_toks)

            if split:
                return part_b
            part_b()

        def prep_tile(l, g, tl, is_ctx, ctx_src1):
            if is_ctx:
                src = (ctx_d if (l == 0 or not ctx_src1) else ctx1_d)[tl * 128:(tl + 1) * 128, :]
                stoks = [] if (l == 0 or not ctx_src1) else [ctx1_t]
                norm_T(src, stoks, 1, tl)
            else:
                i = g * 4 + tl
                src = (x_d if l == 0 else x1_d)[i * 128:(i + 1) * 128, :]
                stoks = [] if l == 0 else [x1_t]
                norm_T(src, stoks, 0, tl, rstd[l][:, i:i + 1], [rstd_t[l]])

        def finish_rstd(l):
            fw.op(dve, lambda e: e.tensor_scalar(out=ssq[l][:], in0=ssq[l][:], scalar1=1.0 / D, scalar2=EPS, op0=ALU.mult, op1=ALU.add),
                  reads=[ssq_t[l]], writes=[ssq_t[l]])
            fw.op(act, lambda e: e.activation(out=ssq[l][:], in_=ssq[l][:], func=AF.Sqrt), reads=[ssq_t[l]], writes=[ssq_t[l]])
            fw.op(dve, lambda e: e.reciprocal(out=rstd[l][:], in_=ssq[l][:]), reads=[ssq_t[l]], writes=[rstd_t[l]])

        def prepass0():
            xs, xs_t, junk, junk_t = V.xs, V.xs_t, V.junk, V.junk_t
            for i in range(NT):
                s_ = nxt("xs", len(xs))
                fw.dma(pool, xs_sem[s_], xs[s_][:], x_d[i * 128:(i + 1) * 128, :], writes=[xs_t[s_]])
                fw.op(act, lambda e: e.activation(out=junk[:], in_=xs[s_][:], func=AF.Square, accum_out=ssq[0][:, i:i + 1]),
                      reads=[xs_t[s_]], writes=[junk_t, ssq_t[0]])
            finish_rstd(0)

        def use_hx(k):
            V.hxT, V.hxT_t = HXS[k]

        def phase1a_group(l, g, is_ctx, full, prepped=False, prep_next=False):
            if not is_ctx:
                use_hx(g % 2)
            hxT, hxT_t = V.hxT, V.hxT_t
            pq = [tl for tl in range(4)] if (prep_next and not is_ctx) else []

            def prep_one():
                if pq:
                    use_hx((g + 1) % 2)
                    prep_tile(l, g + 1, pq.pop(0), False, True)
                    use_hx(g % 2)
            ntl = 2 if is_ctx else 4
            n = ntl * 128
            v = 1 if is_ctx else 0
            if not prepped:
                for tl in range(ntl):
                    prep_tile(l, g, tl, is_ctx, True)
            import os
            KSUB = int(os.environ.get("KSUB", "99"))
            if KSUB <= 1:
                return
            ps, ps_t = proj_fm(C_K, ntl)
            if is_ctx:
                fw.op(act, lambda e: e.copy(out=kcT[:], in_=ps[:, 0:n]), reads=[ps_t], writes=[kcT_t])
            elif int(os.environ.get("KR", "99")) <= 0:
                pass
            else:
                cslot = load_cs(g)
                rope_chunk(ps, ps_t, cslot, kT_all[:, (1 + g * 4) * 128:(1 + g * 4) * 128 + 512], [kT_t[1 + g * 4 + i] for i in range(4)], 512)
            prep_one()
            vb = PS[PVB]; vb_t = PS_t[PVB]
            for tl in range(ntl):
                for kc in range(8):
                    mm(vb[:, tl * 128:(tl + 1) * 128], hxT[:, kc, tl * 128:(tl + 1) * 128], wbf[:, kc, C_V:C_V + 128], kc == 0, kc == 7,
                       [wbf_t, hxT_t[tl][kc % 2]], [vb_t], kc == 7 and tl == ntl - 1)
            vbv = vb[:, 0:n].rearrange("p (t c) -> p t c", c=128)
            if is_ctx:
                fw.op(dve, lambda e: e.tensor_copy(out=vc_aug[:, :, 0, 0:64], in_=vbv[:, :, 0:64]), reads=[vb_t], writes=[vc_t])
                fw.op(dve, lambda e: e.tensor_copy(out=vc_aug[:, :, 1, 64:128], in_=vbv[:, :, 64:128]), reads=[vb_t], writes=[vc_t])
            else:
                vt = [v_t[1 + g * 4 + i] for i in range(4)]
                fw.op(dve, lambda e: e.tensor_copy(out=v_aug[:, 1 + g * 4:5 + g * 4, 0, 0:64], in_=vbv[:, :, 0:64]), reads=[vb_t], writes=vt)
                fw.op(dve, lambda e: e.tensor_copy(out=v_aug[:, 1 + g * 4:5 + g * 4, 1, 64:128], in_=vbv[:, :, 64:128]), reads=[vb_t], writes=vt)
            if is_ctx and not full:
                return
            prep_one()
            ufT, ufT_t, zc_sb, zc_t = V.ufT, V.ufT_t, V.zc_sb, V.zc_sb_t
            for cg in range(2):
                ps, ps_t = proj_fm(C_UF + cg * 128, ntl)
                fw.op(act, lambda e: e.copy(out=ufT[:, cg, 0:n], in_=ps[:, 0:n]), reads=[ps_t], writes=[ufT_t[cg]])
            prep_one()
            zs = g % 2
            for tl in range(ntl):
                zb = YB[nxt("zb", 2)]
                for cg in range(2):
                    mm(PS[zb][:, cg * 256:(cg + 1) * 256], ufT[:, cg, tl * 128:(tl + 1) * 128], AB[:, cg, :], True, True,
                       [ufT_t[cg], AB_t], [PS_t[zb]], cg == 1)
                if is_ctx:
                    fw.op(dve, lambda e: e.tensor_copy(out=V.zctx[:, tl, :], in_=PS[zb][:]), reads=[PS_t[zb]], writes=[V.zctx_t])
                else:
                    z_sb, z_t = V.z_sb, V.z_sb_t
                    for cg in range(2):
                        src = PS[zb][:, cg * 256:(cg + 1) * 256].rearrange("p (ri h c) -> p h ri c", ri=2, h=2)
                        dst = z_sb[zs][:, tl, cg * 4:(cg + 1) * 4, :].rearrange("p (h ri) c -> p h ri c", h=2)
                        cast_copy(dve if cg == 0 else act, dst, src, [PS_t[zb]], [z_t[zs]])
            if not is_ctx:
                for q8 in range(8):
                    dz = ez_in[l][q8].rearrange("p (x c) -> (p x) c", c=64)[g * 512:(g + 1) * 512, :].rearrange("(tl tok) c -> tok tl c", tl=4)
                    fw.dma([sp, pool][q8 % 2], zsems[zs], dz, V.z_sb[zs][:, :, q8, :], reads=[V.z_sb_t[zs]], writes=[ezin_t[l]])
            prep_one()
            for j in range(2):
                ps, ps_t = proj_fm(C_ZC + j * 128, ntl)
                fw.op(act, lambda e: e.copy(out=zc_sb[:, 0:n], in_=ps[:, 0:n]), reads=[ps_t], writes=[zc_t])
                ps2, ps2_t = proj_fm(C_CC + j * 128, ntl)
                if is_ctx:
                    o = V.tctx[:, j, 1:1 + n]; ot = [V.tctx_t]
                else:
                    o = t_all[:, j, 1 + g * 512:1 + g * 512 + 512]; ot = [t_t[1 + g]]
                fw.op(dve, lambda e: e.tensor_tensor(out=o, in0=ps2[:, 0:n], in1=zc_sb[:, 0:n], op=ALU.mult), reads=[ps2_t, zc_t], writes=ot)
            while pq:
                prep_one()

        def _p1a_tail(l, g, prep_next):
            if prep_next:
                for tl in range(4):
                    prep_tile(l, g + 1, tl, False, True)

        def exchange(l):
            hb, hb_t, th, th_t = V.hb, V.hb_t, V.th, V.th_t
            cps = [(hb[:, 0:128], kT_all[:, 128:256], [kT_t[1]]), (hb[:, 128:256], kT_all[:, NT * 128:(NT + 1) * 128], [kT_t[NT]]),
                   (hb[:, 256:320], v_aug[:, 1, 0, 0:64], [v_t[1]]), (hb[:, 320:384], v_aug[:, 1, 1, 64:128], [v_t[1]]),
                   (hb[:, 384:448], v_aug[:, NT, 0, 0:64], [v_t[NT]]), (hb[:, 448:512], v_aug[:, NT, 1, 64:128], [v_t[NT]]),
                   (hb[:, 512:514], t_all[:, :, 1], [t_t[1]]), (hb[:, 514:516], t_all[:, :, T0], [t_t[NG]])]
            for o, i_, tk in cps:
                fw.op(pool, lambda e: e.tensor_copy(out=o, in_=i_), reads=tk, writes=[hb_t])
            fw.dma(pool, hsem, eh_in[l][:, 0:516], hb[:, 0:516], reads=[hb_t], writes=[ehin_t[l]])
            for (i_t, o_t, i_ap, o_ap) in [(ehin_t[l], ehout_t[l], eh_in[l], eh_out[l])] + [(ezin_t[l], ezout_t[l][q8], ez_in[l][q8], ez_out[l][q8]) for q8 in range(8)]:
                fw._deps(pool, [i_t], [o_t])
                inst = nc.gpsimd.collective_compute("AllGather", ALU.bypass, replica_groups=[[0, 1], [2, 3], [4, 5], [6, 7]], ins=[i_ap], outs=[o_ap])
                ccsem.cnt += 1
                inst.then_inc(ccsem.h, 1)
                fw._post((ccsem, ccsem.cnt), [i_t], [o_t])
            eo = eh_out[l]
            ups = [(kT_all[:, 0:128], eo[0:128, 128:256], kT_t[0]), (kT_all[:, (NT + 1) * 128:(NT + 2) * 128], eo[128:256, 0:128], kT_t[NT + 1]),
                   (v_aug[:, 0, 0, 0:64], eo[0:128, 384:448], v_t[0]), (v_aug[:, 0, 1, 64:128], eo[0:128, 448:512], v_t[0]),
                   (v_aug[:, NT + 1, 0, 0:64], eo[128:256, 256:320], v_t[NT + 1]), (v_aug[:, NT + 1, 1, 64:128], eo[128:256, 320:384], v_t[NT + 1]),
                   (th[:, 0:2], eo[0:128, 514:516], th_t), (th[:, 2:4], eo[128:256, 512:514], th_t)]
            for o, i_, tk in ups:
                fw.dma(sp, hsem, o, i_, reads=[ehout_t[l]], writes=[tk])
            for _, _, tk in ups:
                tk.w = (hsem, hsem.cnt)
            fw.op(dve, lambda e: e.tensor_scalar(out=t_all[:, :, 0], in0=th[:, 0:2], scalar1=flags[:, 0:1], scalar2=None, op0=ALU.mult),
                  reads=[th_t, const_t], writes=[t_t[0]])
            fw.op(dve, lambda e: e.tensor_scalar(out=t_all[:, :, T0 + 1], in0=th[:, 2:4], scalar1=flags[:, 1:2], scalar2=None, op0=ALU.mult),
                  reads=[th_t, const_t], writes=[t_t[NG + 1]])

        def fft(l):
            G_b, G_t, zin, zin_t, Y_sb, Y_t = V.G_b, V.G_b_t, V.zin, V.zin_t, V.Y_sb, V.Y_t
            fw.dma(sp, csem, G_b[:], g_d, writes=[G_t])
            zo = ez_out[l]
            for hh in range(2):
                for qq in range(2):
                    qt = hh * 2 + qq
                    for ri in range(2):
                        for r in range(2):
                            src = zo[qt * 2 + ri][r * 128:(r + 1) * 128, :].rearrange("(a x) f -> a (x f)", a=32).rearrange("a (n c) -> a n c", c=64)
                            p0 = ri * 64 + r * 32
                            fw.dma(sp, zin_sem, zin[p0:p0 + 32, :, :], src, reads=[ezout_t[l][qt * 2 + ri]], writes=[zin_t])
                    for c4 in range(16):
                        b = PJ[nxt("pj", 2)]
                        for ci in range(4):
                            c = c4 * 4 + ci
                            mm(PS[b][:, ci * 128:(ci + 1) * 128], zin[:, :, c], m1_b[:], True, True, [zin_t, const_t], [PS_t[b]], ci == 3)
                        src = PS[b][:].rearrange("p (c j) -> p j c", c=4)
                        dst = Y_sb[:, :, qq * 64 + c4 * 4:qq * 64 + c4 * 4 + 4]
                        cast_copy(dve if c4 % 2 == 0 else act, dst, src, [PS_t[b]], [Y_t[qq]])
                fv = fT[:, hh, :].rearrange("p (k2 k1) -> p k1 k2", k1=64)
                for k8 in range(8):
                    b = STB[nxt("st", 2)]
                    for ki in range(8):
                        k1 = k8 * 8 + ki
                        for ri in range(2):
                            mm(PS[b][:, ki * 64:(ki + 1) * 64], Y_sb[:, ri * 64 + k1, :], G_b[:, k1, ri, :], ri == 0, ri == 1,
                               [Y_t[0], Y_t[1], G_t], [PS_t[b]], ki == 7 and ri == 1)
                    src = PS[b][:].rearrange("p (k1 k2) -> p k1 k2", k1=8)
                    dst = fv[:, k8 * 8:(k8 + 1) * 8, :]
                    cast_copy(dve if k8 % 2 == 0 else act, dst, src, [PS_t[b]], [fT_t[hh]])

        def attention_tile(tl, qcol, chunks, slotmap=None, defer=False):
            qT_g, qT_t, PT, PT_t, rec, rec_t, ntmp, ntmp_t, sga, sga_t, hT, hT_t = (V.qT_g, V.qT_t, V.PT, V.PT_t, V.rec, V.rec_t, V.ntmp, V.ntmp_t,
                                                                                     V.sga, V.sga_t, V.hT, V.hT_t)
            slotmap = dict(slotmap or {})
            slot = [0]
            pending_norm = [None]

            def fill():
                for f in slotmap.pop(slot[0], []):
                    f()
                slot[0] += 1

            for s_ in range(2):
                P0 = s_ * 64
                rn = slice(P0, P0 + 64)
                rd = slice(64 - P0, 128 - P0)
                pvi = [PVB, PJ[1]][s_]
                pvb = PS[pvi]; pvb_t = PS_t[pvi]
                nch = len(chunks)
                pts = [None] * nch

                def qk(ci):
                    kten, kcol, ktoks, _, _, midx = chunks[ci]
                    b = STB[nxt("st", 2)]
                    mm(PS[b][:], kten[rn, kcol:kcol + 128], qT_g[rn, :, qcol:qcol + 128], True, midx is None, ktoks + qT_t, [PS_t[b]], midx is None)
                    if midx is not None:
                        mm(PS[b][:].rearrange("p (j q) -> p j q", j=4), ident_b[:], bc_mid(masks_b[:, midx, :], 4), False, True, [const_t], [PS_t[b]], True)
                    p = nxt("pt", 4)
                    fw.op(act, lambda e: e.activation(out=PT[p][:], in_=PS[b][:], func=AF.Exp, scale=0.125), reads=[PS_t[b]], writes=[PT_t[p]])
                    pts[ci] = p

                def pv(ci):
                    _, _, _, vfn, vtoks, _ = chunks[ci]
                    p = pts[ci]
                    mm(pvb[:], vfn(s_), PT[p][:], ci == 0, ci == nch - 1, vtoks + [PT_t[p]], [pvb_t], ci == nch - 1)

                qk(0)
                if nch > 1:
                    qk(1)
                for ci in range(nch):
                    pv(ci)
                    if ci + 2 < nch:
                        qk(ci + 2)
                    fill()
                def normalize(s_=s_, rn=rn, rd=rd, pvb=pvb, pvb_t=pvb_t):
                    es = bc_last(esink[rd, s_ * 4:(s_ + 1) * 4], 128)
                    r3 = rec[rd, :].rearrange("p (j q) -> p j q", j=4)
                    fw.op(dve, lambda e: e.tensor_tensor(out=r3, in0=pvb[rd, :].rearrange("p (j q) -> p j q", j=4), in1=es, op=ALU.add),
                          reads=[pvb_t, esink_t], writes=[rec_t])
                    fw.op(act, lambda e: e.activation(out=rec[rd, :], in_=rec[rd, :], func=AF.Ln), reads=[rec_t], writes=[rec_t])
                    fw.op(act, lambda e: e.activation(out=rec[rd, :], in_=rec[rd, :], func=AF.Exp, scale=-1.0), reads=[rec_t], writes=[rec_t])
                    fw.op(dve, lambda e: e.tensor_tensor(out=ntmp[rn, :], in0=pvb[rn, :], in1=rec[rd, :], op=ALU.mult), reads=[pvb_t, rec_t], writes=[ntmp_t])
                    fw.op(dve, lambda e: e.tensor_tensor(out=hT[rn, 0:4, qcol:qcol + 128], in0=ntmp[rn, :].rearrange("p (j q) -> p j q", j=4),
                                                        in1=sga[rn, :, qcol:qcol + 128], op=ALU.mult),
                          reads=[ntmp_t] + sga_t, writes=[hT_t[j][tl] for j in range(4)])

                if defer and s_ == 0:
                    slotmap.setdefault(6, []).insert(0, normalize)
                elif defer:
                    pending_norm[0] = normalize
                else:
                    normalize()
            for k_ in sorted(slotmap):
                for f in slotmap[k_]:
                    f()
            return pending_norm[0]

        def phase2_group(l, g, is_ctx, prepped=False, prep_next=False):
            (qT_g, qT_t, sga, sga_t, sg2, sg2_t, bc_sb, bc_t, cy, cy_t, hT, hT_t, xe, xe_t, etmp, etmp_t, junk, junk_t, ss, ss_t) = (
                V.qT_g, V.qT_t, V.sga, V.sga_t, V.sg2, V.sg2_t, V.bc_sb, V.bc_sb_t, V.cy, V.cy_t, V.hT, V.hT_t, V.xe, V.xe_t, V.etmp, V.etmp_t,
                V.junk, V.junk_t, V.ss, V.ss_t)
            ntl = 2 if is_ctx else 4
            n = ntl * 128
            v = 1 if is_ctx else 0
            last = (l == depth - 1)
            if not prepped:
                for tl in range(ntl):
                    prep_tile(l, g, tl, is_ctx, False)
            if not is_ctx:
                cslot = load_cs(g)
            sgf, sgf_t = (sg2, sg2_t) if is_ctx else (V.sgf, V.sgf_t)

            def do_gf(j):
                ps, ps_t = proj_fm(C_GF + j * 128, ntl)
                fw.op(act, lambda e: e.activation(out=sgf[:, 0:n], in_=ps[:, 0:n], func=AF.Silu), reads=[ps_t], writes=[sgf_t])
                if is_ctx:
                    fsrc = V.fcT[:, j, :]; ftk = [V.fcT_t]
                else:
                    fsrc = fT[:, j, g * 512:(g + 1) * 512]; ftk = [fT_t[j]]
                fw.op(pool, lambda e: e.tensor_tensor(out=hT[:, 4 + j, 0:n], in0=sgf[:, 0:n], in1=fsrc, op=ALU.mult),
                      reads=[sgf_t] + ftk, writes=[hT_t[4 + j][tl] for tl in range(ntl)])

            def do_conv(j):
                if is_ctx:
                    o0 = j * 256
                    cyj, cyj_t, sgj, sgj_t, bcj, bcj_t = cy[:, o0:o0 + n], cy_t, sg2[:, o0:o0 + n], sg2_t, bc_sb[:, o0:o0 + n], bc_t
                elif j == 0:
                    cyj, cyj_t, sgj, sgj_t, bcj, bcj_t = cy[:, 0:n], cy_t, sg2[:, 0:n], sg2_t, bc_sb[:, 0:n], bc_t
                else:
                    cyj, cyj_t, sgj, sgj_t, bcj, bcj_t = V.cy2[:, 0:n], V.cy2_t, V.sg3[:, 0:n], V.sg3_t, V.bc2[:, 0:n], V.bc2_t
                ps, ps_t = proj_fm(C_BC + j * 128, ntl)
                fw.op(act, lambda e: e.copy(out=bcj, in_=ps[:, 0:n]), reads=[ps_t], writes=[bcj_t])
                ps2, ps2_t = proj_fm(C_GC + j * 128, ntl)
                fw.op(act, lambda e: e.activation(out=sgj, in_=ps2[:, 0:n], func=AF.Silu), reads=[ps2_t], writes=[sgj_t])
                if is_ctx:
                    tsrc = V.tctx; c0 = 0; ttk = [V.tctx_t]
                else:
                    tsrc = t_all; c0 = g * 512; ttk = [t_t[g], t_t[g + 1], t_t[g + 2]]
                ce = pool if j == 0 else dve
                fw.op(ce, lambda e: e.tensor_scalar(out=cyj, in0=tsrc[:, j, c0:c0 + n], scalar1=convw[:, j, 0:1], scalar2=convb[:, j:j + 1],
                                                    op0=ALU.mult, op1=ALU.add), reads=ttk + [conv_t], writes=[cyj_t])
                for tap in (1, 2):
                    if ce is pool:
                        fw.op(pool, lambda e: e.tensor_scalar(out=V.ntmp[:, 0:n], in0=tsrc[:, j, c0 + tap:c0 + tap + n], scalar1=convw[:, j, tap:tap + 1],
                                                              scalar2=None, op0=ALU.mult), reads=ttk + [conv_t], writes=[V.ntmp_t])
                        fw.op(pool, lambda e: e.tensor_tensor(out=cyj, in0=cyj, in1=V.ntmp[:, 0:n], op=ALU.add), reads=[V.ntmp_t], writes=[cyj_t])
                    else:
                        fw.op(dve, lambda e: e.scalar_tensor_tensor(out=cyj, in0=tsrc[:, j, c0 + tap:c0 + tap + n], scalar=convw[:, j, tap:tap + 1],
                                                                  in1=cyj, op0=ALU.mult, op1=ALU.add), reads=ttk + [conv_t], writes=[cyj_t])
                fw.op(ce, lambda e: e.tensor_tensor(out=cyj, in0=cyj, in1=bcj, op=ALU.mult), reads=[bcj_t], writes=[cyj_t])
                fw.op(ce, lambda e: e.tensor_tensor(out=hT[:, 6 + j, 0:n], in0=cyj, in1=sgj, op=ALU.mult),
                      reads=[cyj_t, sgj_t], writes=[hT_t[6 + j][tl] for tl in range(ntl)])
            def do_ga(j):
                ps, ps_t = proj_fm(C_GA + j * 128, ntl)
                fw.op(act, lambda e: e.activation(out=sga[:, j, 0:n], in_=ps[:, 0:n], func=AF.Silu), reads=[ps_t], writes=[sga_t[j]])

            def do_q(j):
                ps, ps_t = proj_fm(C_Q + j * 128, ntl)
                if is_ctx:
                    fw.op(act, lambda e: e.copy(out=qT_g[:, j, 0:n], in_=ps[:, 0:n]), reads=[ps_t], writes=[qT_t[j]])
                else:
                    return rope_chunk(ps, ps_t, cslot, qT_g[:, j, :], [qT_t[j]], 512, split=True)
            if is_ctx:
                for j in range(2):
                    do_gf(j)
                for j in range(2):
                    do_conv(j)
                for j in range(4):
                    do_ga(j)
                for j in range(4):
                    do_q(j)
                tile_fill = [{} for _ in range(ntl)]
            else:
                qb = do_q(0); do_conv(0); qb()
                qb = do_q(1); do_ga(0); do_ga(1); qb()
                qb = do_q(2); do_ga(2); do_ga(3); qb()
                qb = do_q(3); do_gf(0); do_gf(1); qb()
                do_conv(1)
                pf = [(lambda tl=tl: prep_tile(l, g + 1, tl, False, False)) for tl in range(4)] if prep_next else []
                tile_fill = [{}, {}, {}, {}]
                if pf:
                    tile_fill[1] = {0: [pf[0]], 5: [pf[1]]}
                    tile_fill[2] = {0: [pf[2]], 5: [pf[3]]}
            ggb, ggb_t = (V.ggc_b, V.ggc_b_t) if is_ctx else (gg_b, gg_t)
            for tl in range(ntl):
                cch = [(kcT, 0, [kcT_t], (lambda s_: vc_aug[:, 0, s_, :]), [vc_t], None),
                       (kcT, 128, [kcT_t], (lambda s_: vc_aug[:, 1, s_, :]), [vc_t], None)]
                if not is_ctx:
                    i = g * 4 + tl
                    for dlt, midx in [(0, 0 if i == 0 else 1), (1, None), (2, 3 if i == NT - 1 else 2)]:
                        ti = i + dlt
                        cch.append((kT_all, ti * 128, [kT_t[ti]], (lambda s_, ti=ti: v_aug[:, ti, s_, :]), [v_t[ti]], midx))
                sm = {k_: list(v_) for k_, v_ in tile_fill[tl].items()}

                def outproj(hlf, tl=tl):
                    yb = PS[YB[hlf]]; yb_t = PS_t[YB[hlf]]
                    for j in range(8):
                        mm(yb[:], hT[:, j, tl * 128:(tl + 1) * 128], wob[:, j, hlf * 512:(hlf + 1) * 512], j == 0, j == 7,
                           [hT_t[j][tl], wob_t], [yb_t], j == 7)

                if is_ctx:
                    attention_tile(tl, tl * 128, cch, sm)
                    outproj(0); outproj(1)
                    for f in make_epilogue(l, g, tl, is_ctx, last, ggb, ggb_t):
                        f()
                    continue
                if PNORM:
                    pn, op_prev, st_prev = PNORM.pop()
                    sm.setdefault(0, []).insert(0, pn)
                    sm.setdefault(1, []).append(lambda: op_prev(0))
                    sm.setdefault(2, []).append(lambda: op_prev(1))
                    PEND.extend(st_prev)
                for k_, f in zip((5, 6, 7, 8), PEND):
                    sm.setdefault(k_, []).append(f)
                del PEND[:]
                pn = attention_tile(tl, tl * 128, cch, sm, defer=True)
                stages = make_epilogue(l, g, tl, is_ctx, last, ggb, ggb_t)
                if tl == ntl - 1:
                    pn(); outproj(0); outproj(1)
                    PEND.extend(stages)
                else:
                    PNORM.append((pn, outproj, stages))

        PNORM = []
        HXS = []
        PEND = []

        def make_epilogue(l, g, tl, is_ctx, last, ggb, ggb_t):
            xe, xe_t, etmp, etmp_t, junk, junk_t, ss, ss_t = V.xe, V.xe_t, V.etmp, V.etmp_t, V.junk, V.junk_t, V.ss, V.ss_t
            st = {}
            if is_ctx:
                rsrc = ctx_d[tl * 128:(tl + 1) * 128, :]; rtk = []
                r0 = tl * 128
            else:
                r0 = (g * 4 + tl) * 128
                rsrc = (x_d if l == 0 else x1_d)[r0:r0 + 128, :]; rtk = [] if l == 0 else [x1_t]

            def stage_a():
                st["es"] = nxt("xe", len(xe)); st["s2"] = nxt("xn", 2)
                es_, s2 = st["es"], st["s2"]
                fw.dma(sp, xe_sem[es_], xe[es_][:], rsrc, reads=rtk, writes=[xe_t[es_]])
                for hlf in range(2):
                    fw.op(act, lambda e: e.activation(out=junk[:, hlf * 512:(hlf + 1) * 512], in_=PS[YB[hlf]][:], func=AF.Square, accum_out=ss[s2][:, hlf:hlf + 1]),
                          reads=[PS_t[YB[hlf]]], writes=[junk_t, ss_t[s2]])

            def stage_b():
                s2 = st["s2"]
                fw.op(dve, lambda e: e.tensor_tensor(out=ss[s2][:, 2:3], in0=ss[s2][:, 0:1], in1=ss[s2][:, 1:2], op=ALU.add), reads=[ss_t[s2]], writes=[ss_t[s2]])
                fw.op(dve, lambda e: e.tensor_scalar(out=ss[s2][:, 2:3], in0=ss[s2][:, 2:3], scalar1=1.0 / D, scalar2=EPS, op0=ALU.mult, op1=ALU.add),
                      reads=[ss_t[s2]], writes=[ss_t[s2]])
                fw.op(act, lambda e: e.activation(out=ss[s2][:, 0:1], in_=ss[s2][:, 2:3], func=AF.Ln), reads=[ss_t[s2]], writes=[ss_t[s2]])
                fw.op(act, lambda e: e.activation(out=ss[s2][:, 3:4], in_=ss[s2][:, 0:1], func=AF.Exp, scale=-0.5), reads=[ss_t[s2]], writes=[ss_t[s2]])

            def stage_c():
                es_, s2 = st["es"], st["s2"]
                for hlf in range(2):
                    fw.op(dve, lambda e: e.scalar_tensor_tensor(out=etmp[:, hlf * 512:(hlf + 1) * 512], in0=PS[YB[hlf]][:], scalar=ss[s2][:, 3:4],
                                                              in1=ggb[:, hlf * 512:(hlf + 1) * 512], op0=ALU.mult, op1=ALU.mult),
                          reads=[PS_t[YB[hlf]], ss_t[s2], ggb_t], writes=[etmp_t])
                fw.op(dve, lambda e: e.tensor_tensor(out=xe[es_][:], in0=xe[es_][:], in1=etmp[:], op=ALU.add), reads=[etmp_t], writes=[xe_t[es_]])

            def stage_d():
                es_ = st["es"]
                if (not is_ctx) and (not last):
                    fw.op(act, lambda e: e.activation(out=junk[:], in_=xe[es_][:], func=AF.Square, accum_out=ssq[l + 1][:, g * 4 + tl:g * 4 + tl + 1]),
                          reads=[xe_t[es_]], writes=[junk_t, ssq_t[l + 1]])
                if is_ctx:
                    fw.dma(pool, msem, ctx1_d[r0:r0 + 128, :], xe[es_][:], reads=[xe_t[es_]], writes=[ctx1_t])
                elif not last:
                    fw.dma(pool, x1sem, x1_d[r0:r0 + 128, :], xe[es_][:], reads=[xe_t[es_]], writes=[x1_t])
                else:
                    fw.dma(pool, osem, out_d[r0:r0 + 128, :], xe[es_][:], reads=[xe_t[es_]], writes=[out_t])

            return [stage_a, stage_b, stage_c, stage_d]

        def ctx_fourier():
            zctx, zctx_t, cs256_b, cs256_t, fcT, fcT_t = V.zctx, V.zctx_t, V.cs256_b, V.cs256_b_t, V.fcT, V.fcT_t
            fw.dma(sp, csem, cs256_b[:], cs256_d, writes=[cs256_t])
            for cg in range(2):
                b = STB[nxt("st", 2)]
                i = 0
                for nt in range(2):
                    for ri in range(2):
                        mm(PS[b][:, 0:256], zctx[:, nt, cg * 256 + ri * 128:cg * 256 + ri * 128 + 128], cs256_b[:, nt, ri, :], i == 0, i == 3,
                           [zctx_t, cs256_t], [PS_t[b]], i == 3)
                        i += 1
                fw.op(dve, lambda e: e.tensor_copy(out=fcT[:, cg, :], in_=PS[b][:, 0:256]), reads=[PS_t[b]], writes=[fcT_t])

        import os
        KSTOP = int(os.environ.get("KSTOP", "99"))
        for l in range(depth):
            full = (l < depth - 1)
            with contextlib.ExitStack() as sc:
                alloc(sc, ["wst", "ident_f", "ones_f", "c64", "s64", "rt1", "xs", "junk", "modrow"], {"xs": 2})
                if l == 0:
                    prepass0()
                setup_layer(l, full)
                fw.barrier()
            if KSTOP <= 0:
                break
            with contextlib.ExitStack() as sc:
                names = NORM + P1
                if full:
                    names = names + ["zctx", "tctx", "fcT", "cs256_b"] + P2 + ["ggc_b", "ones_f", "ident_f", "rt1"]
                alloc(sc, names)
                if full:
                    st_t = Tok()
                    fw.dma(sp, csem, V.ident_f[:], identf_d, writes=[st_t])
                    fw.op(pool, lambda e: e.memset(V.ones_f[:], 1.0), writes=[st_t])
                    fw.op(pool, lambda e: e.memset(V.tctx[:], 0.0), writes=[V.tctx_t])
                    for hlf in range(2):
                        yb = PS[YB[hlf]]; yb_t = PS_t[YB[hlf]]
                        for k4 in range(4):
                            kc = hlf * 4 + k4
                            fw.op(dve, lambda e: e.tensor_scalar(out=V.rt1[:, 0:128], in0=V.ones_f[:], scalar1=ggm[:, kc, 1:2], scalar2=None, op0=ALU.mult),
                                  reads=[mod_t, st_t], writes=[V.rt1_t])
                            mm(yb[:, k4 * 128:(k4 + 1) * 128], V.rt1[:, 0:128], V.ident_f[:], True, True, [V.rt1_t, st_t], [yb_t], True)
                        fw.op(act, lambda e: e.copy(out=V.ggc_b[:, hlf * 512:(hlf + 1) * 512], in_=yb[:]), reads=[yb_t], writes=[V.ggc_b_t])
                phase1a_group(l, 0, True, full)
                if full:
                    ctx_fourier()
                    phase2_group(l, 0, True)
                fw.barrier()
            if KSTOP <= 1:
                break
            with contextlib.ExitStack() as sc:
                alloc(sc, NORM + P1 + ROPE + ["z_sb", "hb", "th", "hxT2"], {"xs": 2})
                HXS[:] = [(V.hxT, V.hxT_t), (V.hxT2, [[Tok(), Tok()] for _ in range(4)])]
                use_hx(0)
                for tl in range(4):
                    prep_tile(l, 0, tl, False, True)
                for g in range(NG):
                    phase1a_group(l, g, False, True, prepped=True, prep_next=(g < NG - 1))
                if KSTOP <= 2:
                    fw.barrier()
                    break
                exchange(l)
                if DBG and l == 0:
                    fw.dma(sp, osem, dbg["dbg_kT"], kT_all[:], reads=kT_t, writes=[out_t])
                    fw.dma(sp, osem, dbg["dbg_v"], v_aug[:].rearrange("p a b c -> p (a b c)"), reads=v_t, writes=[out_t])
                    fw.dma(sp, osem, dbg["dbg_t"], t_all[:].rearrange("p a b -> p (a b)"), reads=t_t, writes=[out_t])
                fw.barrier()
            if KSTOP <= 3:
                break
            with contextlib.ExitStack() as sc:
                alloc(sc, ["G_b", "zin", "Y_sb"])
                fft(l)
                if DBG and l == 0:
                    fw.dma(sp, osem, dbg["dbg_fT"], fT[:].rearrange("p a b -> p (a b)"), reads=fT_t, writes=[out_t])
                fw.barrier()
            if KSTOP <= 4:
                break
            with contextlib.ExitStack() as sc:
                alloc(sc, NORM + ROPE + P2 + ["cy2", "sg3", "bc2", "sgf"], {"xs": 2})
                for tl in range(4):
                    prep_tile(l, 0, tl, False, False)
                for g in range(NG):
                    phase2_group(l, g, False, prepped=True, prep_next=(g < NG - 1))
                    if DBG and l == 0 and g == 0:
                        fw.barrier()
                        fw.dma(sp, osem, dbg["dbg_q"], V.qT_g[:].rearrange("p a b -> p (a b)"), writes=[out_t])
                        fw.dma(sp, osem, dbg["dbg_hT"], V.hT[:].rearrange("p a b -> p (a b)"), writes=[out_t])
                        fw.dma(sp, osem, dbg["dbg_sga"], V.sga[:].rearrange("p a b -> p (a b)"), writes=[out_t])
                        fw.barrier()
                for f in PEND:
                    f()
                del PEND[:]
                if l < depth - 1:
                    finish_rstd(l + 1)
                fw.barrier()
        fin = [out_t, x1_t, ctx1_t]
        for e in fw.engs:
            fw.wait_all(e, fin)
    return nc


def _consts(half):
    bf = ml_dtypes.bfloat16
    c = {}
    c["identf"] = np.eye(128, dtype=np.float32)
    c["identb"] = np.eye(128, dtype=np.float32).astype(bf)
    pm = np.zeros((128, 128), np.float32)
    for p in range(128):
        pm[p, p ^ 16] = 1.0
    c["pm"] = pm.astype(bf)
    tok = np.arange(T0) + half * T0
    row = (tok // 64).astype(np.float32)
    col = (tok % 64).astype(np.float32)
    inv = np.power(np.float32(10000.0), -np.arange(16, dtype=np.float32) / np.float32(16)).astype(np.float32)
    cosT = np.zeros((128, T0), np.float32)
    sinT = np.zeros((128, T0), np.float32)
    for p in range(128):
        d = p % 64
        axis = d // 32
        hf = (d % 32) // 16
        fr = d % 16
        ang = ((row if axis == 0 else col) * inv[fr]).astype(np.float32)
        cosT[p] = np.cos(ang)
        sinT[p] = np.sin(ang) * (-1.0 if hf == 0 else 1.0)
    c["cosT"] = cosT
    c["sinT"] = sinT
    cc = np.arange(64)
    th = 2 * np.pi * np.outer(cc, cc) / 64.0
    c["c64x2"] = np.concatenate([np.cos(th), np.cos(th)], 1).astype(np.float32)
    c["ns64x2"] = np.concatenate([-np.sin(th), -np.sin(th)], 1).astype(np.float32)
    n1 = np.arange(64)
    ph = 2 * np.pi * np.outer(n1, n1) / 64.0
    nrm = 1.0 / np.sqrt(8192.0 * 64.0)
    m1 = np.zeros((128, 128))
    m1[0:64, 0:64] = np.cos(ph)
    m1[64:128, 0:64] = np.sin(ph)
    m1[0:64, 64:128] = -np.sin(ph)
    m1[64:128, 64:128] = np.cos(ph)
    c["m1"] = (m1 * nrm).astype(np.float32).astype(bf)
    n2 = np.arange(128)[:, None, None]
    k1 = np.arange(64)[None, :, None]
    k2 = (np.arange(64) + 64 * half)[None, None, :]
    ang = 2 * np.pi * ((n2 * (k1 + 64 * k2)) % 8192) / 8192.0
    G = np.stack([np.cos(ang), np.sin(ang)], axis=2)
    c["gtab"] = G.astype(np.float32).astype(bf)
    n = np.arange(256)
    a2 = 2 * np.pi * np.outer(n, n) / 256.0
    nr2 = 1.0 / np.sqrt(256.0 * 64.0)
    cs = np.stack([np.cos(a2) * nr2, np.sin(a2) * nr2], axis=1)
    cs = cs.reshape(2, 128, 2, 256).transpose(1, 0, 2, 3)
    c["cs256"] = np.ascontiguousarray(cs).astype(np.float32).astype(bf)
    kk = np.arange(128)[:, None]
    qq = np.arange(128)[None, :]
    NEG = np.float32(-30000.0)
    mprev = np.where(kk >= qq, np.float32(0.0), NEG).astype(np.float32)
    mnext = np.where(kk <= qq, np.float32(0.0), NEG).astype(np.float32)
    allm = np.full_like(mprev, NEG)
    masks = np.stack([mprev if half == 1 else allm, mprev, mnext, mnext if half == 0 else allm], axis=1)
    c["masks"] = np.ascontiguousarray(masks).astype(bf)
    fl = np.zeros((128, 2), np.float32)
    fl[:, 0] = 1.0 if half == 1 else 0.0
    fl[:, 1] = 1.0 if half == 0 else 0.0
    c["flags"] = fl
    return c


def _perm_heads():
    idx = []
    for j in range(4):
        for s in range(2):
            h = s * 4 + j
            idx.extend(range(h * 64, (h + 1) * 64))
    return np.array(idx)


_NC_CACHE = {}


def kernel(x, c, ctx, c_ctx, w_mod, b_mod, g_pre, g_post, w_in, w_out, sink, w_fourier, conv_w, conv_b):
    x = np.asarray(x, np.float32)
    ph = _perm_heads()
    w_in = np.asarray(w_in, np.float32)
    cols = np.concatenate([ph, np.arange(512, 768), 768 + ph, np.arange(1280, 2816)])
    w_in_p = np.ascontiguousarray(w_in[:, :, cols])
    w_out = np.asarray(w_out, np.float32)
    rows = np.concatenate([ph, np.arange(512, 1024)])
    w_out_p = np.ascontiguousarray(w_out[:, rows, :])
    b_mod = np.asarray(b_mod, np.float32)
    bmodT = np.ascontiguousarray(b_mod.reshape(2, 24, 128).transpose(0, 2, 1))
    gpreT = np.ascontiguousarray(np.asarray(g_pre, np.float32).reshape(2, 8, 128).transpose(0, 2, 1))
    gpostT = np.ascontiguousarray(np.asarray(g_post, np.float32).reshape(2, 8, 128).transpose(0, 2, 1))
    convwT = np.ascontiguousarray(np.asarray(conv_w, np.float32).reshape(2, 3, 2, 128).transpose(0, 3, 2, 1))
    convbT = np.ascontiguousarray(np.asarray(conv_b, np.float32).reshape(2, 2, 128).transpose(0, 2, 1))
    c = np.asarray(c, np.float32)
    c_ctx = np.asarray(c_ctx, np.float32)
    if "nc" not in _NC_CACHE:
        _NC_CACHE["nc"] = build_nc()
    nc = _NC_CACHE["nc"]
    consts = [_consts(0), _consts(1)]
    in_maps = []
    for core in range(8):
        b, half = core // 2, core % 2
        cT = np.stack([c[b].reshape(8, 128).T, c_ctx.reshape(8, 128).T], axis=-1)
        m = {"x": np.ascontiguousarray(x[b, half * T0:(half + 1) * T0]), "ctx": np.ascontiguousarray(np.asarray(ctx, np.float32)[b]),
             "cT": np.ascontiguousarray(cT.astype(np.float32)), "w_mod": np.asarray(w_mod, np.float32), "bmodT": bmodT, "gpreT": gpreT, "gpostT": gpostT,
             "w_in": w_in_p, "w_out": w_out_p, "sink": np.asarray(sink, np.float32), "w_fourier": np.asarray(w_fourier, np.float32),
             "convwT": convwT, "convbT": convbT}
        m.update(consts[half])
        in_maps.append(m)
    res = run_bass_kernel_spmd(nc, in_maps, core_ids=list(range(8)))
    out = np.empty((4, 2 * T0, D), np.float32)
    for core in range(8):
        b, half = core // 2, core % 2
        out[b, half * T0:(half + 1) * T0] = np.asarray(res.results[core]["out"], np.float32)
    return out
```

```python
import contextlib
import numpy as np
import ml_dtypes
import concourse.bass as bass
import concourse.mybir as mybir
from concourse.bass_utils import run_bass_kernel_spmd

F32 = mybir.dt.float32
BF16 = mybir.dt.bfloat16
AF = mybir.ActivationFunctionType
ALU = mybir.AluOpType
SEM_ROT = 30000


class SemW:
    def __init__(s, h, name):
        s.h = h
        s.name = name
        s.cnt = 0


class Tok:
    def __init__(s, name=""):
        s.name = name
        s.w = None
        s.r = {}
        s.excl = False
        s.multi = False
        s.wm = {}


class Eng:
    def __init__(s, fwk, name, h, is_pe=False):
        s.fw = fwk
        s.name = name
        s.h = h
        s.is_pe = is_pe
        s.sem = fwk.new_sem("p_" + name)
        s.seen = {}

    def wait(s, ev):
        if ev is None:
            return
        sw, val = ev
        if s.seen.get(sw, 0) >= val:
            return
        if sw is s.sem:
            if s.is_pe or not s.fw.same_eng_sync:
                return
            assert val <= sw.cnt, "self-wait on future inc (%s)" % s.name
        s.h.wait_ge(sw.h, val)
        s.seen[sw] = val


class FW:
    def __init__(s, nc, stack, same_eng_sync=True):
        s.nc = nc
        s.stack = stack
        s.same_eng_sync = same_eng_sync
        s.nsem = 0
        s.all_sems = []
        s.pe = Eng(s, "pe", nc.tensor, is_pe=True)
        s.act = Eng(s, "act", nc.scalar)
        s.dve = Eng(s, "dve", nc.vector)
        s.pool = Eng(s, "pool", nc.gpsimd)
        s.sp = Eng(s, "sp", nc.sync)
        s.engs = [s.pe, s.act, s.dve, s.pool, s.sp]

    def new_sem(s, name):
        s.nsem += 1
        h = s.stack.enter_context(s.nc.semaphore("%s_%d" % (name, s.nsem)))
        sw = SemW(h, name)
        s.all_sems.append(sw)
        return sw

    def sbuf(s, name, shape, dt, stack=None):
        s.nalloc = getattr(s, "nalloc", 0) + 1
        return (stack or s.stack).enter_context(s.nc.sbuf_tensor("%s_%d" % (name, s.nalloc), list(shape), dt))

    def barrier(s):
        for e in s.engs:
            for sw in s.all_sems:
                if sw.cnt > 0:
                    e.wait((sw, sw.cnt))

    def psum(s, name, shape, dt=F32):
        return s.stack.enter_context(s.nc.psum_tensor(name, list(shape), dt))

    def _deps(s, eng, reads, writes):
        for b in reads:
            eng.wait(b.w)
            if b.multi:
                for sw, v in list(b.wm.items()):
                    eng.wait((sw, v))
            if b.excl:
                for sw, v in list(b.r.items()):
                    if sw is not eng.sem:
                        eng.wait((sw, v))
        for b in writes:
            if not b.multi:
                eng.wait(b.w)
            for sw, v in list(b.r.items()):
                eng.wait((sw, v))

    def _post(s, ev, reads, writes):
        sw, v = ev
        for b in reads:
            if b.r.get(sw, 0) < v:
                b.r[sw] = v
        for b in writes:
            if b.multi:
                if b.wm.get(sw, 0) < v:
                    b.wm[sw] = v
                continue
            b.w = ev
            b.r = {}

    def op(s, eng, fn, reads=(), writes=(), inc=True):
        s._deps(eng, reads, writes)
        inst = fn(eng.h)
        if eng.sem.cnt >= SEM_ROT:
            eng.sem = s.new_sem("p_" + eng.name)
        if inc:
            eng.sem.cnt += 1
            inst.then_inc(eng.sem.h, 1)
            ev = (eng.sem, eng.sem.cnt)
        else:
            assert eng.is_pe
            ev = (eng.sem, eng.sem.cnt + 1)
        s._post(ev, reads, writes)
        return inst

    def dma(s, q, dsem, out, in_, reads=(), writes=(), **kw):
        s._deps(q, reads, writes)
        inst = q.h.dma_start(out=out, in_=in_, **kw)
        dsem.cnt += 16
        inst.then_inc(dsem.h, 16)
        ev = (dsem, dsem.cnt)
        s._post(ev, reads, writes)
        return ev

    def wait_all(s, eng, toks):
        for b in toks:
            eng.wait(b.w)
            for sw, v in list(b.wm.items()):
                eng.wait((sw, v))
            for sw, v in list(b.r.items()):
                eng.wait((sw, v))


T0 = 4096
NT = 32
NG = 8
D = 1024
PW = 2816
NCTX = 256
EPS = 1e-6
C_Q, C_K, C_V, C_GA, C_UF, C_GF, C_ZC, C_BC, C_CC, C_GC = 0, 512, 640, 768, 1280, 1536, 1792, 2048, 2304, 2560


def build_nc(depth=2):
    nc = bass.Bass("TRN2", target_bir_lowering=False)

    def din(name, shape, dt=F32):
        return nc.dram_tensor(name, list(shape), dt, kind="ExternalInput").ap()

    x_d = din("x", [T0, D])
    ctx_d = din("ctx", [NCTX, D])
    cT_d = din("cT", [128, 8, 2])
    wmod_d = din("w_mod", [2, D, 3 * D])
    bmodT_d = din("bmodT", [2, 128, 24])
    gpreT_d = din("gpreT", [2, 128, 8])
    gpostT_d = din("gpostT", [2, 128, 8])
    win_d = din("w_in", [2, D, PW])
    wout_d = din("w_out", [2, D, D])
    sink_d = din("sink", [2, 8])
    wf_d = din("w_fourier", [2, 4, 64, 64])
    convwT_d = din("convwT", [2, 128, 2, 3])
    convbT_d = din("convbT", [2, 128, 2])
    cos_d = din("cosT", [128, T0])
    sin_d = din("sinT", [128, T0])
    identf_d = din("identf", [128, 128])
    c64_d = din("c64x2", [64, 128])
    s64_d = din("ns64x2", [64, 128])
    flags_d = din("flags", [128, 2])
    identb_d = din("identb", [128, 128], BF16)
    pm_d = din("pm", [128, 128], BF16)
    m1_d = din("m1", [128, 128], BF16)
    masks_d = din("masks", [128, 4, 128], BF16)
    cs256_d = din("cs256", [128, 2, 2, 256], BF16)
    g_d = din("gtab", [128, 64, 2, 64], BF16)
    out_d = nc.dram_tensor("out", [T0, D], F32, kind="ExternalOutput").ap()

    modscr = [nc.dram_tensor("modscr%d" % l, [2, 3 * D], F32).ap() for l in range(2)]
    x1_d = nc.dram_tensor("x1s", [T0, D], F32).ap()
    ctx1_d = nc.dram_tensor("ctx1s", [NCTX, D], F32).ap()
    ez_in = [[nc.dram_tensor("ezin%d_%d" % (l, q), [128, 2048], BF16).ap() for q in range(8)] for l in range(2)]
    ez_out = [[nc.dram_tensor("ezout%d_%d" % (l, q), [256, 2048], BF16).ap() for q in range(8)] for l in range(2)]
    import os
    DBG = bool(os.environ.get("KDBG"))
    dbg = {}
    if DBG:
        for nm, shp in [("dbg_kT", [128, (NT + 2) * 128]), ("dbg_v", [128, (NT + 2) * 256]), ("dbg_t", [128, 2 * (T0 + 2)]),
                        ("dbg_fT", [128, 2 * T0]), ("dbg_q", [128, 4 * 512]), ("dbg_hT", [128, 8 * 512]), ("dbg_sga", [128, 4 * 512])]:
            dbg[nm] = nc.dram_tensor(nm, shp, BF16, kind="ExternalOutput").ap()
    eh_in = [nc.dram_tensor("ehin%d" % l, [128, 640], BF16).ap() for l in range(2)]
    eh_out = [nc.dram_tensor("ehout%d" % l, [256, 640], BF16).ap() for l in range(2)]

    with contextlib.ExitStack() as st:
        fw = FW(nc, st)
        pe, act, dve, pool, sp = fw.pe, fw.act, fw.dve, fw.pool, fw.sp
        st.enter_context(nc.Block())

        import os
        wbf = fw.sbuf("wbf", [128, 8, PW], BF16); wbf_t = Tok("wbf")
        wob = fw.sbuf("wob", [128, 8, D], BF16); wob_t = Tok("wob")
        kT_all = fw.sbuf("kT_all", [128, (NT + 2) * 128], BF16); kT_t = [Tok("kT%d" % i) for i in range(NT + 2)]
        v_aug = fw.sbuf("v_aug", [128, NT + 2, 2, 128], BF16); v_t = [Tok("v%d" % i) for i in range(NT + 2)]
        t_all = fw.sbuf("t_all", [128, 2, T0 + 2], BF16); t_t = [Tok("t%d" % i) for i in range(NG + 2)]
        fT = fw.sbuf("fT", [128, 2, T0], BF16); fT_t = [Tok("fT0"), Tok("fT1")]
        kcT = fw.sbuf("kcT", [128, NCTX], BF16); kcT_t = Tok("kcT")
        vc_aug = fw.sbuf("vc_aug", [128, 2, 2, 128], BF16); vc_t = Tok("vc")
        ident_b = fw.sbuf("ident_b", [128, 128], BF16)
        pm_b = fw.sbuf("pm_b", [128, 128], BF16); m1_b = fw.sbuf("m1_b", [128, 128], BF16)
        masks_b = fw.sbuf("masks_b", [128, 4, 128], BF16)
        flags = fw.sbuf("flags", [128, 2], F32)
        AB = fw.sbuf("AB", [128, 2, 256], BF16); AB_t = Tok("AB")
        gg_b = fw.sbuf("gg_b", [128, D], F32); gg_t = Tok("gg")
        cT = fw.sbuf("cT", [128, 8, 2], F32); scs = fw.sbuf("scs", [128, 8, 2], F32)
        modT = fw.sbuf("modT", [128, 24, 2], F32); am = fw.sbuf("am", [128, 8, 2], F32); ggm = fw.sbuf("ggm", [128, 8, 2], F32)
        mod_t = Tok("mod")
        bmodT = fw.sbuf("bmodT", [128, 24], F32); gpreT = fw.sbuf("gpreT", [128, 8], F32); gpostT = fw.sbuf("gpostT", [128, 8], F32)
        esink = fw.sbuf("esink", [128, 8], F32); esink_t = Tok("esink")
        convw = fw.sbuf("convw", [128, 2, 3], F32); convb = fw.sbuf("convb", [128, 2], F32); conv_t = Tok("convp")
        const_t = Tok("const")
        ssq = [fw.sbuf("ssq%d" % l_, [128, NT], F32) for l_ in range(2)]; ssq_t = [Tok("ssq0"), Tok("ssq1")]
        rstd = [fw.sbuf("rstd%d" % l_, [128, NT], F32) for l_ in range(2)]; rstd_t = [Tok("rstd0"), Tok("rstd1")]
        SPECS = {
            "xs": ([128, D], F32, 1), "xe": ([128, D], F32, 1), "etmp": ([128, D], F32, 0), "junk": ([128, D], BF16, 0),
            "ss": ([128, 4], F32, 2), "xn": ([128, D], BF16, 2), "hxT": ([128, 8, 512], BF16, 0),
            "raw_b": ([128, 512], BF16, 0), "rt1": ([128, 512], F32, 0), "rt2": ([128, 512], F32, 0),
            "cs_sb": ([128, 2, 512], F32, 1), "qT_g": ([128, 4, 512], BF16, 0), "sga": ([128, 4, 512], BF16, 0),
            "sg2": ([128, 512], BF16, 0), "bc_sb": ([128, 512], F32, 0), "cy": ([128, 512], F32, 0),
            "hT": ([128, 8, 512], BF16, 0), "PT": ([128, 512], BF16, 4), "rec": ([128, 512], F32, 0), "ntmp": ([128, 512], F32, 0),
            "ufT": ([128, 2, 512], BF16, 0), "zc_sb": ([128, 512], F32, 0), "z_sb": ([128, 4, 8, 64], BF16, 2), "hxT2": ([128, 8, 512], BF16, 0), "modrow": ([2, 3 * D], F32, 0),
            "zctx": ([128, 2, 512], BF16, 0), "tctx": ([128, 2, NCTX + 2], BF16, 0), "fcT": ([128, 2, NCTX], BF16, 0),
            "hb": ([128, 640], BF16, 0), "th": ([128, 4], BF16, 0), "wst": ([128, PW], F32, 2),
            "G_b": ([128, 64, 2, 64], BF16, 0), "zin": ([128, 128, 64], BF16, 0), "Y_sb": ([128, 128, 128], BF16, 0),
            "ident_f": ([128, 128], F32, 0), "ones_f": ([128, 128], F32, 0), "c64": ([64, 128], F32, 0), "s64": ([64, 128], F32, 0),
            "cs256_b": ([128, 2, 2, 256], BF16, 0), "ggc_b": ([128, D], F32, 0),
            "cy2": ([128, 512], F32, 0), "sg3": ([128, 512], BF16, 0), "bc2": ([128, 512], F32, 0), "sgf": ([128, 512], BF16, 0),
        }
        NORM = ["xs", "junk", "ss", "xn", "hxT"]
        ROPE = ["raw_b", "rt1", "rt2", "cs_sb"]
        P2 = ["qT_g", "sga", "sg2", "bc_sb", "cy", "hT", "PT", "rec", "ntmp", "xe", "etmp"]
        P1 = ["ufT", "zc_sb"]
        import types
        V = types.SimpleNamespace()

        ARENA = 38400
        arena = fw.sbuf("arena", [128, ARENA], BF16)

        def carve(off, shape, dt):
            nel = 1
            for d_ in shape[1:]:
                nel *= d_
            sz = nel * (2 if dt == F32 else 1)
            sz = (sz + 15) // 16 * 16
            ap = arena[0:shape[0], off:off + nel * (2 if dt == F32 else 1)]
            if dt == F32:
                ap = ap.bitcast(F32)
            if len(shape) == 3:
                ap = ap.rearrange("p (a b) -> p a b", a=shape[1])
            elif len(shape) == 4:
                ap = ap.rearrange("p (a b c) -> p a b c", a=shape[1], b=shape[2])
            return ap, off + sz

        def alloc(stack, names, slots={}):
            off = 0
            for nm in names:
                shape, dt, ns = SPECS[nm]
                ns = slots.get(nm, ns)
                if ns == 0:
                    ap, off = carve(off, shape, dt)
                    setattr(V, nm, ap); setattr(V, nm + "_t", Tok(nm))
                else:
                    lst = []
                    for _ in range(ns):
                        ap, off = carve(off, shape, dt)
                        lst.append(ap)
                    setattr(V, nm, lst); setattr(V, nm + "_t", [Tok(nm) for _ in range(ns)])
            assert off <= ARENA, "arena overflow %d > %d" % (off, ARENA)
            V.hxT_t = [[Tok(), Tok()] for _ in range(4)]
            V.hT_t = [[Tok() for _ in range(4)] for _ in range(8)]
            V.qT_t = [Tok() for _ in range(4)]
            V.sga_t = [Tok() for _ in range(4)]
            V.ufT_t = [Tok(), Tok()]
            V.Y_t = [Tok(), Tok()]

        xs_sem = [fw.new_sem("xs") for _ in range(2)]; xe_sem = [fw.new_sem("xe") for _ in range(2)]
        cs_sem = [fw.new_sem("cs") for _ in range(2)]; wst_sem = [fw.new_sem("wst") for _ in range(4)]
        zin_sem = fw.new_sem("zin")

        TR = fw.psum("TR", [128, 8, 128], BF16); TR_t = Tok()
        PS = [fw.psum("ps%d" % i, [128, 512]) for i in range(7)]; PS_t = [Tok() for _ in range(7)]
        PJ = [0, 1]; STB = [2, 3]; PVB = 4; YB = [5, 6]
        TR_t.excl = True
        for t_ in PS_t:
            t_.excl = True
        rot = {}

        def nxt(key, n):
            v = rot.get(key, 0) % n
            rot[key] = rot.get(key, 0) + 1
            return v

        csem = fw.new_sem("const"); osem = fw.new_sem("out"); x1sem = fw.new_sem("x1"); zsem = fw.new_sem("zst")
        zsems = [fw.new_sem("zst0"), fw.new_sem("zst1")]
        msem2 = fw.new_sem("modld"); modscr_t = [Tok(), Tok()]
        hsem = fw.new_sem("halo"); ccsem = fw.new_sem("cc"); msem = fw.new_sem("misc")
        x1_t = Tok("x1"); ctx1_t = Tok("ctx1"); out_t = Tok("out")
        ezin_t = [Tok(), Tok()]; ezout_t = [[Tok() for _ in range(8)] for _ in range(2)];
        for t_ in ezin_t + [x1_t, ctx1_t, out_t]:
            t_.multi = True
        ehin_t = [Tok(), Tok()]; ehout_t = [Tok(), Tok()]

        def bc_last(ap, n):
            return bass.AP(ap.tensor, ap.offset, [list(d) for d in ap.ap] + [[0, n]])

        def bc_mid(ap, n):
            d = [list(x) for x in ap.ap]
            return bass.AP(ap.tensor, ap.offset, [d[0], [0, n]] + d[1:])

        for dst, src in [(ident_b, identb_d), (pm_b, pm_d), (m1_b, m1_d), (masks_b, masks_d), (flags, flags_d), (cT, cT_d)]:
            fw.dma(sp, csem, dst[:], src, writes=[const_t])
        fw.op(pool, lambda e: e.memset(AB[:], 0.0), writes=[AB_t])
        fw.op(pool, lambda e: e.memset(v_aug[:], 1.0), writes=v_t)
        fw.op(pool, lambda e: e.memset(vc_aug[:], 1.0), writes=[vc_t])
        fw.op(act, lambda e: e.activation(out=scs[:], in_=cT[:], func=AF.Silu), reads=[const_t], writes=[mod_t])

        def mm(out, lhsT, rhs, start, stop, reads, writes, last):
            fw.op(pe, lambda e: e.matmul(out, lhsT=lhsT, rhs=rhs, start=start, stop=stop), reads=reads, writes=writes, inc=last)

        def cast_copy(eng, out, in_, reads, writes):
            if eng is act:
                fw.op(act, lambda e: e.copy(out=out, in_=in_), reads=reads, writes=writes)
            else:
                fw.op(eng, lambda e: e.tensor_copy(out=out, in_=in_), reads=reads, writes=writes)

        def setup_layer(l, full):
            wst, wst_t, ident_f, ones_f, c64, s64, rt1, rt1_t = V.wst, V.wst_t, V.ident_f, V.ones_f, V.c64, V.s64, V.rt1, V.rt1_t
            st_t = Tok("setup")
            for dst, src in [(ident_f, identf_d), (c64, c64_d), (s64, s64_d)]:
                fw.dma(sp, csem, dst[:], src, writes=[st_t])
            fw.op(pool, lambda e: e.memset(ones_f[:], 1.0), writes=[st_t])
            for dst, src in [(bmodT, bmodT_d[l]), (gpreT, gpreT_d[l]), (gpostT, gpostT_d[l])]:
                fw.dma(sp, csem, dst[:], src, writes=[mod_t])
            fw.dma(sp, csem, convw[:], convwT_d[l], writes=[conv_t])
            fw.dma(sp, csem, convb[:], convbT_d[l], writes=[conv_t])
            fw.dma(sp, csem, esink[:], bass.AP(sink_d.tensor, l * 8, [[0, 128], [1, 8]]), writes=[esink_t])
            for tk in (st_t, mod_t, conv_t, esink_t):
                tk.w = (csem, csem.cnt)
            fw.op(act, lambda e: e.activation(out=esink[:], in_=esink[:], func=AF.Exp), reads=[], writes=[esink_t])
            mps = PS[PVB]; mps_t = PS_t[PVB]
            modrow, modrow_t = V.modrow, V.modrow_t
            wm_v = wmod_d[l].rearrange("(kc p) j -> p kc j", p=128)
            for jq in range(12):
                s_ = nxt("wst", len(wst))
                wv = wst[s_][:, 0:2048].rearrange("p (kc j) -> p kc j", kc=8)
                fw.dma([sp, pool][jq % 2], wst_sem[s_], wv, wm_v[:, :, jq * 256:(jq + 1) * 256], writes=[wst_t[s_]])
                po = mps[0:2, (jq % 2) * 256:(jq % 2) * 256 + 256]
                for kc in range(8):
                    mm(po, scs[:, kc, :], wv[:, kc, :], kc == 0, kc == 7, [wst_t[s_], mod_t], [mps_t], kc == 7)
                cast_copy(act if jq % 2 == 0 else dve, modrow[0:2, jq * 256:(jq + 1) * 256], po, [mps_t], [modrow_t])
            fw.dma(sp, msem, modscr[l], modrow[0:2, :], reads=[modrow_t], writes=[modscr_t[l]])
            for v_ in range(2):
                for h_ in range(2):
                    src = modscr[l][v_:v_ + 1, h_ * 1536:(h_ + 1) * 1536].rearrange("o (j p) -> (o p) j", p=128)
                    fw.dma(sp, msem2, modT[:, h_ * 12:(h_ + 1) * 12, v_], src, reads=[modscr_t[l]], writes=[mod_t], allow_slow_non_contiguous=True)
            mod_t.w = (msem2, msem2.cnt)
            fw.op(dve, lambda e: e.tensor_tensor(out=modT[:], in0=modT[:], in1=bc_last(bmodT[:], 2), op=ALU.add), reads=[mod_t], writes=[mod_t])
            fw.op(dve, lambda e: e.scalar_tensor_tensor(out=am[:], in0=modT[:, 8:16, :], scalar=1.0, in1=bc_last(gpreT[:], 2), op0=ALU.add, op1=ALU.mult),
                  reads=[mod_t], writes=[mod_t])
            fw.op(dve, lambda e: e.tensor_tensor(out=ggm[:], in0=modT[:, 16:24, :], in1=bc_last(gpostT[:], 2), op=ALU.mult), reads=[mod_t], writes=[mod_t])
            for v in range(1):
                dstb, dst_t = gg_b, gg_t
                for hlf in range(2):
                    yb = PS[YB[hlf]]; yb_t = PS_t[YB[hlf]]
                    for k4 in range(4):
                        kc = hlf * 4 + k4
                        fw.op(dve, lambda e: e.tensor_scalar(out=rt1[:, 0:128], in0=ones_f[:], scalar1=ggm[:, kc, v:v + 1], scalar2=None, op0=ALU.mult),
                              reads=[mod_t, st_t], writes=[rt1_t])
                        mm(yb[:, k4 * 128:(k4 + 1) * 128], rt1[:, 0:128], ident_f[:], True, True, [rt1_t, st_t], [yb_t], True)
                    fw.op(act, lambda e: e.copy(out=dstb[:, hlf * 512:(hlf + 1) * 512], in_=yb[:]), reads=[yb_t], writes=[dst_t])
            for kc in range(8):
                s_ = nxt("wst", len(wst))
                fw.dma([sp, pool][kc % 2], wst_sem[s_], wst[s_][:], win_d[l, kc * 128:(kc + 1) * 128, :], writes=[wst_t[s_]])
                cast_copy([dve, act][kc % 2], wbf[:, kc, :], wst[s_][:], [wst_t[s_]], [wbf_t])
            for k2 in range(4):
                s_ = nxt("wst", len(wst))
                wv = wst[s_][:, 0:2048].rearrange("p (a j) -> p a j", a=2)
                fw.dma([sp, pool][k2 % 2], wst_sem[s_], wv, wout_d[l, k2 * 256:(k2 + 1) * 256, :].rearrange("(a p) j -> p a j", p=128), writes=[wst_t[s_]])
                cast_copy([dve, act][k2 % 2], wob[:, k2 * 2:k2 * 2 + 2, :], wv, [wst_t[s_]], [wob_t])
            s_ = nxt("wst", len(wst))
            wfv = wst[s_][0:64, 0:256].rearrange("p (g d) -> p g d", g=4)
            fw.dma(sp, wst_sem[s_], wfv, wf_d[l].rearrange("g c d -> c g d"), writes=[wst_t[s_]])
            for ri, cm in enumerate([c64, s64]):
                pb = PS[STB[ri]]; pb_t = PS_t[STB[ri]]
                mm(pb[:, 0:256], cm[:], wst[s_][0:64, 0:256], True, True, [wst_t[s_], st_t], [pb_t], True)
                for cg in range(2):
                    fw.op(dve, lambda e: e.tensor_copy(out=AB[0:64, cg, ri * 128:ri * 128 + 64], in_=pb[0:64, (2 * cg) * 64:(2 * cg) * 64 + 64]),
                          reads=[pb_t], writes=[AB_t])
                    fw.op(dve, lambda e: e.tensor_copy(out=AB[64:128, cg, ri * 128 + 64:ri * 128 + 128], in_=pb[64:128, (2 * cg + 1) * 64:(2 * cg + 1) * 64 + 64]),
                          reads=[pb_t], writes=[AB_t])

        def norm_T(src_ap, src_toks, v, tl, rs_ap=None, rs_toks=()):
            xs, xs_t, ss, ss_t, xn, xn_t, hxT, hxT_t, junk, junk_t = V.xs, V.xs_t, V.ss, V.ss_t, V.xn, V.xn_t, V.hxT, V.hxT_t, V.junk, V.junk_t
            s_ = nxt("xs", len(xs))
            n_ = nxt("xn", 2)
            fw.dma(sp, xs_sem[s_], xs[s_][:], src_ap, reads=src_toks, writes=[xs_t[s_]])
            if rs_ap is None:
                fw.op(act, lambda e: e.activation(out=junk[:], in_=xs[s_][:], func=AF.Square, accum_out=ss[n_][:, 0:1]),
                      reads=[xs_t[s_]], writes=[junk_t, ss_t[n_]])
                fw.op(dve, lambda e: e.tensor_scalar(out=ss[n_][:, 1:2], in0=ss[n_][:, 0:1], scalar1=1.0 / D, scalar2=EPS, op0=ALU.mult, op1=ALU.add),
                      reads=[ss_t[n_]], writes=[ss_t[n_]])
                fw.op(act, lambda e: e.activation(out=ss[n_][:, 3:4], in_=ss[n_][:, 1:2], func=AF.Sqrt), reads=[ss_t[n_]], writes=[ss_t[n_]])
                fw.op(dve, lambda e: e.reciprocal(out=ss[n_][:, 2:3], in_=ss[n_][:, 3:4]), reads=[ss_t[n_]], writes=[ss_t[n_]])
                rs_ap = ss[n_][:, 2:3]; rs_toks = [ss_t[n_]]
            fw.op(act, lambda e: e.activation(out=xn[n_][:], in_=xs[s_][:], func=AF.Identity, scale=rs_ap),
                  reads=[xs_t[s_]] + list(rs_toks), writes=[xn_t[n_]])
            for kc in range(8):
                fw.op(pe, lambda e: e.transpose(out=TR[:, kc, :], in_=xn[n_][:, kc * 128:(kc + 1) * 128], identity=ident_b[:]),
                      reads=[xn_t[n_], const_t], writes=[TR_t], inc=(kc == 7))
            o = hxT[:, :, tl * 128:(tl + 1) * 128]
            wt = [hxT_t[tl][0], hxT_t[tl][1]]
            fw.op(dve, lambda e: e.tensor_tensor(out=o, in0=TR[:], in1=bc_last(am[:, :, v], 128), op=ALU.mult), reads=[TR_t, mod_t], writes=wt)
            fw.op(dve, lambda e: e.tensor_tensor(out=o, in0=o, in1=bc_last(modT[:, 0:8, v], 128), op=ALU.add), reads=[mod_t], writes=wt)

        def proj_fm(col, ntl):
            hxT, hxT_t = V.hxT, V.hxT_t
            b = PJ[nxt("pj", 2)]
            for kc in range(8):
                mm(PS[b][:, 0:ntl * 128], wbf[:, kc, col:col + 128], hxT[:, kc, 0:ntl * 128], kc == 0, kc == 7,
                   [wbf_t] + [hxT_t[tl][kc % 2] for tl in range(ntl)], [PS_t[b]], kc == 7)
            return PS[b], PS_t[b]

        def load_cs(g):
            cs_sb, cs_t = V.cs_sb, V.cs_sb_t
            s_ = nxt("cs", len(cs_sb))
            fw.dma(sp, cs_sem[s_], cs_sb[s_][:, 0, :], cos_d[:, g * 512:(g + 1) * 512], writes=[cs_t[s_]])
            fw.dma(sp, cs_sem[s_], cs_sb[s_][:, 1, :], sin_d[:, g * 512:(g + 1) * 512], writes=[cs_t[s_]])
            return s_

        def rope_chunk(ps, ps_t, cs_slot, out_ap, out_toks, n, split=False):
            raw_b, raw_t, rt1, rt1_t, rt2, rt2_t, cs_sb, cs_t = V.raw_b, V.raw_b_t, V.rt1, V.rt1_t, V.rt2, V.rt2_t, V.cs_sb, V.cs_sb_t
            fw.op(act, lambda e: e.copy(out=raw_b[:, 0:n], in_=ps[:, 0:n]), reads=[ps_t], writes=[raw_t])
            if os.environ.get("KR2"):
                return
            fw.op(dve, lambda e: e.tensor_tensor(out=rt1[:, 0:n], in0=ps[:, 0:n], in1=cs_sb[cs_slot][:, 0, 0:n], op=ALU.mult),
                  reads=[ps_t, cs_t[cs_slot]], writes=[rt1_t])
            def part_b():
                b = PJ[nxt("pj", 2)]
                mm(PS[b][:, 0:n], pm_b[:], raw_b[:, 0:n], True, True, [raw_t, const_t], [PS_t[b]], True)
                fw.op(dve, lambda e: e.tensor_tensor(out=rt2[:, 0:n], in0=PS[b][:, 0:n], in1=cs_sb[cs_slot][:, 1, 0:n], op=ALU.mult),
                      reads=[PS_t[b], cs_t[cs_slot]], writes=[rt2_t])
                fw.op(dve, lambda e: e.tensor_tensor(out=out_ap, in0=rt1[:, 0:n], in1=rt2[:, 0:n], op=ALU.add),
                      reads=[rt1_t, rt2_t], writes=out_toks)

            if split:
                return part_b
            part_b()

        def prep_tile(l, g, tl, is_ctx, ctx_src1):
            if is_ctx:
                src = (ctx_d if (l == 0 or not ctx_src1) else ctx1_d)[tl * 128:(tl + 1) * 128, :]
                stoks = [] if (l == 0 or not ctx_src1) else [ctx1_t]
                norm_T(src, stoks, 1, tl)
            else:
                i = g * 4 + tl
                src = (x_d if l == 0 else x1_d)[i * 128:(i + 1) * 128, :]
                stoks = [] if l == 0 else [x1_t]
                norm_T(src, stoks, 0, tl, rstd[l][:, i:i + 1], [rstd_t[l]])

        def finish_rstd(l):
            fw.op(dve, lambda e: e.tensor_scalar(out=ssq[l][:], in0=ssq[l][:], scalar1=1.0 / D, scalar2=EPS, op0=ALU.mult, op1=ALU.add),
                  reads=[ssq_t[l]], writes=[ssq_t[l]])
            fw.op(act, lambda e: e.activation(out=ssq[l][:], in_=ssq[l][:], func=AF.Sqrt), reads=[ssq_t[l]], writes=[ssq_t[l]])
            fw.op(dve, lambda e: e.reciprocal(out=rstd[l][:], in_=ssq[l][:]), reads=[ssq_t[l]], writes=[rstd_t[l]])

        def prepass0():
            xs, xs_t, junk, junk_t = V.xs, V.xs_t, V.junk, V.junk_t
            for i in range(NT):
                s_ = nxt("xs", len(xs))
                fw.dma(act, xs_sem[s_], xs[s_][:], x_d[i * 128:(i + 1) * 128, :], writes=[xs_t[s_]])
                fw.op(act, lambda e: e.activation(out=junk[:], in_=xs[s_][:], func=AF.Square, accum_out=ssq[0][:, i:i + 1]),
                      reads=[xs_t[s_]], writes=[junk_t, ssq_t[0]])
            finish_rstd(0)

        def use_hx(k):
            V.hxT, V.hxT_t = HXS[k]

        def phase1a_group(l, g, is_ctx, full, prepped=False, prep_next=False):
            if not is_ctx:
                use_hx(g % 2)
            hxT, hxT_t = V.hxT, V.hxT_t
            pq = [tl for tl in range(4)] if (prep_next and not is_ctx) else []

            def prep_one():
                if pq:
                    use_hx((g + 1) % 2)
                    prep_tile(l, g + 1, pq.pop(0), False, True)
                    use_hx(g % 2)
            ntl = 2 if is_ctx else 4
            n = ntl * 128
            v = 1 if is_ctx else 0
            if not prepped:
                for tl in range(ntl):
                    prep_tile(l, g, tl, is_ctx, True)
            import os
            KSUB = int(os.environ.get("KSUB", "99"))
            if KSUB <= 1:
                return
            ps, ps_t = proj_fm(C_K, ntl)
            if is_ctx:
                fw.op(act, lambda e: e.copy(out=kcT[:], in_=ps[:, 0:n]), reads=[ps_t], writes=[kcT_t])
            elif int(os.environ.get("KR", "99")) <= 0:
                pass
            else:
                cslot = load_cs(g)
                rope_chunk(ps, ps_t, cslot, kT_all[:, (1 + g * 4) * 128:(1 + g * 4) * 128 + 512], [kT_t[1 + g * 4 + i] for i in range(4)], 512)
            prep_one()
            vb = PS[PVB]; vb_t = PS_t[PVB]
            for tl in range(ntl):
                for kc in range(8):
                    mm(vb[:, tl * 128:(tl + 1) * 128], hxT[:, kc, tl * 128:(tl + 1) * 128], wbf[:, kc, C_V:C_V + 128], kc == 0, kc == 7,
                       [wbf_t, hxT_t[tl][kc % 2]], [vb_t], kc == 7 and tl == ntl - 1)
            vbv = vb[:, 0:n].rearrange("p (t c) -> p t c", c=128)
            if is_ctx:
                fw.op(dve, lambda e: e.tensor_copy(out=vc_aug[:, :, 0, 0:64], in_=vbv[:, :, 0:64]), reads=[vb_t], writes=[vc_t])
                fw.op(dve, lambda e: e.tensor_copy(out=vc_aug[:, :, 1, 64:128], in_=vbv[:, :, 64:128]), reads=[vb_t], writes=[vc_t])
            else:
                vt = [v_t[1 + g * 4 + i] for i in range(4)]
                fw.op(dve, lambda e: e.tensor_copy(out=v_aug[:, 1 + g * 4:5 + g * 4, 0, 0:64], in_=vbv[:, :, 0:64]), reads=[vb_t], writes=vt)
                fw.op(dve, lambda e: e.tensor_copy(out=v_aug[:, 1 + g * 4:5 + g * 4, 1, 64:128], in_=vbv[:, :, 64:128]), reads=[vb_t], writes=vt)
            if is_ctx and not full:
                return
            prep_one()
            ufT, ufT_t, zc_sb, zc_t = V.ufT, V.ufT_t, V.zc_sb, V.zc_sb_t
            for cg in range(2):
                ps, ps_t = proj_fm(C_UF + cg * 128, ntl)
                fw.op(act, lambda e: e.copy(out=ufT[:, cg, 0:n], in_=ps[:, 0:n]), reads=[ps_t], writes=[ufT_t[cg]])
            prep_one()
            zs = g % 2
            for tl in range(ntl):
                zb = YB[nxt("zb", 2)]
                for cg in range(2):
                    mm(PS[zb][:, cg * 256:(cg + 1) * 256], ufT[:, cg, tl * 128:(tl + 1) * 128], AB[:, cg, :], True, True,
                       [ufT_t[cg], AB_t], [PS_t[zb]], cg == 1)
                if is_ctx:
                    fw.op(dve, lambda e: e.tensor_copy(out=V.zctx[:, tl, :], in_=PS[zb][:]), reads=[PS_t[zb]], writes=[V.zctx_t])
                else:
                    z_sb, z_t = V.z_sb, V.z_sb_t
                    for cg in range(2):
                        src = PS[zb][:, cg * 256:(cg + 1) * 256].rearrange("p (ri h c) -> p h ri c", ri=2, h=2)
                        dst = z_sb[zs][:, tl, cg * 4:(cg + 1) * 4, :].rearrange("p (h ri) c -> p h ri c", h=2)
                        cast_copy(dve if cg == 0 else act, dst, src, [PS_t[zb]], [z_t[zs]])
            if not is_ctx:
                for q8 in range(8):
                    dz = ez_in[l][q8].rearrange("p (x c) -> (p x) c", c=64)[g * 512:(g + 1) * 512, :].rearrange("(tl tok) c -> tok tl c", tl=4)
                    fw.dma([sp, pool][q8 % 2], zsems[zs], dz, V.z_sb[zs][:, :, q8, :], reads=[V.z_sb_t[zs]], writes=[ezin_t[l]])
            prep_one()
            for j in range(2):
                ps, ps_t = proj_fm(C_ZC + j * 128, ntl)
                fw.op(act, lambda e: e.copy(out=zc_sb[:, 0:n], in_=ps[:, 0:n]), reads=[ps_t], writes=[zc_t])
                ps2, ps2_t = proj_fm(C_CC + j * 128, ntl)
                if is_ctx:
                    o = V.tctx[:, j, 1:1 + n]; ot = [V.tctx_t]
                else:
                    o = t_all[:, j, 1 + g * 512:1 + g * 512 + 512]; ot = [t_t[1 + g]]
                fw.op(dve, lambda e: e.tensor_tensor(out=o, in0=ps2[:, 0:n], in1=zc_sb[:, 0:n], op=ALU.mult), reads=[ps2_t, zc_t], writes=ot)
            while pq:
                prep_one()

        def _p1a_tail(l, g, prep_next):
            if prep_next:
                for tl in range(4):
                    prep_tile(l, g + 1, tl, False, True)

        def exchange(l):
            hb, hb_t, th, th_t = V.hb, V.hb_t, V.th, V.th_t
            cps = [(hb[:, 0:128], kT_all[:, 128:256], [kT_t[1]]), (hb[:, 128:256], kT_all[:, NT * 128:(NT + 1) * 128], [kT_t[NT]]),
                   (hb[:, 256:320], v_aug[:, 1, 0, 0:64], [v_t[1]]), (hb[:, 320:384], v_aug[:, 1, 1, 64:128], [v_t[1]]),
                   (hb[:, 384:448], v_aug[:, NT, 0, 0:64], [v_t[NT]]), (hb[:, 448:512], v_aug[:, NT, 1, 64:128], [v_t[NT]]),
                   (hb[:, 512:514], t_all[:, :, 1], [t_t[1]]), (hb[:, 514:516], t_all[:, :, T0], [t_t[NG]])]
            for o, i_, tk in cps:
                fw.op(pool, lambda e: e.tensor_copy(out=o, in_=i_), reads=tk, writes=[hb_t])
            fw.dma(pool, hsem, eh_in[l][:, 0:516], hb[:, 0:516], reads=[hb_t], writes=[ehin_t[l]])
            for (i_t, o_t, i_ap, o_ap) in [(ehin_t[l], ehout_t[l], eh_in[l], eh_out[l])] + [(ezin_t[l], ezout_t[l][q8], ez_in[l][q8], ez_out[l][q8]) for q8 in range(8)]:
                fw._deps(pool, [i_t], [o_t])
                inst = nc.gpsimd.collective_compute("AllGather", ALU.bypass, replica_groups=[[0, 1], [2, 3], [4, 5], [6, 7]], ins=[i_ap], outs=[o_ap])
                ccsem.cnt += 1
                inst.then_inc(ccsem.h, 1)
                fw._post((ccsem, ccsem.cnt), [i_t], [o_t])
            eo = eh_out[l]
            ups = [(kT_all[:, 0:128], eo[0:128, 128:256], kT_t[0]), (kT_all[:, (NT + 1) * 128:(NT + 2) * 128], eo[128:256, 0:128], kT_t[NT + 1]),
                   (v_aug[:, 0, 0, 0:64], eo[0:128, 384:448], v_t[0]), (v_aug[:, 0, 1, 64:128], eo[0:128, 448:512], v_t[0]),
                   (v_aug[:, NT + 1, 0, 0:64], eo[128:256, 256:320], v_t[NT + 1]), (v_aug[:, NT + 1, 1, 64:128], eo[128:256, 320:384], v_t[NT + 1]),
                   (th[:, 0:2], eo[0:128, 514:516], th_t), (th[:, 2:4], eo[128:256, 512:514], th_t)]
            for o, i_, tk in ups:
                fw.dma(sp, hsem, o, i_, reads=[ehout_t[l]], writes=[tk])
            for _, _, tk in ups:
                tk.w = (hsem, hsem.cnt)
            fw.op(dve, lambda e: e.tensor_scalar(out=t_all[:, :, 0], in0=th[:, 0:2], scalar1=flags[:, 0:1], scalar2=None, op0=ALU.mult),
                  reads=[th_t, const_t], writes=[t_t[0]])
            fw.op(dve, lambda e: e.tensor_scalar(out=t_all[:, :, T0 + 1], in0=th[:, 2:4], scalar1=flags[:, 1:2], scalar2=None, op0=ALU.mult),
                  reads=[th_t, const_t], writes=[t_t[NG + 1]])

        def fft(l):
            G_b, G_t, zin, zin_t, Y_sb, Y_t = V.G_b, V.G_b_t, V.zin, V.zin_t, V.Y_sb, V.Y_t
            fw.dma(sp, csem, G_b[:], g_d, writes=[G_t])
            zo = ez_out[l]
            for hh in range(2):
                for qq in range(2):
                    qt = hh * 2 + qq
                    for ri in range(2):
                        for r in range(2):
                            src = zo[qt * 2 + ri][r * 128:(r + 1) * 128, :].rearrange("(a x) f -> a (x f)", a=32).rearrange("a (n c) -> a n c", c=64)
                            p0 = ri * 64 + r * 32
                            fw.dma(sp, zin_sem, zin[p0:p0 + 32, :, :], src, reads=[ezout_t[l][qt * 2 + ri]], writes=[zin_t])
                    for c4 in range(16):
                        b = PJ[nxt("pj", 2)]
                        for ci in range(4):
                            c = c4 * 4 + ci
                            mm(PS[b][:, ci * 128:(ci + 1) * 128], zin[:, :, c], m1_b[:], True, True, [zin_t, const_t], [PS_t[b]], ci == 3)
                        src = PS[b][:].rearrange("p (c j) -> p j c", c=4)
                        dst = Y_sb[:, :, qq * 64 + c4 * 4:qq * 64 + c4 * 4 + 4]
                        cast_copy(dve if c4 % 2 == 0 else act, dst, src, [PS_t[b]], [Y_t[qq]])
                fv = fT[:, hh, :].rearrange("p (k2 k1) -> p k1 k2", k1=64)
                for k8 in range(8):
                    b = STB[nxt("st", 2)]
                    for ki in range(8):
                        k1 = k8 * 8 + ki
                        for ri in range(2):
                            mm(PS[b][:, ki * 64:(ki + 1) * 64], Y_sb[:, ri * 64 + k1, :], G_b[:, k1, ri, :], ri == 0, ri == 1,
                               [Y_t[0], Y_t[1], G_t], [PS_t[b]], ki == 7 and ri == 1)
                    src = PS[b][:].rearrange("p (k1 k2) -> p k1 k2", k1=8)
                    dst = fv[:, k8 * 8:(k8 + 1) * 8, :]
                    cast_copy(dve if k8 % 2 == 0 else act, dst, src, [PS_t[b]], [fT_t[hh]])

        def attention_tile(tl, qcol, chunks, slotmap=None, defer=False):
            qT_g, qT_t, PT, PT_t, rec, rec_t, ntmp, ntmp_t, sga, sga_t, hT, hT_t = (V.qT_g, V.qT_t, V.PT, V.PT_t, V.rec, V.rec_t, V.ntmp, V.ntmp_t,
                                                                                     V.sga, V.sga_t, V.hT, V.hT_t)
            slotmap = dict(slotmap or {})
            slot = [0]
            pending_norm = [None]

            def fill():
                for f in slotmap.pop(slot[0], []):
                    f()
                slot[0] += 1

            for s_ in range(2):
                P0 = s_ * 64
                rn = slice(P0, P0 + 64)
                rd = slice(64 - P0, 128 - P0)
                pvi = [PVB, PJ[1]][s_]
                pvb = PS[pvi]; pvb_t = PS_t[pvi]
                nch = len(chunks)
                pts = [None] * nch

                def qk(ci):
                    kten, kcol, ktoks, _, _, midx = chunks[ci]
                    b = STB[nxt("st", 2)]
                    mm(PS[b][:], kten[rn, kcol:kcol + 128], qT_g[rn, :, qcol:qcol + 128], True, midx is None, ktoks + qT_t, [PS_t[b]], midx is None)
                    if midx is not None:
                        mm(PS[b][:].rearrange("p (j q) -> p j q", j=4), ident_b[:], bc_mid(masks_b[:, midx, :], 4), False, True, [const_t], [PS_t[b]], True)
                    p = nxt("pt", 4)
                    fw.op(act, lambda e: e.activation(out=PT[p][:], in_=PS[b][:], func=AF.Exp, scale=0.125), reads=[PS_t[b]], writes=[PT_t[p]])
                    pts[ci] = p

                def pv(ci):
                    _, _, _, vfn, vtoks, _ = chunks[ci]
                    p = pts[ci]
                    mm(pvb[:], vfn(s_), PT[p][:], ci == 0, ci == nch - 1, vtoks + [PT_t[p]], [pvb_t], ci == nch - 1)

                qk(0)
                if nch > 1:
                    qk(1)
                for ci in range(nch):
                    pv(ci)
                    if ci + 2 < nch:
                        qk(ci + 2)
                    fill()
                def normalize(s_=s_, rn=rn, rd=rd, pvb=pvb, pvb_t=pvb_t):
                    es = bc_last(esink[rd, s_ * 4:(s_ + 1) * 4], 128)
                    r3 = rec[rd, :].rearrange("p (j q) -> p j q", j=4)
                    fw.op(dve, lambda e: e.tensor_tensor(out=r3, in0=pvb[rd, :].rearrange("p (j q) -> p j q", j=4), in1=es, op=ALU.add),
                          reads=[pvb_t, esink_t], writes=[rec_t])
                    fw.op(act, lambda e: e.activation(out=rec[rd, :], in_=rec[rd, :], func=AF.Ln), reads=[rec_t], writes=[rec_t])
                    fw.op(act, lambda e: e.activation(out=rec[rd, :], in_=rec[rd, :], func=AF.Exp, scale=-1.0), reads=[rec_t], writes=[rec_t])
                    fw.op(dve, lambda e: e.tensor_tensor(out=ntmp[rn, :], in0=pvb[rn, :], in1=rec[rd, :], op=ALU.mult), reads=[pvb_t, rec_t], writes=[ntmp_t])
                    fw.op(dve, lambda e: e.tensor_tensor(out=hT[rn, 0:4, qcol:qcol + 128], in0=ntmp[rn, :].rearrange("p (j q) -> p j q", j=4),
                                                        in1=sga[rn, :, qcol:qcol + 128], op=ALU.mult),
                          reads=[ntmp_t] + sga_t, writes=[hT_t[j][tl] for j in range(4)])

                if defer and s_ == 0:
                    slotmap.setdefault(6, []).insert(0, normalize)
                elif defer:
                    pending_norm[0] = normalize
                else:
                    normalize()
            for k_ in sorted(slotmap):
                for f in slotmap[k_]:
                    f()
            return pending_norm[0]

        def phase2_group(l, g, is_ctx, prepped=False, prep_next=False):
            (qT_g, qT_t, sga, sga_t, sg2, sg2_t, bc_sb, bc_t, cy, cy_t, hT, hT_t, xe, xe_t, etmp, etmp_t, junk, junk_t, ss, ss_t) = (
                V.qT_g, V.qT_t, V.sga, V.sga_t, V.sg2, V.sg2_t, V.bc_sb, V.bc_sb_t, V.cy, V.cy_t, V.hT, V.hT_t, V.xe, V.xe_t, V.etmp, V.etmp_t,
                V.junk, V.junk_t, V.ss, V.ss_t)
            ntl = 2 if is_ctx else 4
            n = ntl * 128
            v = 1 if is_ctx else 0
            last = (l == depth - 1)
            if not prepped:
                for tl in range(ntl):
                    prep_tile(l, g, tl, is_ctx, False)
            if not is_ctx:
                cslot = load_cs(g)
            sgf, sgf_t = (sg2, sg2_t) if is_ctx else (V.sgf, V.sgf_t)

            def do_gf(j):
                ps, ps_t = proj_fm(C_GF + j * 128, ntl)
                fw.op(act, lambda e: e.activation(out=sgf[:, 0:n], in_=ps[:, 0:n], func=AF.Silu), reads=[ps_t], writes=[sgf_t])
                if is_ctx:
                    fsrc = V.fcT[:, j, :]; ftk = [V.fcT_t]
                else:
                    fsrc = fT[:, j, g * 512:(g + 1) * 512]; ftk = [fT_t[j]]
                fw.op(pool, lambda e: e.tensor_tensor(out=hT[:, 4 + j, 0:n], in0=sgf[:, 0:n], in1=fsrc, op=ALU.mult),
                      reads=[sgf_t] + ftk, writes=[hT_t[4 + j][tl] for tl in range(ntl)])

            def do_conv(j):
                if is_ctx:
                    o0 = j * 256
                    cyj, cyj_t, sgj, sgj_t, bcj, bcj_t = cy[:, o0:o0 + n], cy_t, sg2[:, o0:o0 + n], sg2_t, bc_sb[:, o0:o0 + n], bc_t
                elif j == 0:
                    cyj, cyj_t, sgj, sgj_t, bcj, bcj_t = cy[:, 0:n], cy_t, sg2[:, 0:n], sg2_t, bc_sb[:, 0:n], bc_t
                else:
                    cyj, cyj_t, sgj, sgj_t, bcj, bcj_t = V.cy2[:, 0:n], V.cy2_t, V.sg3[:, 0:n], V.sg3_t, V.bc2[:, 0:n], V.bc2_t
                ps, ps_t = proj_fm(C_BC + j * 128, ntl)
                fw.op(act, lambda e: e.copy(out=bcj, in_=ps[:, 0:n]), reads=[ps_t], writes=[bcj_t])
                ps2, ps2_t = proj_fm(C_GC + j * 128, ntl)
                fw.op(act, lambda e: e.activation(out=sgj, in_=ps2[:, 0:n], func=AF.Silu), reads=[ps2_t], writes=[sgj_t])
                if is_ctx:
                    tsrc = V.tctx; c0 = 0; ttk = [V.tctx_t]
                else:
                    tsrc = t_all; c0 = g * 512; ttk = [t_t[g], t_t[g + 1], t_t[g + 2]]
                ce = pool if j == 0 else dve
                fw.op(ce, lambda e: e.tensor_scalar(out=cyj, in0=tsrc[:, j, c0:c0 + n], scalar1=convw[:, j, 0:1], scalar2=convb[:, j:j + 1],
                                                    op0=ALU.mult, op1=ALU.add), reads=ttk + [conv_t], writes=[cyj_t])
                for tap in (1, 2):
                    if ce is pool:
                        fw.op(pool, lambda e: e.tensor_scalar(out=V.ntmp[:, 0:n], in0=tsrc[:, j, c0 + tap:c0 + tap + n], scalar1=convw[:, j, tap:tap + 1],
                                                              scalar2=None, op0=ALU.mult), reads=ttk + [conv_t], writes=[V.ntmp_t])
                        fw.op(pool, lambda e: e.tensor_tensor(out=cyj, in0=cyj, in1=V.ntmp[:, 0:n], op=ALU.add), reads=[V.ntmp_t], writes=[cyj_t])
                    else:
                        fw.op(dve, lambda e: e.scalar_tensor_tensor(out=cyj, in0=tsrc[:, j, c0 + tap:c0 + tap + n], scalar=convw[:, j, tap:tap + 1],
                                                                  in1=cyj, op0=ALU.mult, op1=ALU.add), reads=ttk + [conv_t], writes=[cyj_t])
                fw.op(ce, lambda e: e.tensor_tensor(out=cyj, in0=cyj, in1=bcj, op=ALU.mult), reads=[bcj_t], writes=[cyj_t])
                fw.op(ce, lambda e: e.tensor_tensor(out=hT[:, 6 + j, 0:n], in0=cyj, in1=sgj, op=ALU.mult),
                      reads=[cyj_t, sgj_t], writes=[hT_t[6 + j][tl] for tl in range(ntl)])
            def do_ga(j):
                ps, ps_t = proj_fm(C_GA + j * 128, ntl)
                fw.op(act, lambda e: e.activation(out=sga[:, j, 0:n], in_=ps[:, 0:n], func=AF.Silu), reads=[ps_t], writes=[sga_t[j]])

            def do_q(j):
                ps, ps_t = proj_fm(C_Q + j * 128, ntl)
                if is_ctx:
                    fw.op(act, lambda e: e.copy(out=qT_g[:, j, 0:n], in_=ps[:, 0:n]), reads=[ps_t], writes=[qT_t[j]])
                else:
                    return rope_chunk(ps, ps_t, cslot, qT_g[:, j, :], [qT_t[j]], 512, split=True)
            if is_ctx:
                for j in range(2):
                    do_gf(j)
                for j in range(2):
                    do_conv(j)
                for j in range(4):
                    do_ga(j)
                for j in range(4):
                    do_q(j)
                tile_fill = [{} for _ in range(ntl)]
            else:
                qb = do_q(0); do_conv(0); qb()
                qb = do_q(1); do_ga(0); do_ga(1); qb()
                qb = do_q(2); do_ga(2); do_ga(3); qb()
                qb = do_q(3); do_gf(0); do_gf(1); qb()
                do_conv(1)
                pf = [(lambda tl=tl: prep_tile(l, g + 1, tl, False, False)) for tl in range(4)] if prep_next else []
                tile_fill = [{}, {}, {}, {}]
                if pf:
                    tile_fill[1] = {0: [pf[0]], 5: [pf[1]]}
                    tile_fill[2] = {0: [pf[2]], 5: [pf[3]]}
            ggb, ggb_t = (V.ggc_b, V.ggc_b_t) if is_ctx else (gg_b, gg_t)
            for tl in range(ntl):
                cch = [(kcT, 0, [kcT_t], (lambda s_: vc_aug[:, 0, s_, :]), [vc_t], None),
                       (kcT, 128, [kcT_t], (lambda s_: vc_aug[:, 1, s_, :]), [vc_t], None)]
                if not is_ctx:
                    i = g * 4 + tl
                    for dlt, midx in [(0, 0 if i == 0 else 1), (1, None), (2, 3 if i == NT - 1 else 2)]:
                        ti = i + dlt
                        cch.append((kT_all, ti * 128, [kT_t[ti]], (lambda s_, ti=ti: v_aug[:, ti, s_, :]), [v_t[ti]], midx))
                sm = {k_: list(v_) for k_, v_ in tile_fill[tl].items()}

                def outproj(hlf, tl=tl):
                    yb = PS[YB[hlf]]; yb_t = PS_t[YB[hlf]]
                    for j in range(8):
                        mm(yb[:], hT[:, j, tl * 128:(tl + 1) * 128], wob[:, j, hlf * 512:(hlf + 1) * 512], j == 0, j == 7,
                           [hT_t[j][tl], wob_t], [yb_t], j == 7)

                if is_ctx:
                    attention_tile(tl, tl * 128, cch, sm)
                    outproj(0); outproj(1)
                    for f in make_epilogue(l, g, tl, is_ctx, last, ggb, ggb_t):
                        f()
                    continue
                if PNORM:
                    pn, op_prev, st_prev = PNORM.pop()
                    sm.setdefault(0, []).insert(0, pn)
                    sm.setdefault(1, []).append(lambda: op_prev(0))
                    sm.setdefault(2, []).append(lambda: op_prev(1))
                    PEND.extend(st_prev)
                for k_, f in zip((5, 6, 7, 8), PEND):
                    sm.setdefault(k_, []).append(f)
                del PEND[:]
                pn = attention_tile(tl, tl * 128, cch, sm, defer=True)
                stages = make_epilogue(l, g, tl, is_ctx, last, ggb, ggb_t)
                if tl == ntl - 1:
                    pn(); outproj(0); outproj(1)
                    PEND.extend(stages)
                else:
                    PNORM.append((pn, outproj, stages))

        PNORM = []
        HXS = []
        PEND = []

        def make_epilogue(l, g, tl, is_ctx, last, ggb, ggb_t):
            xe, xe_t, etmp, etmp_t, junk, junk_t, ss, ss_t = V.xe, V.xe_t, V.etmp, V.etmp_t, V.junk, V.junk_t, V.ss, V.ss_t
            st = {}
            if is_ctx:
                rsrc = ctx_d[tl * 128:(tl + 1) * 128, :]; rtk = []
                r0 = tl * 128
            else:
                r0 = (g * 4 + tl) * 128
                rsrc = (x_d if l == 0 else x1_d)[r0:r0 + 128, :]; rtk = [] if l == 0 else [x1_t]

            def stage_a():
                st["es"] = nxt("xe", len(xe)); st["s2"] = nxt("xn", 2)
                es_, s2 = st["es"], st["s2"]
                fw.dma(sp, xe_sem[es_], xe[es_][:], rsrc, reads=rtk, writes=[xe_t[es_]])
                for hlf in range(2):
                    fw.op(act, lambda e: e.activation(out=junk[:, hlf * 512:(hlf + 1) * 512], in_=PS[YB[hlf]][:], func=AF.Square, accum_out=ss[s2][:, hlf:hlf + 1]),
                          reads=[PS_t[YB[hlf]]], writes=[junk_t, ss_t[s2]])

            def stage_b():
                s2 = st["s2"]
                fw.op(dve, lambda e: e.tensor_tensor(out=ss[s2][:, 2:3], in0=ss[s2][:, 0:1], in1=ss[s2][:, 1:2], op=ALU.add), reads=[ss_t[s2]], writes=[ss_t[s2]])
                fw.op(dve, lambda e: e.tensor_scalar(out=ss[s2][:, 2:3], in0=ss[s2][:, 2:3], scalar1=1.0 / D, scalar2=EPS, op0=ALU.mult, op1=ALU.add),
                      reads=[ss_t[s2]], writes=[ss_t[s2]])
                fw.op(act, lambda e: e.activation(out=ss[s2][:, 0:1], in_=ss[s2][:, 2:3], func=AF.Ln), reads=[ss_t[s2]], writes=[ss_t[s2]])
                fw.op(act, lambda e: e.activation(out=ss[s2][:, 3:4], in_=ss[s2][:, 0:1], func=AF.Exp, scale=-0.5), reads=[ss_t[s2]], writes=[ss_t[s2]])

            def stage_c():
                es_, s2 = st["es"], st["s2"]
                for hlf in range(2):
                    fw.op(dve, lambda e: e.scalar_tensor_tensor(out=etmp[:, hlf * 512:(hlf + 1) * 512], in0=PS[YB[hlf]][:], scalar=ss[s2][:, 3:4],
                                                              in1=ggb[:, hlf * 512:(hlf + 1) * 512], op0=ALU.mult, op1=ALU.mult),
                          reads=[PS_t[YB[hlf]], ss_t[s2], ggb_t], writes=[etmp_t])
                fw.op(dve, lambda e: e.tensor_tensor(out=xe[es_][:], in0=xe[es_][:], in1=etmp[:], op=ALU.add), reads=[etmp_t], writes=[xe_t[es_]])

            def stage_d():
                es_ = st["es"]
                if (not is_ctx) and (not last):
                    fw.op(act, lambda e: e.activation(out=junk[:], in_=xe[es_][:], func=AF.Square, accum_out=ssq[l + 1][:, g * 4 + tl:g * 4 + tl + 1]),
                          reads=[xe_t[es_]], writes=[junk_t, ssq_t[l + 1]])
                if is_ctx:
                    fw.dma(pool, msem, ctx1_d[r0:r0 + 128, :], xe[es_][:], reads=[xe_t[es_]], writes=[ctx1_t])
                elif not last:
                    fw.dma(pool, x1sem, x1_d[r0:r0 + 128, :], xe[es_][:], reads=[xe_t[es_]], writes=[x1_t])
                else:
                    fw.dma(pool, osem, out_d[r0:r0 + 128, :], xe[es_][:], reads=[xe_t[es_]], writes=[out_t])

            return [stage_a, stage_b, stage_c, stage_d]

        def ctx_fourier():
            zctx, zctx_t, cs256_b, cs256_t, fcT, fcT_t = V.zctx, V.zctx_t, V.cs256_b, V.cs256_b_t, V.fcT, V.fcT_t
            fw.dma(sp, csem, cs256_b[:], cs256_d, writes=[cs256_t])
            for cg in range(2):
                b = STB[nxt("st", 2)]
                i = 0
                for nt in range(2):
                    for ri in range(2):
                        mm(PS[b][:, 0:256], zctx[:, nt, cg * 256 + ri * 128:cg * 256 + ri * 128 + 128], cs256_b[:, nt, ri, :], i == 0, i == 3,
                           [zctx_t, cs256_t], [PS_t[b]], i == 3)
                        i += 1
                fw.op(dve, lambda e: e.tensor_copy(out=fcT[:, cg, :], in_=PS[b][:, 0:256]), reads=[PS_t[b]], writes=[fcT_t])

        import os
        KSTOP = int(os.environ.get("KSTOP", "99"))
        for l in range(depth):
            full = (l < depth - 1)
            with contextlib.ExitStack() as sc:
                alloc(sc, ["wst", "ident_f", "ones_f", "c64", "s64", "rt1", "xs", "junk", "modrow"], {"xs": 2, "wst": 4})
                if l == 0:
                    prepass0()
                setup_layer(l, full)
                fw.barrier()
            if KSTOP <= 0:
                break
            with contextlib.ExitStack() as sc:
                names = NORM + P1
                if full:
                    names = names + ["zctx", "tctx", "fcT", "cs256_b"] + P2 + ["ggc_b", "ones_f", "ident_f", "rt1"]
                alloc(sc, names)
                if full:
                    st_t = Tok()
                    fw.dma(sp, csem, V.ident_f[:], identf_d, writes=[st_t])
                    fw.op(pool, lambda e: e.memset(V.ones_f[:], 1.0), writes=[st_t])
                    fw.op(pool, lambda e: e.memset(V.tctx[:], 0.0), writes=[V.tctx_t])
                    for hlf in range(2):
                        yb = PS[YB[hlf]]; yb_t = PS_t[YB[hlf]]
                        for k4 in range(4):
                            kc = hlf * 4 + k4
                            fw.op(dve, lambda e: e.tensor_scalar(out=V.rt1[:, 0:128], in0=V.ones_f[:], scalar1=ggm[:, kc, 1:2], scalar2=None, op0=ALU.mult),
                                  reads=[mod_t, st_t], writes=[V.rt1_t])
                            mm(yb[:, k4 * 128:(k4 + 1) * 128], V.rt1[:, 0:128], V.ident_f[:], True, True, [V.rt1_t, st_t], [yb_t], True)
                        fw.op(act, lambda e: e.copy(out=V.ggc_b[:, hlf * 512:(hlf + 1) * 512], in_=yb[:]), reads=[yb_t], writes=[V.ggc_b_t])
                phase1a_group(l, 0, True, full)
                if full:
                    ctx_fourier()
                    phase2_group(l, 0, True)
                fw.barrier()
            if KSTOP <= 1:
                break
            with contextlib.ExitStack() as sc:
                alloc(sc, NORM + P1 + ROPE + ["z_sb", "hb", "th", "hxT2"], {"xs": 2})
                HXS[:] = [(V.hxT, V.hxT_t), (V.hxT2, [[Tok(), Tok()] for _ in range(4)])]
                use_hx(0)
                for tl in range(4):
                    prep_tile(l, 0, tl, False, True)
                for g in range(NG):
                    phase1a_group(l, g, False, True, prepped=True, prep_next=(g < NG - 1))
                if KSTOP <= 2:
                    fw.barrier()
                    break
                exchange(l)
                if DBG and l == 0:
                    fw.dma(sp, osem, dbg["dbg_kT"], kT_all[:], reads=kT_t, writes=[out_t])
                    fw.dma(sp, osem, dbg["dbg_v"], v_aug[:].rearrange("p a b c -> p (a b c)"), reads=v_t, writes=[out_t])
                    fw.dma(sp, osem, dbg["dbg_t"], t_all[:].rearrange("p a b -> p (a b)"), reads=t_t, writes=[out_t])
                fw.barrier()
            if KSTOP <= 3:
                break
            with contextlib.ExitStack() as sc:
                alloc(sc, ["G_b", "zin", "Y_sb"])
                fft(l)
                if DBG and l == 0:
                    fw.dma(sp, osem, dbg["dbg_fT"], fT[:].rearrange("p a b -> p (a b)"), reads=fT_t, writes=[out_t])
                fw.barrier()
            if KSTOP <= 4:
                break
            with contextlib.ExitStack() as sc:
                alloc(sc, NORM + ROPE + P2 + ["cy2", "sg3", "bc2", "sgf"], {"xs": 2})
                for tl in range(4):
                    prep_tile(l, 0, tl, False, False)
                for g in range(NG):
                    phase2_group(l, g, False, prepped=True, prep_next=(g < NG - 1))
                    if DBG and l == 0 and g == 0:
                        fw.barrier()
                        fw.dma(sp, osem, dbg["dbg_q"], V.qT_g[:].rearrange("p a b -> p (a b)"), writes=[out_t])
                        fw.dma(sp, osem, dbg["dbg_hT"], V.hT[:].rearrange("p a b -> p (a b)"), writes=[out_t])
                        fw.dma(sp, osem, dbg["dbg_sga"], V.sga[:].rearrange("p a b -> p (a b)"), writes=[out_t])
                        fw.barrier()
                for f in PEND:
                    f()
                del PEND[:]
                if l < depth - 1:
                    finish_rstd(l + 1)
                fw.barrier()
        fin = [out_t, x1_t, ctx1_t]
        for e in fw.engs:
            fw.wait_all(e, fin)
    return nc


def _consts(half):
    bf = ml_dtypes.bfloat16
    c = {}
    c["identf"] = np.eye(128, dtype=np.float32)
    c["identb"] = np.eye(128, dtype=np.float32).astype(bf)
    pm = np.zeros((128, 128), np.float32)
    for p in range(128):
        pm[p, p ^ 16] = 1.0
    c["pm"] = pm.astype(bf)
    tok = np.arange(T0) + half * T0
    row = (tok // 64).astype(np.float32)
    col = (tok % 64).astype(np.float32)
    inv = np.power(np.float32(10000.0), -np.arange(16, dtype=np.float32) / np.float32(16)).astype(np.float32)
    cosT = np.zeros((128, T0), np.float32)
    sinT = np.zeros((128, T0), np.float32)
    for p in range(128):
        d = p % 64
        axis = d // 32
        hf = (d % 32) // 16
        fr = d % 16
        ang = ((row if axis == 0 else col) * inv[fr]).astype(np.float32)
        cosT[p] = np.cos(ang)
        sinT[p] = np.sin(ang) * (-1.0 if hf == 0 else 1.0)
    c["cosT"] = cosT
    c["sinT"] = sinT
    cc = np.arange(64)
    th = 2 * np.pi * np.outer(cc, cc) / 64.0
    c["c64x2"] = np.concatenate([np.cos(th), np.cos(th)], 1).astype(np.float32)
    c["ns64x2"] = np.concatenate([-np.sin(th), -np.sin(th)], 1).astype(np.float32)
    n1 = np.arange(64)
    ph = 2 * np.pi * np.outer(n1, n1) / 64.0
    nrm = 1.0 / np.sqrt(8192.0 * 64.0)
    m1 = np.zeros((128, 128))
    m1[0:64, 0:64] = np.cos(ph)
    m1[64:128, 0:64] = np.sin(ph)
    m1[0:64, 64:128] = -np.sin(ph)
    m1[64:128, 64:128] = np.cos(ph)
    c["m1"] = (m1 * nrm).astype(np.float32).astype(bf)
    n2 = np.arange(128)[:, None, None]
    k1 = np.arange(64)[None, :, None]
    k2 = (np.arange(64) + 64 * half)[None, None, :]
    ang = 2 * np.pi * ((n2 * (k1 + 64 * k2)) % 8192) / 8192.0
    G = np.stack([np.cos(ang), np.sin(ang)], axis=2)
    c["gtab"] = G.astype(np.float32).astype(bf)
    n = np.arange(256)
    a2 = 2 * np.pi * np.outer(n, n) / 256.0
    nr2 = 1.0 / np.sqrt(256.0 * 64.0)
    cs = np.stack([np.cos(a2) * nr2, np.sin(a2) * nr2], axis=1)
    cs = cs.reshape(2, 128, 2, 256).transpose(1, 0, 2, 3)
    c["cs256"] = np.ascontiguousarray(cs).astype(np.float32).astype(bf)
    kk = np.arange(128)[:, None]
    qq = np.arange(128)[None, :]
    NEG = np.float32(-30000.0)
    mprev = np.where(kk >= qq, np.float32(0.0), NEG).astype(np.float32)
    mnext = np.where(kk <= qq, np.float32(0.0), NEG).astype(np.float32)
    allm = np.full_like(mprev, NEG)
    masks = np.stack([mprev if half == 1 else allm, mprev, mnext, mnext if half == 0 else allm], axis=1)
    c["masks"] = np.ascontiguousarray(masks).astype(bf)
    fl = np.zeros((128, 2), np.float32)
    fl[:, 0] = 1.0 if half == 1 else 0.0
    fl[:, 1] = 1.0 if half == 0 else 0.0
    c["flags"] = fl
    return c


def _perm_heads():
    idx = []
    for j in range(4):
        for s in range(2):
            h = s * 4 + j
            idx.extend(range(h * 64, (h + 1) * 64))
    return np.array(idx)


_NC_CACHE = {}


def kernel(x, c, ctx, c_ctx, w_mod, b_mod, g_pre, g_post, w_in, w_out, sink, w_fourier, conv_w, conv_b):
    x = np.asarray(x, np.float32)
    ph = _perm_heads()
    w_in = np.asarray(w_in, np.float32)
    cols = np.concatenate([ph, np.arange(512, 768), 768 + ph, np.arange(1280, 2816)])
    w_in_p = np.ascontiguousarray(w_in[:, :, cols])
    w_out = np.asarray(w_out, np.float32)
    rows = np.concatenate([ph, np.arange(512, 1024)])
    w_out_p = np.ascontiguousarray(w_out[:, rows, :])
    b_mod = np.asarray(b_mod, np.float32)
    bmodT = np.ascontiguousarray(b_mod.reshape(2, 24, 128).transpose(0, 2, 1))
    gpreT = np.ascontiguousarray(np.asarray(g_pre, np.float32).reshape(2, 8, 128).transpose(0, 2, 1))
    gpostT = np.ascontiguousarray(np.asarray(g_post, np.float32).reshape(2, 8, 128).transpose(0, 2, 1))
    convwT = np.ascontiguousarray(np.asarray(conv_w, np.float32).reshape(2, 3, 2, 128).transpose(0, 3, 2, 1))
    convbT = np.ascontiguousarray(np.asarray(conv_b, np.float32).reshape(2, 2, 128).transpose(0, 2, 1))
    c = np.asarray(c, np.float32)
    c_ctx = np.asarray(c_ctx, np.float32)
    if "nc" not in _NC_CACHE:
        _NC_CACHE["nc"] = build_nc()
    nc = _NC_CACHE["nc"]
    consts = [_consts(0), _consts(1)]
    in_maps = []
    for core in range(8):
        b, half = core // 2, core % 2
        cT = np.stack([c[b].reshape(8, 128).T, c_ctx.reshape(8, 128).T], axis=-1)
        m = {"x": np.ascontiguousarray(x[b, half * T0:(half + 1) * T0]), "ctx": np.ascontiguousarray(np.asarray(ctx, np.float32)[b]),
             "cT": np.ascontiguousarray(cT.astype(np.float32)), "w_mod": np.asarray(w_mod, np.float32), "bmodT": bmodT, "gpreT": gpreT, "gpostT": gpostT,
             "w_in": w_in_p, "w_out": w_out_p, "sink": np.asarray(sink, np.float32), "w_fourier": np.asarray(w_fourier, np.float32),
             "convwT": convwT, "convbT": convbT}
        m.update(consts[half])
        in_maps.append(m)
    res = run_bass_kernel_spmd(nc, in_maps, core_ids=list(range(8)))
    out = np.empty((4, 2 * T0, D), np.float32)
    for core in range(8):
        b, half = core // 2, core % 2
        out[b, half * T0:(half + 1) * T0] = np.asarray(res.results[core]["out"], np.float32)
    return out
```

```python
import contextlib
import numpy as np
import ml_dtypes
import concourse.bass as bass
import concourse.mybir as mybir
from concourse.bass_utils import run_bass_kernel_spmd

F32 = mybir.dt.float32
BF16 = mybir.dt.bfloat16
AF = mybir.ActivationFunctionType
ALU = mybir.AluOpType
SEM_ROT = 30000


class SemW:
    def __init__(s, h, name):
        s.h = h
        s.name = name
        s.cnt = 0


class Tok:
    def __init__(s, name=""):
        s.name = name
        s.w = None
        s.r = {}
        s.excl = False
        s.multi = False
        s.wm = {}


class Eng:
    def __init__(s, fwk, name, h, is_pe=False):
        s.fw = fwk
        s.name = name
        s.h = h
        s.is_pe = is_pe
        s.sem = fwk.new_sem("p_" + name)
        s.seen = {}

    def wait(s, ev):
        if ev is None:
            return
        sw, val = ev
        if s.seen.get(sw, 0) >= val:
            return
        if sw is s.sem:
            if s.is_pe or not s.fw.same_eng_sync:
                return
            assert val <= sw.cnt, "self-wait on future inc (%s)" % s.name
        s.h.wait_ge(sw.h, val)
        s.seen[sw] = val


class FW:
    def __init__(s, nc, stack, same_eng_sync=True):
        s.nc = nc
        s.stack = stack
        s.same_eng_sync = same_eng_sync
        s.nsem = 0
        s.all_sems = []
        s.pe = Eng(s, "pe", nc.tensor, is_pe=True)
        s.act = Eng(s, "act", nc.scalar)
        s.dve = Eng(s, "dve", nc.vector)
        s.pool = Eng(s, "pool", nc.gpsimd)
        s.sp = Eng(s, "sp", nc.sync)
        s.engs = [s.pe, s.act, s.dve, s.pool, s.sp]

    def new_sem(s, name):
        s.nsem += 1
        h = s.stack.enter_context(s.nc.semaphore("%s_%d" % (name, s.nsem)))
        sw = SemW(h, name)
        s.all_sems.append(sw)
        return sw

    def sbuf(s, name, shape, dt, stack=None):
        s.nalloc = getattr(s, "nalloc", 0) + 1
        return (stack or s.stack).enter_context(s.nc.sbuf_tensor("%s_%d" % (name, s.nalloc), list(shape), dt))

    def barrier(s):
        for e in s.engs:
            for sw in s.all_sems:
                if sw.cnt > 0:
                    e.wait((sw, sw.cnt))

    def psum(s, name, shape, dt=F32):
        return s.stack.enter_context(s.nc.psum_tensor(name, list(shape), dt))

    def _deps(s, eng, reads, writes):
        for b in reads:
            eng.wait(b.w)
            if b.multi:
                for sw, v in list(b.wm.items()):
                    eng.wait((sw, v))
            if b.excl:
                for sw, v in list(b.r.items()):
                    if sw is not eng.sem:
                        eng.wait((sw, v))
        for b in writes:
            if not b.multi:
                eng.wait(b.w)
            for sw, v in list(b.r.items()):
                eng.wait((sw, v))

    def _post(s, ev, reads, writes):
        sw, v = ev
        for b in reads:
            if b.r.get(sw, 0) < v:
                b.r[sw] = v
        for b in writes:
            if b.multi:
                if b.wm.get(sw, 0) < v:
                    b.wm[sw] = v
                continue
            b.w = ev
            b.r = {}

    def op(s, eng, fn, reads=(), writes=(), inc=True):
        s._deps(eng, reads, writes)
        inst = fn(eng.h)
        if eng.sem.cnt >= SEM_ROT:
            eng.sem = s.new_sem("p_" + eng.name)
        if inc:
            eng.sem.cnt += 1
            inst.then_inc(eng.sem.h, 1)
            ev = (eng.sem, eng.sem.cnt)
        else:
            assert eng.is_pe
            ev = (eng.sem, eng.sem.cnt + 1)
        s._post(ev, reads, writes)
        return inst

    def dma(s, q, dsem, out, in_, reads=(), writes=(), **kw):
        s._deps(q, reads, writes)
        inst = q.h.dma_start(out=out, in_=in_, **kw)
        dsem.cnt += 16
        inst.then_inc(dsem.h, 16)
        ev = (dsem, dsem.cnt)
        s._post(ev, reads, writes)
        return ev

    def wait_all(s, eng, toks):
        for b in toks:
            eng.wait(b.w)
            for sw, v in list(b.wm.items()):
                eng.wait((sw, v))
            for sw, v in list(b.r.items()):
                eng.wait((sw, v))


T0 = 4096
NT = 32
NG = 8
D = 1024
PW = 2816
NCTX = 256
EPS = 1e-6
C_Q, C_K, C_V, C_GA, C_UF, C_GF, C_ZC, C_BC, C_CC, C_GC = 0, 512, 640, 768, 1280, 1536, 1792, 2048, 2304, 2560


def build_nc(depth=2):
    nc = bass.Bass("TRN2", target_bir_lowering=False)

    def din(name, shape, dt=F32):
        return nc.dram_tensor(name, list(shape), dt, kind="ExternalInput").ap()

    x_d = din("x", [T0, D])
    ctx_d = din("ctx", [NCTX, D])
    cT_d = din("cT", [128, 8, 2])
    wmod_d = din("w_mod", [2, D, 3 * D])
    bmodT_d = din("bmodT", [2, 128, 24])
    gpreT_d = din("gpreT", [2, 128, 8])
    gpostT_d = din("gpostT", [2, 128, 8])
    win_d = din("w_in", [2, D, PW])
    wout_d = din("w_out", [2, D, D])
    sink_d = din("sink", [2, 8])
    wf_d = din("w_fourier", [2, 4, 64, 64])
    convwT_d = din("convwT", [2, 128, 2, 3])
    convbT_d = din("convbT", [2, 128, 2])
    cos_d = din("cosT", [128, T0])
    sin_d = din("sinT", [128, T0])
    identf_d = din("identf", [128, 128])
    c64_d = din("c64x2", [64, 128])
    s64_d = din("ns64x2", [64, 128])
    flags_d = din("flags", [128, 2])
    identb_d = din("identb", [128, 128], BF16)
    pm_d = din("pm", [128, 128], BF16)
    m1_d = din("m1", [128, 128], BF16)
    masks_d = din("masks", [128, 4, 128], BF16)
    cs256_d = din("cs256", [128, 2, 2, 256], BF16)
    g_d = din("gtab", [128, 64, 2, 64], BF16)
    out_d = nc.dram_tensor("out", [T0, D], F32, kind="ExternalOutput").ap()

    x1_d = nc.dram_tensor("x1s", [T0, D], F32).ap()
    ctx1_d = nc.dram_tensor("ctx1s", [NCTX, D], F32).ap()
    ez_in = [[nc.dram_tensor("ezin%d_%d" % (l, q), [128, 2048], BF16).ap() for q in range(8)] for l in range(2)]
    ez_out = [[nc.dram_tensor("ezout%d_%d" % (l, q), [256, 2048], BF16).ap() for q in range(8)] for l in range(2)]
    import os
    DBG = bool(os.environ.get("KDBG"))
    dbg = {}
    if DBG:
        for nm, shp in [("dbg_kT", [128, (NT + 2) * 128]), ("dbg_v", [128, (NT + 2) * 256]), ("dbg_t", [128, 2 * (T0 + 2)]),
                        ("dbg_fT", [128, 2 * T0]), ("dbg_q", [128, 4 * 512]), ("dbg_hT", [128, 8 * 512]), ("dbg_sga", [128, 4 * 512])]:
            dbg[nm] = nc.dram_tensor(nm, shp, BF16, kind="ExternalOutput").ap()
    eh_in = [nc.dram_tensor("ehin%d" % l, [128, 640], BF16).ap() for l in range(2)]
    eh_out = [nc.dram_tensor("ehout%d" % l, [256, 640], BF16).ap() for l in range(2)]

    with contextlib.ExitStack() as st:
        fw = FW(nc, st)
        pe, act, dve, pool, sp = fw.pe, fw.act, fw.dve, fw.pool, fw.sp
        st.enter_context(nc.Block())

        import os
        wbf = fw.sbuf("wbf", [128, 8, PW], BF16); wbf_t = Tok("wbf")
        wob = fw.sbuf("wob", [128, 8, D], BF16); wob_t = Tok("wob")
        kT_all = fw.sbuf("kT_all", [128, (NT + 2) * 128], BF16); kT_t = [Tok("kT%d" % i) for i in range(NT + 2)]
        v_aug = fw.sbuf("v_aug", [128, NT + 2, 2, 128], BF16); v_t = [Tok("v%d" % i) for i in range(NT + 2)]
        t_all = fw.sbuf("t_all", [128, 2, T0 + 2], BF16); t_t = [Tok("t%d" % i) for i in range(NG + 2)]
        fT = fw.sbuf("fT", [128, 2, T0], BF16); fT_t = [Tok("fT0"), Tok("fT1")]
        kcT = fw.sbuf("kcT", [128, NCTX], BF16); kcT_t = Tok("kcT")
        vc_aug = fw.sbuf("vc_aug", [128, 2, 2, 128], BF16); vc_t = Tok("vc")
        ident_b = fw.sbuf("ident_b", [128, 128], BF16)
        pm_b = fw.sbuf("pm_b", [128, 128], BF16); m1_b = fw.sbuf("m1_b", [128, 128], BF16)
        masks_b = fw.sbuf("masks_b", [128, 4, 128], BF16)
        flags = fw.sbuf("flags", [128, 2], F32)
        AB = fw.sbuf("AB", [128, 2, 256], BF16); AB_t = Tok("AB")
        gg_b = fw.sbuf("gg_b", [128, D], F32); gg_t = Tok("gg")
        cT = fw.sbuf("cT", [128, 8, 2], F32); scs = fw.sbuf("scs", [128, 8, 2], F32)
        modT = fw.sbuf("modT", [128, 24, 2], F32); am = fw.sbuf("am", [128, 8, 2], F32); ggm = fw.sbuf("ggm", [128, 8, 2], F32)
        mod_t = Tok("mod")
        bmodT = fw.sbuf("bmodT", [128, 24], F32); gpreT = fw.sbuf("gpreT", [128, 8], F32); gpostT = fw.sbuf("gpostT", [128, 8], F32)
        esink = fw.sbuf("esink", [128, 8], F32); esink_t = Tok("esink")
        convw = fw.sbuf("convw", [128, 2, 3], F32); convb = fw.sbuf("convb", [128, 2], F32); conv_t = Tok("convp")
        const_t = Tok("const")
        ssq = [fw.sbuf("ssq%d" % l_, [128, NT], F32) for l_ in range(2)]; ssq_t = [Tok("ssq0"), Tok("ssq1")]
        rstd = [fw.sbuf("rstd%d" % l_, [128, NT], F32) for l_ in range(2)]; rstd_t = [Tok("rstd0"), Tok("rstd1")]
        SPECS = {
            "xs": ([128, D], F32, 1), "xe": ([128, D], F32, 1), "etmp": ([128, D], F32, 0), "junk": ([128, D], BF16, 0),
            "ss": ([128, 4], F32, 2), "xn": ([128, D], BF16, 2), "hxT": ([128, 8, 512], BF16, 0),
            "raw_b": ([128, 512], BF16, 0), "rt1": ([128, 512], F32, 0), "rt2": ([128, 512], F32, 0),
            "cs_sb": ([128, 2, 512], F32, 1), "qT_g": ([128, 4, 512], BF16, 0), "sga": ([128, 4, 512], BF16, 0),
            "sg2": ([128, 512], BF16, 0), "bc_sb": ([128, 512], F32, 0), "cy": ([128, 512], F32, 0),
            "hT": ([128, 8, 512], BF16, 0), "PT": ([128, 512], BF16, 4), "rec": ([128, 512], F32, 0), "ntmp": ([128, 512], F32, 0),
            "ufT": ([128, 2, 512], BF16, 0), "zc_sb": ([128, 512], F32, 0), "z_sb": ([128, 4, 8, 64], BF16, 2), "hxT2": ([128, 8, 512], BF16, 0),
            "zctx": ([128, 2, 512], BF16, 0), "tctx": ([128, 2, NCTX + 2], BF16, 0), "fcT": ([128, 2, NCTX], BF16, 0),
            "hb": ([128, 640], BF16, 0), "th": ([128, 4], BF16, 0), "wst": ([128, PW], F32, 2),
            "G_b": ([128, 64, 2, 64], BF16, 0), "zin": ([128, 128, 64], BF16, 0), "Y_sb": ([128, 128, 128], BF16, 0),
            "ident_f": ([128, 128], F32, 0), "ones_f": ([128, 128], F32, 0), "c64": ([64, 128], F32, 0), "s64": ([64, 128], F32, 0),
            "cs256_b": ([128, 2, 2, 256], BF16, 0), "ggc_b": ([128, D], F32, 0),
            "cy2": ([128, 512], F32, 0), "sg3": ([128, 512], BF16, 0), "bc2": ([128, 512], F32, 0), "sgf": ([128, 512], BF16, 0),
        }
        NORM = ["xs", "junk", "ss", "xn", "hxT"]
        ROPE = ["raw_b", "rt1", "rt2", "cs_sb"]
        P2 = ["qT_g", "sga", "sg2", "bc_sb", "cy", "hT", "PT", "rec", "ntmp", "xe", "etmp"]
        P1 = ["ufT", "zc_sb"]
        import types
        V = types.SimpleNamespace()

        ARENA = 38400
        arena = fw.sbuf("arena", [128, ARENA], BF16)

        def carve(off, shape, dt):
            nel = 1
            for d_ in shape[1:]:
                nel *= d_
            sz = nel * (2 if dt == F32 else 1)
            sz = (sz + 15) // 16 * 16
            ap = arena[0:shape[0], off:off + nel * (2 if dt == F32 else 1)]
            if dt == F32:
                ap = ap.bitcast(F32)
            if len(shape) == 3:
                ap = ap.rearrange("p (a b) -> p a b", a=shape[1])
            elif len(shape) == 4:
                ap = ap.rearrange("p (a b c) -> p a b c", a=shape[1], b=shape[2])
            return ap, off + sz

        def alloc(stack, names, slots={}):
            off = 0
            for nm in names:
                shape, dt, ns = SPECS[nm]
                ns = slots.get(nm, ns)
                if ns == 0:
                    ap, off = carve(off, shape, dt)
                    setattr(V, nm, ap); setattr(V, nm + "_t", Tok(nm))
                else:
                    lst = []
                    for _ in range(ns):
                        ap, off = carve(off, shape, dt)
                        lst.append(ap)
                    setattr(V, nm, lst); setattr(V, nm + "_t", [Tok(nm) for _ in range(ns)])
            assert off <= ARENA, "arena overflow %d > %d" % (off, ARENA)
            V.hxT_t = [[Tok(), Tok()] for _ in range(4)]
            V.hT_t = [[Tok() for _ in range(4)] for _ in range(8)]
            V.qT_t = [Tok() for _ in range(4)]
            V.sga_t = [Tok() for _ in range(4)]
            V.ufT_t = [Tok(), Tok()]
            V.Y_t = [Tok(), Tok()]

        xs_sem = [fw.new_sem("xs") for _ in range(2)]; xe_sem = [fw.new_sem("xe") for _ in range(2)]
        cs_sem = [fw.new_sem("cs") for _ in range(2)]; wst_sem = [fw.new_sem("wst") for _ in range(2)]
        zin_sem = fw.new_sem("zin")

        TR = fw.psum("TR", [128, 8, 128], BF16); TR_t = Tok()
        PS = [fw.psum("ps%d" % i, [128, 512]) for i in range(7)]; PS_t = [Tok() for _ in range(7)]
        PJ = [0, 1]; STB = [2, 3]; PVB = 4; YB = [5, 6]
        TR_t.excl = True
        for t_ in PS_t:
            t_.excl = True
        rot = {}

        def nxt(key, n):
            v = rot.get(key, 0) % n
            rot[key] = rot.get(key, 0) + 1
            return v

        csem = fw.new_sem("const"); osem = fw.new_sem("out"); x1sem = fw.new_sem("x1"); zsem = fw.new_sem("zst")
        zsems = [fw.new_sem("zst0"), fw.new_sem("zst1")]
        hsem = fw.new_sem("halo"); ccsem = fw.new_sem("cc"); msem = fw.new_sem("misc")
        x1_t = Tok("x1"); ctx1_t = Tok("ctx1"); out_t = Tok("out")
        ezin_t = [Tok(), Tok()]; ezout_t = [[Tok() for _ in range(8)] for _ in range(2)];
        for t_ in ezin_t + [x1_t, ctx1_t, out_t]:
            t_.multi = True
        ehin_t = [Tok(), Tok()]; ehout_t = [Tok(), Tok()]

        def bc_last(ap, n):
            return bass.AP(ap.tensor, ap.offset, [list(d) for d in ap.ap] + [[0, n]])

        def bc_mid(ap, n):
            d = [list(x) for x in ap.ap]
            return bass.AP(ap.tensor, ap.offset, [d[0], [0, n]] + d[1:])

        for dst, src in [(ident_b, identb_d), (pm_b, pm_d), (m1_b, m1_d), (masks_b, masks_d), (flags, flags_d), (cT, cT_d)]:
            fw.dma(sp, csem, dst[:], src, writes=[const_t])
        fw.op(pool, lambda e: e.memset(AB[:], 0.0), writes=[AB_t])
        fw.op(pool, lambda e: e.memset(v_aug[:], 1.0), writes=v_t)
        fw.op(pool, lambda e: e.memset(vc_aug[:], 1.0), writes=[vc_t])
        fw.op(act, lambda e: e.activation(out=scs[:], in_=cT[:], func=AF.Silu), reads=[const_t], writes=[mod_t])

        def mm(out, lhsT, rhs, start, stop, reads, writes, last):
            fw.op(pe, lambda e: e.matmul(out, lhsT=lhsT, rhs=rhs, start=start, stop=stop), reads=reads, writes=writes, inc=last)

        def cast_copy(eng, out, in_, reads, writes):
            if eng is act:
                fw.op(act, lambda e: e.copy(out=out, in_=in_), reads=reads, writes=writes)
            else:
                fw.op(eng, lambda e: e.tensor_copy(out=out, in_=in_), reads=reads, writes=writes)

        def setup_layer(l, full):
            wst, wst_t, ident_f, ones_f, c64, s64, rt1, rt1_t = V.wst, V.wst_t, V.ident_f, V.ones_f, V.c64, V.s64, V.rt1, V.rt1_t
            st_t = Tok("setup")
            for dst, src in [(ident_f, identf_d), (c64, c64_d), (s64, s64_d)]:
                fw.dma(sp, csem, dst[:], src, writes=[st_t])
            fw.op(pool, lambda e: e.memset(ones_f[:], 1.0), writes=[st_t])
            for dst, src in [(bmodT, bmodT_d[l]), (gpreT, gpreT_d[l]), (gpostT, gpostT_d[l])]:
                fw.dma(sp, csem, dst[:], src, writes=[mod_t])
            fw.dma(sp, csem, convw[:], convwT_d[l], writes=[conv_t])
            fw.dma(sp, csem, convb[:], convbT_d[l], writes=[conv_t])
            fw.dma(sp, csem, esink[:], bass.AP(sink_d.tensor, l * 8, [[0, 128], [1, 8]]), writes=[esink_t])
            for tk in (st_t, mod_t, conv_t, esink_t):
                tk.w = (csem, csem.cnt)
            fw.op(act, lambda e: e.activation(out=esink[:], in_=esink[:], func=AF.Exp), reads=[], writes=[esink_t])
            mps = PS[PVB]; mps_t = PS_t[PVB]
            wm_v = wmod_d[l].rearrange("(kc p) j -> p kc j", p=128)
            for jq in range(12):
                s_ = nxt("wst", 2)
                wv = wst[s_][:, 0:2048].rearrange("p (kc j) -> p kc j", kc=8)
                fw.dma(sp, wst_sem[s_], wv, wm_v[:, :, jq * 256:(jq + 1) * 256], writes=[wst_t[s_]])
                for jj in range(2):
                    jc = jq * 2 + jj
                    for kc in range(8):
                        mm(mps[:, jc * 2:jc * 2 + 2], wv[:, kc, jj * 128:(jj + 1) * 128], scs[:, kc, :], kc == 0, kc == 7,
                           [wst_t[s_], mod_t], [mps_t], kc == 7)
            fw.op(dve, lambda e: e.tensor_tensor(out=modT[:], in0=mps[:, 0:48].rearrange("p (j v) -> p j v", v=2), in1=bc_last(bmodT[:], 2), op=ALU.add),
                  reads=[mps_t, mod_t], writes=[mod_t])
            fw.op(dve, lambda e: e.scalar_tensor_tensor(out=am[:], in0=modT[:, 8:16, :], scalar=1.0, in1=bc_last(gpreT[:], 2), op0=ALU.add, op1=ALU.mult),
                  reads=[mod_t], writes=[mod_t])
            fw.op(dve, lambda e: e.tensor_tensor(out=ggm[:], in0=modT[:, 16:24, :], in1=bc_last(gpostT[:], 2), op=ALU.mult), reads=[mod_t], writes=[mod_t])
            for v in range(1):
                dstb, dst_t = gg_b, gg_t
                for hlf in range(2):
                    yb = PS[YB[hlf]]; yb_t = PS_t[YB[hlf]]
                    for k4 in range(4):
                        kc = hlf * 4 + k4
                        fw.op(dve, lambda e: e.tensor_scalar(out=rt1[:, 0:128], in0=ones_f[:], scalar1=ggm[:, kc, v:v + 1], scalar2=None, op0=ALU.mult),
                              reads=[mod_t, st_t], writes=[rt1_t])
                        mm(yb[:, k4 * 128:(k4 + 1) * 128], rt1[:, 0:128], ident_f[:], True, True, [rt1_t, st_t], [yb_t], True)
                    fw.op(act, lambda e: e.copy(out=dstb[:, hlf * 512:(hlf + 1) * 512], in_=yb[:]), reads=[yb_t], writes=[dst_t])
            for kc in range(8):
                s_ = nxt("wst", 2)
                fw.dma(sp, wst_sem[s_], wst[s_][:], win_d[l, kc * 128:(kc + 1) * 128, :], writes=[wst_t[s_]])
                cast_copy([pool, dve, act][kc % 3], wbf[:, kc, :], wst[s_][:], [wst_t[s_]], [wbf_t])
            for k2 in range(4):
                s_ = nxt("wst", 2)
                wv = wst[s_][:, 0:2048].rearrange("p (a j) -> p a j", a=2)
                fw.dma(sp, wst_sem[s_], wv, wout_d[l, k2 * 256:(k2 + 1) * 256, :].rearrange("(a p) j -> p a j", p=128), writes=[wst_t[s_]])
                cast_copy([pool, dve][k2 % 2], wob[:, k2 * 2:k2 * 2 + 2, :], wv, [wst_t[s_]], [wob_t])
            s_ = nxt("wst", 2)
            wfv = wst[s_][0:64, 0:256].rearrange("p (g d) -> p g d", g=4)
            fw.dma(sp, wst_sem[s_], wfv, wf_d[l].rearrange("g c d -> c g d"), writes=[wst_t[s_]])
            for ri, cm in enumerate([c64, s64]):
                pb = PS[STB[ri]]; pb_t = PS_t[STB[ri]]
                mm(pb[:, 0:256], cm[:], wst[s_][0:64, 0:256], True, True, [wst_t[s_], st_t], [pb_t], True)
                for cg in range(2):
                    fw.op(dve, lambda e: e.tensor_copy(out=AB[0:64, cg, ri * 128:ri * 128 + 64], in_=pb[0:64, (2 * cg) * 64:(2 * cg) * 64 + 64]),
                          reads=[pb_t], writes=[AB_t])
                    fw.op(dve, lambda e: e.tensor_copy(out=AB[64:128, cg, ri * 128 + 64:ri * 128 + 128], in_=pb[64:128, (2 * cg + 1) * 64:(2 * cg + 1) * 64 + 64]),
                          reads=[pb_t], writes=[AB_t])

        def norm_T(src_ap, src_toks, v, tl, rs_ap=None, rs_toks=()):
            xs, xs_t, ss, ss_t, xn, xn_t, hxT, hxT_t, junk, junk_t = V.xs, V.xs_t, V.ss, V.ss_t, V.xn, V.xn_t, V.hxT, V.hxT_t, V.junk, V.junk_t
            s_ = nxt("xs", len(xs))
            n_ = nxt("xn", 2)
            fw.dma(sp, xs_sem[s_], xs[s_][:], src_ap, reads=src_toks, writes=[xs_t[s_]])
            if rs_ap is None:
                fw.op(act, lambda e: e.activation(out=junk[:], in_=xs[s_][:], func=AF.Square, accum_out=ss[n_][:, 0:1]),
                      reads=[xs_t[s_]], writes=[junk_t, ss_t[n_]])
                fw.op(dve, lambda e: e.tensor_scalar(out=ss[n_][:, 1:2], in0=ss[n_][:, 0:1], scalar1=1.0 / D, scalar2=EPS, op0=ALU.mult, op1=ALU.add),
                      reads=[ss_t[n_]], writes=[ss_t[n_]])
                fw.op(act, lambda e: e.activation(out=ss[n_][:, 3:4], in_=ss[n_][:, 1:2], func=AF.Sqrt), reads=[ss_t[n_]], writes=[ss_t[n_]])
                fw.op(dve, lambda e: e.reciprocal(out=ss[n_][:, 2:3], in_=ss[n_][:, 3:4]), reads=[ss_t[n_]], writes=[ss_t[n_]])
                rs_ap = ss[n_][:, 2:3]; rs_toks = [ss_t[n_]]
            fw.op(act, lambda e: e.activation(out=xn[n_][:], in_=xs[s_][:], func=AF.Identity, scale=rs_ap),
                  reads=[xs_t[s_]] + list(rs_toks), writes=[xn_t[n_]])
            for kc in range(8):
                fw.op(pe, lambda e: e.transpose(out=TR[:, kc, :], in_=xn[n_][:, kc * 128:(kc + 1) * 128], identity=ident_b[:]),
                      reads=[xn_t[n_], const_t], writes=[TR_t], inc=(kc == 7))
            o = hxT[:, :, tl * 128:(tl + 1) * 128]
            wt = [hxT_t[tl][0], hxT_t[tl][1]]
            fw.op(dve, lambda e: e.tensor_tensor(out=o, in0=TR[:], in1=bc_last(am[:, :, v], 128), op=ALU.mult), reads=[TR_t, mod_t], writes=wt)
            fw.op(dve, lambda e: e.tensor_tensor(out=o, in0=o, in1=bc_last(modT[:, 0:8, v], 128), op=ALU.add), reads=[mod_t], writes=wt)

        def proj_fm(col, ntl):
            hxT, hxT_t = V.hxT, V.hxT_t
            b = PJ[nxt("pj", 2)]
            for kc in range(8):
                mm(PS[b][:, 0:ntl * 128], wbf[:, kc, col:col + 128], hxT[:, kc, 0:ntl * 128], kc == 0, kc == 7,
                   [wbf_t] + [hxT_t[tl][kc % 2] for tl in range(ntl)], [PS_t[b]], kc == 7)
            return PS[b], PS_t[b]

        def load_cs(g):
            cs_sb, cs_t = V.cs_sb, V.cs_sb_t
            s_ = nxt("cs", len(cs_sb))
            fw.dma(sp, cs_sem[s_], cs_sb[s_][:, 0, :], cos_d[:, g * 512:(g + 1) * 512], writes=[cs_t[s_]])
            fw.dma(sp, cs_sem[s_], cs_sb[s_][:, 1, :], sin_d[:, g * 512:(g + 1) * 512], writes=[cs_t[s_]])
            return s_

        def rope_chunk(ps, ps_t, cs_slot, out_ap, out_toks, n, split=False):
            raw_b, raw_t, rt1, rt1_t, rt2, rt2_t, cs_sb, cs_t = V.raw_b, V.raw_b_t, V.rt1, V.rt1_t, V.rt2, V.rt2_t, V.cs_sb, V.cs_sb_t
            fw.op(act, lambda e: e.copy(out=raw_b[:, 0:n], in_=ps[:, 0:n]), reads=[ps_t], writes=[raw_t])
            if os.environ.get("KR2"):
                return
            fw.op(dve, lambda e: e.tensor_tensor(out=rt1[:, 0:n], in0=ps[:, 0:n], in1=cs_sb[cs_slot][:, 0, 0:n], op=ALU.mult),
                  reads=[ps_t, cs_t[cs_slot]], writes=[rt1_t])
            def part_b():
                b = PJ[nxt("pj", 2)]
                mm(PS[b][:, 0:n], pm_b[:], raw_b[:, 0:n], True, True, [raw_t, const_t], [PS_t[b]], True)
                fw.op(dve, lambda e: e.tensor_tensor(out=rt2[:, 0:n], in0=PS[b][:, 0:n], in1=cs_sb[cs_slot][:, 1, 0:n], op=ALU.mult),
                      reads=[PS_t[b], cs_t[cs_slot]], writes=[rt2_t])
                fw.op(dve, lambda e: e.tensor_tensor(out=out_ap, in0=rt1[:, 0:n], in1=rt2[:, 0:n], op=ALU.add),
                      reads=[rt1_t, rt2_t], writes=out_toks)

            if split:
                return part_b
            part_b()

        def prep_tile(l, g, tl, is_ctx, ctx_src1):
            if is_ctx:
                src = (ctx_d if (l == 0 or not ctx_src1) else ctx1_d)[tl * 128:(tl + 1) * 128, :]
                stoks = [] if (l == 0 or not ctx_src1) else [ctx1_t]
                norm_T(src, stoks, 1, tl)
            else:
                i = g * 4 + tl
                src = (x_d if l == 0 else x1_d)[i * 128:(i + 1) * 128, :]
                stoks = [] if l == 0 else [x1_t]
                norm_T(src, stoks, 0, tl, rstd[l][:, i:i + 1], [rstd_t[l]])

        def finish_rstd(l):
            fw.op(dve, lambda e: e.tensor_scalar(out=ssq[l][:], in0=ssq[l][:], scalar1=1.0 / D, scalar2=EPS, op0=ALU.mult, op1=ALU.add),
                  reads=[ssq_t[l]], writes=[ssq_t[l]])
            fw.op(act, lambda e: e.activation(out=ssq[l][:], in_=ssq[l][:], func=AF.Sqrt), reads=[ssq_t[l]], writes=[ssq_t[l]])
            fw.op(dve, lambda e: e.reciprocal(out=rstd[l][:], in_=ssq[l][:]), reads=[ssq_t[l]], writes=[rstd_t[l]])

        def prepass0():
            xs, xs_t, junk, junk_t = V.xs, V.xs_t, V.junk, V.junk_t
            for i in range(NT):
                s_ = nxt("xs", len(xs))
                fw.dma(pool, xs_sem[s_], xs[s_][:], x_d[i * 128:(i + 1) * 128, :], writes=[xs_t[s_]])
                fw.op(act, lambda e: e.activation(out=junk[:], in_=xs[s_][:], func=AF.Square, accum_out=ssq[0][:, i:i + 1]),
                      reads=[xs_t[s_]], writes=[junk_t, ssq_t[0]])
            finish_rstd(0)

        def use_hx(k):
            V.hxT, V.hxT_t = HXS[k]

        def phase1a_group(l, g, is_ctx, full, prepped=False, prep_next=False):
            if not is_ctx:
                use_hx(g % 2)
            hxT, hxT_t = V.hxT, V.hxT_t
            pq = [tl for tl in range(4)] if (prep_next and not is_ctx) else []

            def prep_one():
                if pq:
                    use_hx((g + 1) % 2)
                    prep_tile(l, g + 1, pq.pop(0), False, True)
                    use_hx(g % 2)
            ntl = 2 if is_ctx else 4
            n = ntl * 128
            v = 1 if is_ctx else 0
            if not prepped:
                for tl in range(ntl):
                    prep_tile(l, g, tl, is_ctx, True)
            import os
            KSUB = int(os.environ.get("KSUB", "99"))
            if KSUB <= 1:
                return
            ps, ps_t = proj_fm(C_K, ntl)
            if is_ctx:
                fw.op(act, lambda e: e.copy(out=kcT[:], in_=ps[:, 0:n]), reads=[ps_t], writes=[kcT_t])
            elif int(os.environ.get("KR", "99")) <= 0:
                pass
            else:
                cslot = load_cs(g)
                rope_chunk(ps, ps_t, cslot, kT_all[:, (1 + g * 4) * 128:(1 + g * 4) * 128 + 512], [kT_t[1 + g * 4 + i] for i in range(4)], 512)
            prep_one()
            vb = PS[PVB]; vb_t = PS_t[PVB]
            for tl in range(ntl):
                for kc in range(8):
                    mm(vb[:, tl * 128:(tl + 1) * 128], hxT[:, kc, tl * 128:(tl + 1) * 128], wbf[:, kc, C_V:C_V + 128], kc == 0, kc == 7,
                       [wbf_t, hxT_t[tl][kc % 2]], [vb_t], kc == 7 and tl == ntl - 1)
            vbv = vb[:, 0:n].rearrange("p (t c) -> p t c", c=128)
            if is_ctx:
                fw.op(dve, lambda e: e.tensor_copy(out=vc_aug[:, :, 0, 0:64], in_=vbv[:, :, 0:64]), reads=[vb_t], writes=[vc_t])
                fw.op(dve, lambda e: e.tensor_copy(out=vc_aug[:, :, 1, 64:128], in_=vbv[:, :, 64:128]), reads=[vb_t], writes=[vc_t])
            else:
                vt = [v_t[1 + g * 4 + i] for i in range(4)]
                fw.op(dve, lambda e: e.tensor_copy(out=v_aug[:, 1 + g * 4:5 + g * 4, 0, 0:64], in_=vbv[:, :, 0:64]), reads=[vb_t], writes=vt)
                fw.op(dve, lambda e: e.tensor_copy(out=v_aug[:, 1 + g * 4:5 + g * 4, 1, 64:128], in_=vbv[:, :, 64:128]), reads=[vb_t], writes=vt)
            if is_ctx and not full:
                return
            prep_one()
            ufT, ufT_t, zc_sb, zc_t = V.ufT, V.ufT_t, V.zc_sb, V.zc_sb_t
            for cg in range(2):
                ps, ps_t = proj_fm(C_UF + cg * 128, ntl)
                fw.op(act, lambda e: e.copy(out=ufT[:, cg, 0:n], in_=ps[:, 0:n]), reads=[ps_t], writes=[ufT_t[cg]])
            prep_one()
            zs = g % 2
            for tl in range(ntl):
                zb = YB[nxt("zb", 2)]
                for cg in range(2):
                    mm(PS[zb][:, cg * 256:(cg + 1) * 256], ufT[:, cg, tl * 128:(tl + 1) * 128], AB[:, cg, :], True, True,
                       [ufT_t[cg], AB_t], [PS_t[zb]], cg == 1)
                if is_ctx:
                    fw.op(dve, lambda e: e.tensor_copy(out=V.zctx[:, tl, :], in_=PS[zb][:]), reads=[PS_t[zb]], writes=[V.zctx_t])
                else:
                    z_sb, z_t = V.z_sb, V.z_sb_t
                    for cg in range(2):
                        src = PS[zb][:, cg * 256:(cg + 1) * 256].rearrange("p (ri h c) -> p h ri c", ri=2, h=2)
                        dst = z_sb[zs][:, tl, cg * 4:(cg + 1) * 4, :].rearrange("p (h ri) c -> p h ri c", h=2)
                        cast_copy(dve if cg == 0 else act, dst, src, [PS_t[zb]], [z_t[zs]])
            if not is_ctx:
                for q8 in range(8):
                    dz = ez_in[l][q8].rearrange("p (x c) -> (p x) c", c=64)[g * 512:(g + 1) * 512, :].rearrange("(tl tok) c -> tok tl c", tl=4)
                    fw.dma([sp, pool][q8 % 2], zsems[zs], dz, V.z_sb[zs][:, :, q8, :], reads=[V.z_sb_t[zs]], writes=[ezin_t[l]])
            prep_one()
            for j in range(2):
                ps, ps_t = proj_fm(C_ZC + j * 128, ntl)
                fw.op(act, lambda e: e.copy(out=zc_sb[:, 0:n], in_=ps[:, 0:n]), reads=[ps_t], writes=[zc_t])
                ps2, ps2_t = proj_fm(C_CC + j * 128, ntl)
                if is_ctx:
                    o = V.tctx[:, j, 1:1 + n]; ot = [V.tctx_t]
                else:
                    o = t_all[:, j, 1 + g * 512:1 + g * 512 + 512]; ot = [t_t[1 + g]]
                fw.op(dve, lambda e: e.tensor_tensor(out=o, in0=ps2[:, 0:n], in1=zc_sb[:, 0:n], op=ALU.mult), reads=[ps2_t, zc_t], writes=ot)
            while pq:
                prep_one()

        def _p1a_tail(l, g, prep_next):
            if prep_next:
                for tl in range(4):
                    prep_tile(l, g + 1, tl, False, True)

        def exchange(l):
            hb, hb_t, th, th_t = V.hb, V.hb_t, V.th, V.th_t
            cps = [(hb[:, 0:128], kT_all[:, 128:256], [kT_t[1]]), (hb[:, 128:256], kT_all[:, NT * 128:(NT + 1) * 128], [kT_t[NT]]),
                   (hb[:, 256:320], v_aug[:, 1, 0, 0:64], [v_t[1]]), (hb[:, 320:384], v_aug[:, 1, 1, 64:128], [v_t[1]]),
                   (hb[:, 384:448], v_aug[:, NT, 0, 0:64], [v_t[NT]]), (hb[:, 448:512], v_aug[:, NT, 1, 64:128], [v_t[NT]]),
                   (hb[:, 512:514], t_all[:, :, 1], [t_t[1]]), (hb[:, 514:516], t_all[:, :, T0], [t_t[NG]])]
            for o, i_, tk in cps:
                fw.op(pool, lambda e: e.tensor_copy(out=o, in_=i_), reads=tk, writes=[hb_t])
            fw.dma(pool, hsem, eh_in[l][:, 0:516], hb[:, 0:516], reads=[hb_t], writes=[ehin_t[l]])
            for (i_t, o_t, i_ap, o_ap) in [(ehin_t[l], ehout_t[l], eh_in[l], eh_out[l])] + [(ezin_t[l], ezout_t[l][q8], ez_in[l][q8], ez_out[l][q8]) for q8 in range(8)]:
                fw._deps(pool, [i_t], [o_t])
                inst = nc.gpsimd.collective_compute("AllGather", ALU.bypass, replica_groups=[[0, 1], [2, 3], [4, 5], [6, 7]], ins=[i_ap], outs=[o_ap])
                ccsem.cnt += 1
                inst.then_inc(ccsem.h, 1)
                fw._post((ccsem, ccsem.cnt), [i_t], [o_t])
            eo = eh_out[l]
            ups = [(kT_all[:, 0:128], eo[0:128, 128:256], kT_t[0]), (kT_all[:, (NT + 1) * 128:(NT + 2) * 128], eo[128:256, 0:128], kT_t[NT + 1]),
                   (v_aug[:, 0, 0, 0:64], eo[0:128, 384:448], v_t[0]), (v_aug[:, 0, 1, 64:128], eo[0:128, 448:512], v_t[0]),
                   (v_aug[:, NT + 1, 0, 0:64], eo[128:256, 256:320], v_t[NT + 1]), (v_aug[:, NT + 1, 1, 64:128], eo[128:256, 320:384], v_t[NT + 1]),
                   (th[:, 0:2], eo[0:128, 514:516], th_t), (th[:, 2:4], eo[128:256, 512:514], th_t)]
            for o, i_, tk in ups:
                fw.dma(sp, hsem, o, i_, reads=[ehout_t[l]], writes=[tk])
            for _, _, tk in ups:
                tk.w = (hsem, hsem.cnt)
            fw.op(dve, lambda e: e.tensor_scalar(out=t_all[:, :, 0], in0=th[:, 0:2], scalar1=flags[:, 0:1], scalar2=None, op0=ALU.mult),
                  reads=[th_t, const_t], writes=[t_t[0]])
            fw.op(dve, lambda e: e.tensor_scalar(out=t_all[:, :, T0 + 1], in0=th[:, 2:4], scalar1=flags[:, 1:2], scalar2=None, op0=ALU.mult),
                  reads=[th_t, const_t], writes=[t_t[NG + 1]])

        def fft(l):
            G_b, G_t, zin, zin_t, Y_sb, Y_t = V.G_b, V.G_b_t, V.zin, V.zin_t, V.Y_sb, V.Y_t
            fw.dma(sp, csem, G_b[:], g_d, writes=[G_t])
            zo = ez_out[l]
            for hh in range(2):
                for qq in range(2):
                    qt = hh * 2 + qq
                    for ri in range(2):
                        for r in range(2):
                            src = zo[qt * 2 + ri][r * 128:(r + 1) * 128, :].rearrange("(a x) f -> a (x f)", a=32).rearrange("a (n c) -> a n c", c=64)
                            p0 = ri * 64 + r * 32
                            fw.dma(sp, zin_sem, zin[p0:p0 + 32, :, :], src, reads=[ezout_t[l][qt * 2 + ri]], writes=[zin_t])
                    for c4 in range(16):
                        b = PJ[nxt("pj", 2)]
                        for ci in range(4):
                            c = c4 * 4 + ci
                            mm(PS[b][:, ci * 128:(ci + 1) * 128], zin[:, :, c], m1_b[:], True, True, [zin_t, const_t], [PS_t[b]], ci == 3)
                        src = PS[b][:].rearrange("p (c j) -> p j c", c=4)
                        dst = Y_sb[:, :, qq * 64 + c4 * 4:qq * 64 + c4 * 4 + 4]
                        cast_copy(dve if c4 % 2 == 0 else act, dst, src, [PS_t[b]], [Y_t[qq]])
                fv = fT[:, hh, :].rearrange("p (k2 k1) -> p k1 k2", k1=64)
                for k8 in range(8):
                    b = STB[nxt("st", 2)]
                    for ki in range(8):
                        k1 = k8 * 8 + ki
                        for ri in range(2):
                            mm(PS[b][:, ki * 64:(ki + 1) * 64], Y_sb[:, ri * 64 + k1, :], G_b[:, k1, ri, :], ri == 0, ri == 1,
                               [Y_t[0], Y_t[1], G_t], [PS_t[b]], ki == 7 and ri == 1)
                    src = PS[b][:].rearrange("p (k1 k2) -> p k1 k2", k1=8)
                    dst = fv[:, k8 * 8:(k8 + 1) * 8, :]
                    cast_copy(dve if k8 % 2 == 0 else act, dst, src, [PS_t[b]], [fT_t[hh]])

        def attention_tile(tl, qcol, chunks, slotmap=None, defer=False):
            qT_g, qT_t, PT, PT_t, rec, rec_t, ntmp, ntmp_t, sga, sga_t, hT, hT_t = (V.qT_g, V.qT_t, V.PT, V.PT_t, V.rec, V.rec_t, V.ntmp, V.ntmp_t,
                                                                                     V.sga, V.sga_t, V.hT, V.hT_t)
            slotmap = dict(slotmap or {})
            slot = [0]
            pending_norm = [None]

            def fill():
                for f in slotmap.pop(slot[0], []):
                    f()
                slot[0] += 1

            for s_ in range(2):
                P0 = s_ * 64
                rn = slice(P0, P0 + 64)
                rd = slice(64 - P0, 128 - P0)
                pvi = [PVB, PJ[1]][s_]
                pvb = PS[pvi]; pvb_t = PS_t[pvi]
                nch = len(chunks)
                pts = [None] * nch

                def qk(ci):
                    kten, kcol, ktoks, _, _, midx = chunks[ci]
                    b = [STB[0], STB[1], PJ[0]][nxt("st3", 3)]
                    mm(PS[b][:], kten[rn, kcol:kcol + 128], qT_g[rn, :, qcol:qcol + 128], True, midx is None, ktoks + qT_t, [PS_t[b]], midx is None)
                    if midx is not None:
                        mm(PS[b][:].rearrange("p (j q) -> p j q", j=4), ident_b[:], bc_mid(masks_b[:, midx, :], 4), False, True, [const_t], [PS_t[b]], True)
                    p = nxt("pt", 4)
                    fw.op(act, lambda e: e.activation(out=PT[p][:], in_=PS[b][:], func=AF.Exp, scale=0.125), reads=[PS_t[b]], writes=[PT_t[p]])
                    pts[ci] = p

                def pv(ci):
                    _, _, _, vfn, vtoks, _ = chunks[ci]
                    p = pts[ci]
                    mm(pvb[:], vfn(s_), PT[p][:], ci == 0, ci == nch - 1, vtoks + [PT_t[p]], [pvb_t], ci == nch - 1)

                qk(0)
                if nch > 1:
                    qk(1)
                if nch > 2:
                    qk(2)
                for ci in range(nch):
                    pv(ci)
                    if ci + 3 < nch:
                        qk(ci + 3)
                    fill()
                def normalize(s_=s_, rn=rn, rd=rd, pvb=pvb, pvb_t=pvb_t):
                    es = bc_last(esink[rd, s_ * 4:(s_ + 1) * 4], 128)
                    r3 = rec[rd, :].rearrange("p (j q) -> p j q", j=4)
                    fw.op(dve, lambda e: e.tensor_tensor(out=r3, in0=pvb[rd, :].rearrange("p (j q) -> p j q", j=4), in1=es, op=ALU.add),
                          reads=[pvb_t, esink_t], writes=[rec_t])
                    fw.op(act, lambda e: e.activation(out=rec[rd, :], in_=rec[rd, :], func=AF.Ln), reads=[rec_t], writes=[rec_t])
                    fw.op(act, lambda e: e.activation(out=rec[rd, :], in_=rec[rd, :], func=AF.Exp, scale=-1.0), reads=[rec_t], writes=[rec_t])
                    fw.op(dve, lambda e: e.tensor_tensor(out=ntmp[rn, :], in0=pvb[rn, :], in1=rec[rd, :], op=ALU.mult), reads=[pvb_t, rec_t], writes=[ntmp_t])
                    fw.op(dve, lambda e: e.tensor_tensor(out=hT[rn, 0:4, qcol:qcol + 128], in0=ntmp[rn, :].rearrange("p (j q) -> p j q", j=4),
                                                        in1=sga[rn, :, qcol:qcol + 128], op=ALU.mult),
                          reads=[ntmp_t] + sga_t, writes=[hT_t[j][tl] for j in range(4)])

                if defer and s_ == 0:
                    slotmap.setdefault(6, []).insert(0, normalize)
                elif defer:
                    pending_norm[0] = normalize
                else:
                    normalize()
            for k_ in sorted(slotmap):
                for f in slotmap[k_]:
                    f()
            return pending_norm[0]

        def phase2_group(l, g, is_ctx, prepped=False, prep_next=False):
            (qT_g, qT_t, sga, sga_t, sg2, sg2_t, bc_sb, bc_t, cy, cy_t, hT, hT_t, xe, xe_t, etmp, etmp_t, junk, junk_t, ss, ss_t) = (
                V.qT_g, V.qT_t, V.sga, V.sga_t, V.sg2, V.sg2_t, V.bc_sb, V.bc_sb_t, V.cy, V.cy_t, V.hT, V.hT_t, V.xe, V.xe_t, V.etmp, V.etmp_t,
                V.junk, V.junk_t, V.ss, V.ss_t)
            ntl = 2 if is_ctx else 4
            n = ntl * 128
            v = 1 if is_ctx else 0
            last = (l == depth - 1)
            if not prepped:
                for tl in range(ntl):
                    prep_tile(l, g, tl, is_ctx, False)
            if not is_ctx:
                cslot = load_cs(g)
            sgf, sgf_t = (sg2, sg2_t) if is_ctx else (V.sgf, V.sgf_t)

            def do_gf(j):
                ps, ps_t = proj_fm(C_GF + j * 128, ntl)
                fw.op(act, lambda e: e.activation(out=sgf[:, 0:n], in_=ps[:, 0:n], func=AF.Silu), reads=[ps_t], writes=[sgf_t])
                if is_ctx:
                    fsrc = V.fcT[:, j, :]; ftk = [V.fcT_t]
                else:
                    fsrc = fT[:, j, g * 512:(g + 1) * 512]; ftk = [fT_t[j]]
                fw.op(pool, lambda e: e.tensor_tensor(out=hT[:, 4 + j, 0:n], in0=sgf[:, 0:n], in1=fsrc, op=ALU.mult),
                      reads=[sgf_t] + ftk, writes=[hT_t[4 + j][tl] for tl in range(ntl)])

            def do_conv(j):
                if is_ctx:
                    o0 = j * 256
                    cyj, cyj_t, sgj, sgj_t, bcj, bcj_t = cy[:, o0:o0 + n], cy_t, sg2[:, o0:o0 + n], sg2_t, bc_sb[:, o0:o0 + n], bc_t
                elif j == 0:
                    cyj, cyj_t, sgj, sgj_t, bcj, bcj_t = cy[:, 0:n], cy_t, sg2[:, 0:n], sg2_t, bc_sb[:, 0:n], bc_t
                else:
                    cyj, cyj_t, sgj, sgj_t, bcj, bcj_t = V.cy2[:, 0:n], V.cy2_t, V.sg3[:, 0:n], V.sg3_t, V.bc2[:, 0:n], V.bc2_t
                ps, ps_t = proj_fm(C_BC + j * 128, ntl)
                fw.op(act, lambda e: e.copy(out=bcj, in_=ps[:, 0:n]), reads=[ps_t], writes=[bcj_t])
                ps2, ps2_t = proj_fm(C_GC + j * 128, ntl)
                fw.op(act, lambda e: e.activation(out=sgj, in_=ps2[:, 0:n], func=AF.Silu), reads=[ps2_t], writes=[sgj_t])
                if is_ctx:
                    tsrc = V.tctx; c0 = 0; ttk = [V.tctx_t]
                else:
                    tsrc = t_all; c0 = g * 512; ttk = [t_t[g], t_t[g + 1], t_t[g + 2]]
                ce = pool if j == 0 else dve
                fw.op(ce, lambda e: e.tensor_scalar(out=cyj, in0=tsrc[:, j, c0:c0 + n], scalar1=convw[:, j, 0:1], scalar2=convb[:, j:j + 1],
                                                    op0=ALU.mult, op1=ALU.add), reads=ttk + [conv_t], writes=[cyj_t])
                for tap in (1, 2):
                    if ce is pool:
                        fw.op(pool, lambda e: e.tensor_scalar(out=V.ntmp[:, 0:n], in0=tsrc[:, j, c0 + tap:c0 + tap + n], scalar1=convw[:, j, tap:tap + 1],
                                                              scalar2=None, op0=ALU.mult), reads=ttk + [conv_t], writes=[V.ntmp_t])
                        fw.op(pool, lambda e: e.tensor_tensor(out=cyj, in0=cyj, in1=V.ntmp[:, 0:n], op=ALU.add), reads=[V.ntmp_t], writes=[cyj_t])
                    else:
                        fw.op(dve, lambda e: e.scalar_tensor_tensor(out=cyj, in0=tsrc[:, j, c0 + tap:c0 + tap + n], scalar=convw[:, j, tap:tap + 1],
                                                                  in1=cyj, op0=ALU.mult, op1=ALU.add), reads=ttk + [conv_t], writes=[cyj_t])
                fw.op(ce, lambda e: e.tensor_tensor(out=cyj, in0=cyj, in1=bcj, op=ALU.mult), reads=[bcj_t], writes=[cyj_t])
                fw.op(ce, lambda e: e.tensor_tensor(out=hT[:, 6 + j, 0:n], in0=cyj, in1=sgj, op=ALU.mult),
                      reads=[cyj_t, sgj_t], writes=[hT_t[6 + j][tl] for tl in range(ntl)])
            def do_ga(j):
                ps, ps_t = proj_fm(C_GA + j * 128, ntl)
                fw.op(act, lambda e: e.activation(out=sga[:, j, 0:n], in_=ps[:, 0:n], func=AF.Silu), reads=[ps_t], writes=[sga_t[j]])

            def do_q(j):
                ps, ps_t = proj_fm(C_Q + j * 128, ntl)
                if is_ctx:
                    fw.op(act, lambda e: e.copy(out=qT_g[:, j, 0:n], in_=ps[:, 0:n]), reads=[ps_t], writes=[qT_t[j]])
                else:
                    return rope_chunk(ps, ps_t, cslot, qT_g[:, j, :], [qT_t[j]], 512, split=True)
            if is_ctx:
                for j in range(2):
                    do_gf(j)
                for j in range(2):
                    do_conv(j)
                for j in range(4):
                    do_ga(j)
                for j in range(4):
                    do_q(j)
                tile_fill = [{} for _ in range(ntl)]
            else:
                qb = do_q(0); do_conv(0); qb()
                qb = do_q(1); do_ga(0); do_ga(1); qb()
                qb = do_q(2); do_ga(2); do_ga(3); qb()
                qb = do_q(3); do_gf(0); do_gf(1); qb()
                do_conv(1)
                pf = [(lambda tl=tl: prep_tile(l, g + 1, tl, False, False)) for tl in range(4)] if prep_next else []
                tile_fill = [{}, {}, {}, {}]
                if pf:
                    tile_fill[1] = {0: [pf[0]], 5: [pf[1]]}
                    tile_fill[2] = {0: [pf[2]], 5: [pf[3]]}
            ggb, ggb_t = (V.ggc_b, V.ggc_b_t) if is_ctx else (gg_b, gg_t)
            for tl in range(ntl):
                cch = [(kcT, 0, [kcT_t], (lambda s_: vc_aug[:, 0, s_, :]), [vc_t], None),
                       (kcT, 128, [kcT_t], (lambda s_: vc_aug[:, 1, s_, :]), [vc_t], None)]
                if not is_ctx:
                    i = g * 4 + tl
                    for dlt, midx in [(0, 0 if i == 0 else 1), (1, None), (2, 3 if i == NT - 1 else 2)]:
                        ti = i + dlt
                        cch.append((kT_all, ti * 128, [kT_t[ti]], (lambda s_, ti=ti: v_aug[:, ti, s_, :]), [v_t[ti]], midx))
                sm = {k_: list(v_) for k_, v_ in tile_fill[tl].items()}

                def outproj(hlf, tl=tl):
                    yb = PS[YB[hlf]]; yb_t = PS_t[YB[hlf]]
                    for j in range(8):
                        mm(yb[:], hT[:, j, tl * 128:(tl + 1) * 128], wob[:, j, hlf * 512:(hlf + 1) * 512], j == 0, j == 7,
                           [hT_t[j][tl], wob_t], [yb_t], j == 7)

                if is_ctx:
                    attention_tile(tl, tl * 128, cch, sm)
                    outproj(0); outproj(1)
                    for f in make_epilogue(l, g, tl, is_ctx, last, ggb, ggb_t):
                        f()
                    continue
                if PNORM:
                    pn, op_prev, st_prev = PNORM.pop()
                    sm.setdefault(0, []).insert(0, pn)
                    sm.setdefault(1, []).append(lambda: op_prev(0))
                    sm.setdefault(2, []).append(lambda: op_prev(1))
                    PEND.extend(st_prev)
                for k_, f in zip((5, 6, 7, 8), PEND):
                    sm.setdefault(k_, []).append(f)
                del PEND[:]
                pn = attention_tile(tl, tl * 128, cch, sm, defer=True)
                stages = make_epilogue(l, g, tl, is_ctx, last, ggb, ggb_t)
                if tl == ntl - 1:
                    pn(); outproj(0); outproj(1)
                    PEND.extend(stages)
                else:
                    PNORM.append((pn, outproj, stages))

        PNORM = []
        HXS = []
        PEND = []

        def make_epilogue(l, g, tl, is_ctx, last, ggb, ggb_t):
            xe, xe_t, etmp, etmp_t, junk, junk_t, ss, ss_t = V.xe, V.xe_t, V.etmp, V.etmp_t, V.junk, V.junk_t, V.ss, V.ss_t
            st = {}
            if is_ctx:
                rsrc = ctx_d[tl * 128:(tl + 1) * 128, :]; rtk = []
                r0 = tl * 128
            else:
                r0 = (g * 4 + tl) * 128
                rsrc = (x_d if l == 0 else x1_d)[r0:r0 + 128, :]; rtk = [] if l == 0 else [x1_t]

            def stage_a():
                st["es"] = nxt("xe", len(xe)); st["s2"] = nxt("xn", 2)
                es_, s2 = st["es"], st["s2"]
                fw.dma(sp, xe_sem[es_], xe[es_][:], rsrc, reads=rtk, writes=[xe_t[es_]])
                for hlf in range(2):
                    fw.op(act, lambda e: e.activation(out=junk[:, hlf * 512:(hlf + 1) * 512], in_=PS[YB[hlf]][:], func=AF.Square, accum_out=ss[s2][:, hlf:hlf + 1]),
                          reads=[PS_t[YB[hlf]]], writes=[junk_t, ss_t[s2]])

            def stage_b():
                s2 = st["s2"]
                fw.op(dve, lambda e: e.tensor_tensor(out=ss[s2][:, 2:3], in0=ss[s2][:, 0:1], in1=ss[s2][:, 1:2], op=ALU.add), reads=[ss_t[s2]], writes=[ss_t[s2]])
                fw.op(dve, lambda e: e.tensor_scalar(out=ss[s2][:, 2:3], in0=ss[s2][:, 2:3], scalar1=1.0 / D, scalar2=EPS, op0=ALU.mult, op1=ALU.add),
                      reads=[ss_t[s2]], writes=[ss_t[s2]])
                fw.op(act, lambda e: e.activation(out=ss[s2][:, 0:1], in_=ss[s2][:, 2:3], func=AF.Ln), reads=[ss_t[s2]], writes=[ss_t[s2]])
                fw.op(act, lambda e: e.activation(out=ss[s2][:, 3:4], in_=ss[s2][:, 0:1], func=AF.Exp, scale=-0.5), reads=[ss_t[s2]], writes=[ss_t[s2]])

            def stage_c():
                es_, s2 = st["es"], st["s2"]
                for hlf in range(2):
                    fw.op(dve, lambda e: e.scalar_tensor_tensor(out=etmp[:, hlf * 512:(hlf + 1) * 512], in0=PS[YB[hlf]][:], scalar=ss[s2][:, 3:4],
                                                              in1=ggb[:, hlf * 512:(hlf + 1) * 512], op0=ALU.mult, op1=ALU.mult),
                          reads=[PS_t[YB[hlf]], ss_t[s2], ggb_t], writes=[etmp_t])
                fw.op(dve, lambda e: e.tensor_tensor(out=xe[es_][:], in0=xe[es_][:], in1=etmp[:], op=ALU.add), reads=[etmp_t], writes=[xe_t[es_]])

            def stage_d():
                es_ = st["es"]
                if (not is_ctx) and (not last):
                    fw.op(act, lambda e: e.activation(out=junk[:], in_=xe[es_][:], func=AF.Square, accum_out=ssq[l + 1][:, g * 4 + tl:g * 4 + tl + 1]),
                          reads=[xe_t[es_]], writes=[junk_t, ssq_t[l + 1]])
                if is_ctx:
                    fw.dma(pool, msem, ctx1_d[r0:r0 + 128, :], xe[es_][:], reads=[xe_t[es_]], writes=[ctx1_t])
                elif not last:
                    fw.dma(pool, x1sem, x1_d[r0:r0 + 128, :], xe[es_][:], reads=[xe_t[es_]], writes=[x1_t])
                else:
                    fw.dma(pool, osem, out_d[r0:r0 + 128, :], xe[es_][:], reads=[xe_t[es_]], writes=[out_t])

            return [stage_a, stage_b, stage_c, stage_d]

        def ctx_fourier():
            zctx, zctx_t, cs256_b, cs256_t, fcT, fcT_t = V.zctx, V.zctx_t, V.cs256_b, V.cs256_b_t, V.fcT, V.fcT_t
            fw.dma(sp, csem, cs256_b[:], cs256_d, writes=[cs256_t])
            for cg in range(2):
                b = STB[nxt("st", 2)]
                i = 0
                for nt in range(2):
                    for ri in range(2):
                        mm(PS[b][:, 0:256], zctx[:, nt, cg * 256 + ri * 128:cg * 256 + ri * 128 + 128], cs256_b[:, nt, ri, :], i == 0, i == 3,
                           [zctx_t, cs256_t], [PS_t[b]], i == 3)
                        i += 1
                fw.op(dve, lambda e: e.tensor_copy(out=fcT[:, cg, :], in_=PS[b][:, 0:256]), reads=[PS_t[b]], writes=[fcT_t])

        import os
        KSTOP = int(os.environ.get("KSTOP", "99"))
        for l in range(depth):
            full = (l < depth - 1)
            with contextlib.ExitStack() as sc:
                alloc(sc, ["wst", "ident_f", "ones_f", "c64", "s64", "rt1", "xs", "junk"], {"xs": 2})
                if l == 0:
                    prepass0()
                setup_layer(l, full)
                fw.barrier()
            if KSTOP <= 0:
                break
            with contextlib.ExitStack() as sc:
                names = NORM + P1
                if full:
                    names = names + ["zctx", "tctx", "fcT", "cs256_b"] + P2 + ["ggc_b", "ones_f", "ident_f", "rt1"]
                alloc(sc, names)
                if full:
                    st_t = Tok()
                    fw.dma(sp, csem, V.ident_f[:], identf_d, writes=[st_t])
                    fw.op(pool, lambda e: e.memset(V.ones_f[:], 1.0), writes=[st_t])
                    fw.op(pool, lambda e: e.memset(V.tctx[:], 0.0), writes=[V.tctx_t])
                    for hlf in range(2):
                        yb = PS[YB[hlf]]; yb_t = PS_t[YB[hlf]]
                        for k4 in range(4):
                            kc = hlf * 4 + k4
                            fw.op(dve, lambda e: e.tensor_scalar(out=V.rt1[:, 0:128], in0=V.ones_f[:], scalar1=ggm[:, kc, 1:2], scalar2=None, op0=ALU.mult),
                                  reads=[mod_t, st_t], writes=[V.rt1_t])
                            mm(yb[:, k4 * 128:(k4 + 1) * 128], V.rt1[:, 0:128], V.ident_f[:], True, True, [V.rt1_t, st_t], [yb_t], True)
                        fw.op(act, lambda e: e.copy(out=V.ggc_b[:, hlf * 512:(hlf + 1) * 512], in_=yb[:]), reads=[yb_t], writes=[V.ggc_b_t])
                phase1a_group(l, 0, True, full)
                if full:
                    ctx_fourier()
                    phase2_group(l, 0, True)
                fw.barrier()
            if KSTOP <= 1:
                break
            with contextlib.ExitStack() as sc:
                alloc(sc, NORM + P1 + ROPE + ["z_sb", "hb", "th", "hxT2"], {"xs": 2})
                HXS[:] = [(V.hxT, V.hxT_t), (V.hxT2, [[Tok(), Tok()] for _ in range(4)])]
                use_hx(0)
                for tl in range(4):
                    prep_tile(l, 0, tl, False, True)
                for g in range(NG):
                    phase1a_group(l, g, False, True, prepped=True, prep_next=(g < NG - 1))
                if KSTOP <= 2:
                    fw.barrier()
                    break
                exchange(l)
                if DBG and l == 0:
                    fw.dma(sp, osem, dbg["dbg_kT"], kT_all[:], reads=kT_t, writes=[out_t])
                    fw.dma(sp, osem, dbg["dbg_v"], v_aug[:].rearrange("p a b c -> p (a b c)"), reads=v_t, writes=[out_t])
                    fw.dma(sp, osem, dbg["dbg_t"], t_all[:].rearrange("p a b -> p (a b)"), reads=t_t, writes=[out_t])
                fw.barrier()
            if KSTOP <= 3:
                break
            with contextlib.ExitStack() as sc:
                alloc(sc, ["G_b", "zin", "Y_sb"])
                fft(l)
                if DBG and l == 0:
                    fw.dma(sp, osem, dbg["dbg_fT"], fT[:].rearrange("p a b -> p (a b)"), reads=fT_t, writes=[out_t])
                fw.barrier()
            if KSTOP <= 4:
                break
            with contextlib.ExitStack() as sc:
                alloc(sc, NORM + ROPE + P2 + ["cy2", "sg3", "bc2", "sgf"], {"xs": 2})
                for tl in range(4):
                    prep_tile(l, 0, tl, False, False)
                for g in range(NG):
                    phase2_group(l, g, False, prepped=True, prep_next=(g < NG - 1))
                    if DBG and l == 0 and g == 0:
                        fw.barrier()
                        fw.dma(sp, osem, dbg["dbg_q"], V.qT_g[:].rearrange("p a b -> p (a b)"), writes=[out_t])
                        fw.dma(sp, osem, dbg["dbg_hT"], V.hT[:].rearrange("p a b -> p (a b)"), writes=[out_t])
                        fw.dma(sp, osem, dbg["dbg_sga"], V.sga[:].rearrange("p a b -> p (a b)"), writes=[out_t])
                        fw.barrier()
                for f in PEND:
                    f()
                del PEND[:]
                if l < depth - 1:
                    finish_rstd(l + 1)
                fw.barrier()
        fin = [out_t, x1_t, ctx1_t]
        for e in fw.engs:
            fw.wait_all(e, fin)
    return nc


def _consts(half):
    bf = ml_dtypes.bfloat16
    c = {}
    c["identf"] = np.eye(128, dtype=np.float32)
    c["identb"] = np.eye(128, dtype=np.float32).astype(bf)
    pm = np.zeros((128, 128), np.float32)
    for p in range(128):
        pm[p, p ^ 16] = 1.0
    c["pm"] = pm.astype(bf)
    tok = np.arange(T0) + half * T0
    row = (tok // 64).astype(np.float32)
    col = (tok % 64).astype(np.float32)
    inv = np.power(np.float32(10000.0), -np.arange(16, dtype=np.float32) / np.float32(16)).astype(np.float32)
    cosT = np.zeros((128, T0), np.float32)
    sinT = np.zeros((128, T0), np.float32)
    for p in range(128):
        d = p % 64
        axis = d // 32
        hf = (d % 32) // 16
        fr = d % 16
        ang = ((row if axis == 0 else col) * inv[fr]).astype(np.float32)
        cosT[p] = np.cos(ang)
        sinT[p] = np.sin(ang) * (-1.0 if hf == 0 else 1.0)
    c["cosT"] = cosT
    c["sinT"] = sinT
    cc = np.arange(64)
    th = 2 * np.pi * np.outer(cc, cc) / 64.0
    c["c64x2"] = np.concatenate([np.cos(th), np.cos(th)], 1).astype(np.float32)
    c["ns64x2"] = np.concatenate([-np.sin(th), -np.sin(th)], 1).astype(np.float32)
    n1 = np.arange(64)
    ph = 2 * np.pi * np.outer(n1, n1) / 64.0
    nrm = 1.0 / np.sqrt(8192.0 * 64.0)
    m1 = np.zeros((128, 128))
    m1[0:64, 0:64] = np.cos(ph)
    m1[64:128, 0:64] = np.sin(ph)
    m1[0:64, 64:128] = -np.sin(ph)
    m1[64:128, 64:128] = np.cos(ph)
    c["m1"] = (m1 * nrm).astype(np.float32).astype(bf)
    n2 = np.arange(128)[:, None, None]
    k1 = np.arange(64)[None, :, None]
    k2 = (np.arange(64) + 64 * half)[None, None, :]
    ang = 2 * np.pi * ((n2 * (k1 + 64 * k2)) % 8192) / 8192.0
    G = np.stack([np.cos(ang), np.sin(ang)], axis=2)
    c["gtab"] = G.astype(np.float32).astype(bf)
    n = np.arange(256)
    a2 = 2 * np.pi * np.outer(n, n) / 256.0
    nr2 = 1.0 / np.sqrt(256.0 * 64.0)
    cs = np.stack([np.cos(a2) * nr2, np.sin(a2) * nr2], axis=1)
    cs = cs.reshape(2, 128, 2, 256).transpose(1, 0, 2, 3)
    c["cs256"] = np.ascontiguousarray(cs).astype(np.float32).astype(bf)
    kk = np.arange(128)[:, None]
    qq = np.arange(128)[None, :]
    NEG = np.float32(-30000.0)
    mprev = np.where(kk >= qq, np.float32(0.0), NEG).astype(np.float32)
    mnext = np.where(kk <= qq, np.float32(0.0), NEG).astype(np.float32)
    allm = np.full_like(mprev, NEG)
    masks = np.stack([mprev if half == 1 else allm, mprev, mnext, mnext if half == 0 else allm], axis=1)
    c["masks"] = np.ascontiguousarray(masks).astype(bf)
    fl = np.zeros((128, 2), np.float32)
    fl[:, 0] = 1.0 if half == 1 else 0.0
    fl[:, 1] = 1.0 if half == 0 else 0.0
    c["flags"] = fl
    return c


def _perm_heads():
    idx = []
    for j in range(4):
        for s in range(2):
            h = s * 4 + j
            idx.extend(range(h * 64, (h + 1) * 64))
    return np.array(idx)


_NC_CACHE = {}


def kernel(x, c, ctx, c_ctx, w_mod, b_mod, g_pre, g_post, w_in, w_out, sink, w_fourier, conv_w, conv_b):
    x = np.asarray(x, np.float32)
    ph = _perm_heads()
    w_in = np.asarray(w_in, np.float32)
    cols = np.concatenate([ph, np.arange(512, 768), 768 + ph, np.arange(1280, 2816)])
    w_in_p = np.ascontiguousarray(w_in[:, :, cols])
    w_out = np.asarray(w_out, np.float32)
    rows = np.concatenate([ph, np.arange(512, 1024)])
    w_out_p = np.ascontiguousarray(w_out[:, rows, :])
    b_mod = np.asarray(b_mod, np.float32)
    bmodT = np.ascontiguousarray(b_mod.reshape(2, 24, 128).transpose(0, 2, 1))
    gpreT = np.ascontiguousarray(np.asarray(g_pre, np.float32).reshape(2, 8, 128).transpose(0, 2, 1))
    gpostT = np.ascontiguousarray(np.asarray(g_post, np.float32).reshape(2, 8, 128).transpose(0, 2, 1))
    convwT = np.ascontiguousarray(np.asarray(conv_w, np.float32).reshape(2, 3, 2, 128).transpose(0, 3, 2, 1))
    convbT = np.ascontiguousarray(np.asarray(conv_b, np.float32).reshape(2, 2, 128).transpose(0, 2, 1))
    c = np.asarray(c, np.float32)
    c_ctx = np.asarray(c_ctx, np.float32)
    if "nc" not in _NC_CACHE:
        _NC_CACHE["nc"] = build_nc()
    nc = _NC_CACHE["nc"]
    consts = [_consts(0), _consts(1)]
    in_maps = []
    for core in range(8):
        b, half = core // 2, core % 2
        cT = np.stack([c[b].reshape(8, 128).T, c_ctx.reshape(8, 128).T], axis=-1)
        m = {"x": np.ascontiguousarray(x[b, half * T0:(half + 1) * T0]), "ctx": np.ascontiguousarray(np.asarray(ctx, np.float32)[b]),
             "cT": np.ascontiguousarray(cT.astype(np.float32)), "w_mod": np.asarray(w_mod, np.float32), "bmodT": bmodT, "gpreT": gpreT, "gpostT": gpostT,
             "w_in": w_in_p, "w_out": w_out_p, "sink": np.asarray(sink, np.float32), "w_fourier": np.asarray(w_fourier, np.float32),
             "convwT": convwT, "convbT": convbT}
        m.update(consts[half])
        in_maps.append(m)
    res = run_bass_kernel_spmd(nc, in_maps, core_ids=list(range(8)))
    out = np.empty((4, 2 * T0, D), np.float32)
    for core in range(8):
        b, half = core // 2, core % 2
        out[b, half * T0:(half + 1) * T0] = np.asarray(res.results[core]["out"], np.float32)
    return out
```
